# Optimizing a Trainium2 kernel written in Bass

```python
import jax, jax.numpy as jnp
from jax import lax
import numpy as np


D_MODEL = 1024
BATCH = 16
SEQ = 2048
DEPTH = 4

N_MIXERS = 2
N_A = (DEPTH + 1) // 2
N_B = DEPTH // 2
EPS = 1e-6

A_HEADS = 4
A_DQK = D_MODEL // 8
A_DV = D_MODEL // 4
A_CHUNK = 64
A_TOK = 2 * A_HEADS * A_DQK + 2 * A_HEADS * A_DV + 2 * A_HEADS

B_HEADS = 16
B_KV_HEADS = 2
B_GROUP = B_HEADS // B_KV_HEADS
B_HD = D_MODEL // 16
B_WINDOW = 128
B_BLOCK = 128
B_TOK = (B_HEADS + 2 * B_KV_HEADS) * B_HD
ROT_DIM = B_HD // 4
ROPE_THETA = 500000.0

N_MEM = 256
M_HEADS = 4
M_HD = D_MODEL // 8
M_Q = M_HEADS * M_HD

A_IN = A_TOK + M_Q
B_IN = B_TOK + M_Q
A_OUT = A_HEADS * A_DV + M_Q
B_OUT = B_HEADS * B_HD + M_Q

D_FF = 256 * ((8 * D_MODEL // 3 + 255) // 256)

kernel_name = 'hybrid_mlstm_swa_sink_memxattn_macaron'


def rms_norm(x, g):
    xf = x.astype(jnp.float32)
    y = xf * lax.rsqrt(jnp.mean(xf * xf, axis=-1, keepdims=True) + EPS)
    return (y * g.astype(jnp.float32)).astype(x.dtype)


def swiglu(x, w_in, w_out):
    gate, up = jnp.split(x @ w_in, 2, axis=-1)
    return (jax.nn.silu(gate) * up) @ w_out


def rope_tables(positions):
    inv_freq = ROPE_THETA ** (-jnp.arange(0, ROT_DIM, 2, dtype=jnp.float32) / ROT_DIM)
    ang = positions.astype(jnp.float32)[..., None] * inv_freq
    return jnp.cos(ang), jnp.sin(ang)


def apply_partial_rope(x, cos, sin):
    half = ROT_DIM // 2
    x1, x2, rest = x[..., :half], x[..., half:ROT_DIM], x[..., ROT_DIM:]
    c, s = cos[:, :, None, :], sin[:, :, None, :]
    return jnp.concatenate([x1 * c - x2 * s, x2 * c + x1 * s, rest], axis=-1)


def mlstm_chunkwise(q, k, v, i_pre, logf):
    bsz, nh, t, dk = q.shape
    dv = v.shape[-1]
    nc = t // A_CHUNK

    def chunks(a):
        a = a.reshape(a.shape[:2] + (nc, A_CHUNK) + a.shape[3:])
        return jnp.moveaxis(a, 2, 0)

    causal = jnp.tril(jnp.ones((A_CHUNK, A_CHUNK), dtype=bool))

    def step(carry, inp):
        c_st, n_st, m_st = carry
        qc, kc, vc, ic, fc = inp
        b = jnp.cumsum(fc, axis=-1)
        log_d = b[..., :, None] - b[..., None, :] + ic[..., None, :]
        log_d = jnp.where(causal, log_d, -jnp.inf)
        m_inter = b + m_st[..., None]
        m_t = jnp.maximum(m_inter, jnp.max(log_d, axis=-1))
        s = jnp.einsum('bhjd,bhsd->bhjs', qc, kc) * jnp.exp(log_d - m_t[..., None])
        inter = jnp.exp(m_inter - m_t)
        num = (jnp.einsum('bhjs,bhsv->bhjv', s, vc)
               + inter[..., None] * jnp.einsum('bhjd,bhvd->bhjv', qc, c_st))
        den = jnp.sum(s, axis=-1) + inter * jnp.einsum('bhjd,bhd->bhj', qc, n_st)
        h = num / jnp.maximum(jnp.abs(den), jnp.exp(-m_t))[..., None]
        b_last = b[..., -1]
        log_w = b_last[..., None] - b + ic
        m_new = jnp.maximum(b_last + m_st, jnp.max(log_w, axis=-1))
        w = jnp.exp(log_w - m_new[..., None])
        decay = jnp.exp(b_last + m_st - m_new)
        c_new = decay[..., None, None] * c_st + jnp.einsum('bhsv,bhsd->bhvd', w[..., None] * vc, kc)
        n_new = decay[..., None] * n_st + jnp.einsum('bhs,bhsd->bhd', w, kc)
        return (c_new, n_new, m_new), h

    init = (jnp.zeros((bsz, nh, dv, dk), jnp.float32),
            jnp.zeros((bsz, nh, dk), jnp.float32),
            jnp.zeros((bsz, nh), jnp.float32))
    xs = (chunks(q * dk ** -0.5), chunks(k), chunks(v), chunks(i_pre), chunks(logf))
    _, hs = lax.scan(step, init, xs)
    return jnp.moveaxis(hs, 0, 2).reshape(bsz, nh, t, dv)


def mlstm_heads(tok, gate_b, h_norm_g):
    bsz, t, _ = tok.shape
    o1 = A_HEADS * A_DQK
    o2 = 2 * o1
    o3 = o2 + A_HEADS * A_DV
    o4 = o3 + A_HEADS * A_DV
    f32 = jnp.float32
    q = tok[..., :o1].reshape(bsz, t, A_HEADS, A_DQK).transpose(0, 2, 1, 3).astype(f32)
    k = tok[..., o1:o2].reshape(bsz, t, A_HEADS, A_DQK).transpose(0, 2, 1, 3).astype(f32)
    v = tok[..., o2:o3].reshape(bsz, t, A_HEADS, A_DV).transpose(0, 2, 1, 3).astype(f32)
    o_gate = jax.nn.sigmoid(tok[..., o3:o4].astype(f32))
    gates = tok[..., o4:].astype(f32) + gate_b.astype(f32)
    i_pre = gates[..., :A_HEADS].transpose(0, 2, 1)
    logf = jax.nn.log_sigmoid(gates[..., A_HEADS:]).transpose(0, 2, 1)
    h = mlstm_chunkwise(q, k, v, i_pre, logf).transpose(0, 2, 1, 3)
    h = rms_norm(h, h_norm_g.reshape(A_HEADS, A_DV)).reshape(bsz, t, A_HEADS * A_DV)
    return (h * o_gate).astype(tok.dtype)


def swa_heads(tok, cos, sin, q_norm_g, k_norm_g, sinks):
    bsz, t, _ = tok.shape
    f32 = jnp.float32
    nq = B_HEADS * B_HD
    nk = B_KV_HEADS * B_HD
    q = tok[..., :nq].reshape(bsz, t, B_HEADS, B_HD).astype(f32)
    k = tok[..., nq:nq + nk].reshape(bsz, t, B_KV_HEADS, B_HD).astype(f32)
    v = tok[..., nq + nk:].reshape(bsz, t, B_KV_HEADS, B_HD).astype(f32)
    q = apply_partial_rope(rms_norm(q, q_norm_g), cos, sin) * B_HD ** -0.5
    k = apply_partial_rope(rms_norm(k, k_norm_g), cos, sin)
    nb = t // B_BLOCK
    qb = q.reshape(bsz, nb, B_BLOCK, B_KV_HEADS, B_GROUP, B_HD)
    kb = k.reshape(bsz, nb, B_BLOCK, B_KV_HEADS, B_HD)
    vb = v.reshape(bsz, nb, B_BLOCK, B_KV_HEADS, B_HD)

    def band(a):
        prev = jnp.concatenate([jnp.zeros_like(a[:, :1]), a[:, :-1]], axis=1)
        return jnp.moveaxis(jnp.concatenate([prev, a], axis=2), 1, 0)

    qi = jnp.arange(B_BLOCK)[:, None]
    kj = jnp.arange(2 * B_BLOCK)[None, :]
    diff = B_BLOCK + qi - kj
    in_window = (diff >= 0) & (diff < B_WINDOW)
    sink = sinks.astype(f32).reshape(B_KV_HEADS, B_GROUP)[None, :, :, None, None]

    def block(args):
        n, qn, kn, vn = args
        valid = in_window & ((n * B_BLOCK - B_BLOCK + kj) >= 0)
        s = jnp.einsum('bqhgd,bkhd->bhgqk', qn, kn)
        s = jnp.where(valid, s, -jnp.inf)
        m = jnp.maximum(jnp.max(s, axis=-1, keepdims=True), sink)
        p = jnp.exp(s - m)
        denom = jnp.sum(p, axis=-1, keepdims=True) + jnp.exp(sink - m)
        return jnp.einsum('bhgqk,bkhd->bqhgd', p / denom, vn)

    out = lax.map(block, (jnp.arange(nb), jnp.moveaxis(qb, 1, 0), band(kb), band(vb)))
    return jnp.moveaxis(out, 0, 1).reshape(bsz, t, nq).astype(tok.dtype)


def memory_heads(xq, mem_k, mem_v, q_norm_g, k_norm_g):
    bsz, t, _ = xq.shape
    f32 = jnp.float32
    q = rms_norm(xq.reshape(bsz, t, M_HEADS, M_HD).astype(f32), q_norm_g)
    k = rms_norm(mem_k.astype(f32), k_norm_g)
    s = jnp.einsum('bthd,bmhd->bhtm', q, k) * M_HD ** -0.5
    p = jax.nn.softmax(s, axis=-1)
    o = jnp.einsum('bhtm,bmhd->bthd', p, mem_v.astype(f32))
    return o.reshape(bsz, t, M_Q).astype(xq.dtype)


def setup_inputs(seed: int = 0) -> dict:
    key = jax.random.key(seed)
    ks = jax.random.split(key, 32)
    f32 = jnp.float32

    def normal(k, shape, scale):
        return jax.random.normal(k, shape, f32) * scale

    def gain(k, shape):
        return 1.0 + 0.05 * jax.random.normal(k, shape, f32)

    x = normal(ks[0], (BATCH, SEQ, D_MODEL), 1.0)
    mem = normal(ks[1], (BATCH, N_MEM, D_MODEL), 1.0)
    start = jax.random.randint(ks[2], (BATCH, 1), 0, 4096, dtype=jnp.int32)
    positions = start + jnp.arange(SEQ, dtype=jnp.int32)[None, :]
    i_bias = normal(ks[3], (N_A, A_HEADS), 0.1)
    f_bias = jnp.linspace(3.0, 6.0, A_HEADS, dtype=f32)[None, :] + normal(ks[4], (N_A, A_HEADS), 0.1)
    return {
        'x': x,
        'mem': mem,
        'positions': positions,
        'mem_norm_g': gain(ks[5], (D_MODEL,)),
        'mem_w_kv': normal(ks[6], (D_MODEL, 2 * M_Q), D_MODEL ** -0.5),
        'ffn1_norm_g': gain(ks[7], (DEPTH, D_MODEL)),
        'ffn1_w_in': normal(ks[8], (DEPTH, D_MODEL, 2 * D_FF), D_MODEL ** -0.5),
        'ffn1_w_out': normal(ks[9], (DEPTH, D_FF, D_MODEL), 0.5 * D_FF ** -0.5),
        'mix_norm_g': gain(ks[10], (DEPTH, D_MODEL)),
        'ffn2_norm_g': gain(ks[11], (DEPTH, D_MODEL)),
        'ffn2_w_in': normal(ks[12], (DEPTH, D_MODEL, 2 * D_FF), D_MODEL ** -0.5),
        'ffn2_w_out': normal(ks[13], (DEPTH, D_FF, D_MODEL), 0.5 * D_FF ** -0.5),
        'xa_q_norm_g': gain(ks[14], (DEPTH, M_HD)),
        'xa_k_norm_g': gain(ks[15], (DEPTH, M_HD)),
        'a_w_in': normal(ks[16], (N_A, D_MODEL, A_IN), D_MODEL ** -0.5),
        'a_gate_b': jnp.concatenate([i_bias, f_bias], axis=-1),
        'a_h_norm_g': gain(ks[17], (N_A, A_HEADS * A_DV)),
        'a_w_out': normal(ks[18], (N_A, A_OUT, D_MODEL), 0.5 * A_OUT ** -0.5),
        'b_w_in': normal(ks[19], (N_B, D_MODEL, B_IN), D_MODEL ** -0.5),
        'b_q_norm_g': gain(ks[20], (N_B, B_HD)),
        'b_k_norm_g': gain(ks[21], (N_B, B_HD)),
        'b_sinks': normal(ks[22], (N_B, B_HEADS), 0.5),
        'b_w_out': normal(ks[23], (N_B, B_OUT, D_MODEL), 0.5 * B_OUT ** -0.5),
    }


def reference(x, mem, positions, mem_norm_g, mem_w_kv, ffn1_norm_g, ffn1_w_in, ffn1_w_out,
              mix_norm_g, ffn2_norm_g, ffn2_w_in, ffn2_w_out, xa_q_norm_g, xa_k_norm_g,
              a_w_in, a_gate_b, a_h_norm_g, a_w_out,
              b_w_in, b_q_norm_g, b_k_norm_g, b_sinks, b_w_out):
    bsz, n_mem, _ = mem.shape
    mem_kv = rms_norm(mem, mem_norm_g) @ mem_w_kv
    mem_k = mem_kv[..., :M_Q].reshape(bsz, n_mem, M_HEADS, M_HD)
    mem_v = mem_kv[..., M_Q:].reshape(bsz, n_mem, M_HEADS, M_HD)
    cos, sin = rope_tables(positions)
    for i in range(DEPTH):
        j = i // N_MIXERS
        x = x + 0.5 * swiglu(rms_norm(x, ffn1_norm_g[i]), ffn1_w_in[i], ffn1_w_out[i])
        hn = rms_norm(x, mix_norm_g[i])
        if i % N_MIXERS == 0:
            proj = hn @ a_w_in[j]
            y_tok = mlstm_heads(proj[..., :A_TOK], a_gate_b[j], a_h_norm_g[j])
            xq = proj[..., A_TOK:]
            w_out = a_w_out[j]
        else:
            proj = hn @ b_w_in[j]
            y_tok = swa_heads(proj[..., :B_TOK], cos, sin, b_q_norm_g[j], b_k_norm_g[j], b_sinks[j])
            xq = proj[..., B_TOK:]
            w_out = b_w_out[j]
        y_mem = memory_heads(xq, mem_k, mem_v, xa_q_norm_g[i], xa_k_norm_g[i])
        x = x + jnp.concatenate([y_tok, y_mem], axis=-1) @ w_out
        x = x + 0.5 * swiglu(rms_norm(x, ffn2_norm_g[i]), ffn2_w_in[i], ffn2_w_out[i])
    return x
```

```python
import math
import os
from contextlib import ExitStack

import numpy as np
import concourse.bass as bass
import concourse.mybir as mybir
from concourse.bass_utils import run_bass_kernel_spmd

F32 = mybir.dt.float32
BF16 = mybir.dt.bfloat16
I32 = mybir.dt.int32
AF = mybir.ActivationFunctionType
ALU = mybir.AluOpType

D = 1024
SEQ = 2048
DEPTH = 4
DFF = 2816
NFF = DFF // 128
EPS = 1e-6
TT = 512
NMEM = 256
A_TOK = 3080
N_CORES = 8
SEQ_PER_CORE = 2


class Dep:
    __slots__ = ("w", "r", "name")

    def __init__(self, name=""):
        self.w = None
        self.r = []
        self.name = name


class Op:
    __slots__ = ("eng", "fn", "waits", "signal", "idx", "dma", "semval", "chan")

    def __init__(self, eng, fn):
        self.eng = eng
        self.fn = fn
        self.waits = []
        self.signal = False
        self.idx = 0
        self.dma = False
        self.semval = 0
        self.chan = None


ENGS = ("pe", "act", "dve", "pool", "sp")


class Prog:
    def __init__(self, nc):
        self.nc = nc
        self.streams = {e: [] for e in ENGS}
        self.waited = {e: {} for e in ENGS}
        self.chan_cnt = {}
        self.chan_last = {}

    def _add_wait(self, op, src):
        if src is None:
            return
        if src.dma:
            key = ("c", src.chan)
            val = src.semval
        else:
            if src.eng == op.eng and op.eng == "pe":
                return
            key = ("e", src.eng)
            val = src.idx
        w = self.waited[op.eng]
        if w.get(key, -1) >= val:
            return
        w[key] = val
        src.signal = True
        op.waits.append(src)

    def op(self, eng, fn, reads=(), writes=(), chan=None):
        o = Op(eng, fn)
        st = self.streams[eng]
        o.idx = len(st)
        if chan is not None:
            o.dma = True
            o.chan = chan
            n = self.chan_cnt.get(chan, 0) + 1
            self.chan_cnt[chan] = n
            o.semval = 16 * n
            o.signal = True
            prev = self.chan_last.get(chan)
            if prev is not None:
                self._add_wait(o, prev)
            self.chan_last[chan] = o
        for d in reads:
            self._add_wait(o, d.w)
        for d in writes:
            self._add_wait(o, d.w)
            for r in d.r:
                self._add_wait(o, r)
        for d in reads:
            d.r.append(o)
        for d in writes:
            d.w = o
            d.r = []
        st.append(o)
        return o

    def barrier(self, engs=("pe", "act", "dve", "pool", "sp")):
        last = {}
        for e in engs:
            last[e] = None
            for o_ in reversed(self.streams[e]):
                if o_.fn is not None:
                    last[e] = o_
                    break
        for e in engs:
            o = Op(e, None)
            o.idx = len(self.streams[e])
            for f in engs:
                if f != e and last[f] is not None:
                    self._add_wait(o, last[f])
            if o.waits:
                self.streams[e].append(o)

    def emit(self, final_waits):
        nc = self.nc
        with ExitStack() as es:
            esem = {e: es.enter_context(nc.semaphore("s_" + e)) for e in ENGS}
            csem = {c: es.enter_context(nc.semaphore("c_%s" % (c,))) for c in self.chan_cnt}
            for e in ENGS:
                n = 0
                for o in self.streams[e]:
                    if o.dma:
                        continue
                    if o.signal:
                        n += 1
                        o.semval = n
            block = es.enter_context(nc.Block())
            handles = {"pe": block.tensor, "act": block.scalar, "dve": block.vector,
                       "pool": block.gpsimd, "sp": block.sync}

            def run(e):
                def body(h):
                    for o in self.streams[e]:
                        for s in o.waits:
                            if s.dma:
                                h.wait_ge(csem[s.chan], s.semval)
                            else:
                                h.wait_ge(esem[s.eng], s.semval)
                        if o.fn is None:
                            continue
                        ins = o.fn(h)
                        if o.dma:
                            ins.then_inc(csem[o.chan], 16)
                        elif o.signal:
                            ins.then_inc(esem[e], 1)
                    if e == "sp":
                        for s in final_waits:
                            h.wait_ge(csem[s.chan], s.semval)
                return body

            for e in ENGS:
                handles[e](run(e))


class Builder:
    def __init__(self, inputs, n_super=8, n_layers=DEPTH, host=True):
        self.inp = inputs
        self.n_super = n_super
        self.n_layers = n_layers
        self.host = host
        self.nc = bass.Bass("TRN2", target_bir_lowering=False)
        self.P = Prog(self.nc)
        self.sb_off = 16640
        self.consts = []
        self.const_off = 0
        self.wgroups = []
        self.w_off = 0
        self.w_offsets = []
        self.wi = 0
        self.first_super = True
        self.psum_i = 0
        self.epsc = EPS

    def sb(self, name, shape, dt):
        size = int(np.prod(shape[1:])) * (4 if dt in (F32, I32) else 2)
        size = (size + 31) // 32 * 32
        t = self.nc.alloc_sbuf_tensor_at(name, list(shape), dt, offset=self.sb_off)
        self.sb_off += size
        return t

    def const(self, arr):
        arr = np.ascontiguousarray(arr, dtype=np.float32)
        assert arr.shape[0] == 128
        off = self.const_off
        self.consts.append(arr)
        self.const_off += arr.shape[1]
        return off

    def dump(self, name, ap, ncols, reads):
        if not getattr(self, "dbg_dump", False):
            return
        if not hasattr(self, "dumps"):
            self.dumps = []
            self.dump_off = 0
        off = self.dump_off
        self.dumps.append((name, off, ncols))
        self.dump_off += ncols
        self.P.op("pool", lambda h, ap=ap, off=off, ncols=ncols: h.dma_start(out=self.ddram[:, off:off + ncols], in_=ap),
                  reads=reads, chan="dbg")

    def bx(self, i):
        return self.psum[i % 8], self.psum_dep[i % 8]

    def bank(self):
        b = self.psum[self.psum_i % 8]
        d = self.psum_dep[self.psum_i % 8]
        self.psum_i += 1
        return b, d

    def wload(self, make_arr, n):
        if self.first_super:
            if self.host:
                a = np.ascontiguousarray(make_arr(), dtype=np.float32)
                assert a.shape == (128, n), (a.shape, n)
                self.wgroups.append(a)
            self.w_offsets.append((self.w_off, n))
            self.w_off += n
        off, n0 = self.w_offsets[self.wi]
        assert n0 == n
        self.wi += 1
        k = self.wslot_i % len(self.wslots)
        self.wslot_i += 1
        slot, dep = self.wslots[k], self.wslot_deps[k]
        assert n <= slot.shape[1]
        dst = slot[:, 0:n]
        self.P.op("pool", lambda h, dst=dst, off=off, n=n: h.dma_start(out=dst, in_=self.wdram[:, off:off + n]),
                  writes=[dep], chan=("w", k))
        return slot, dep

    def build(self):
        nc, P = self.nc, self.P
        self.xdram = nc.dram_tensor("xT", [SEQ_PER_CORE, D, SEQ], F32, kind="ExternalInput").ap()
        self.mdram = nc.dram_tensor("memT", [SEQ_PER_CORE, D, NMEM], F32, kind="ExternalInput").ap()
        self.pdram = nc.dram_tensor("pos", [SEQ_PER_CORE, 128, SEQ], F32, kind="ExternalInput").ap()
        self.odram = nc.dram_tensor("oT", [SEQ_PER_CORE, D, SEQ], F32, kind="ExternalOutput").ap()
        self._register_consts()
        self.cdram = nc.dram_tensor("consts", [128, self.const_off], F32, kind="ExternalInput").ap()
        self.cbdram = nc.dram_tensor("cbf", [128, self.cb_n], F32, kind="ExternalInput").ap()
        self.wmdram = nc.dram_tensor("wmem", [128, 8192], F32, kind="ExternalInput").ap()

        sb = self.sb
        self.C = sb("C", [128, self.const_off], F32)
        self.CB = sb("CB", [128, self.cb_n], BF16)
        self.xT = sb("xT", [128, 8, TT], F32)
        self.xn = sb("xn", [128, 8, TT], BF16)
        self.sq = sb("sq", [128, 8, TT], BF16)
        self.rstd = sb("rstd", [128, TT], F32)
        self.tmpA = sb("tmpA", [128, TT], F32)
        self.wslots = [sb("ws%d" % i, [128, 2048], BF16) for i in range(6)]
        self.wslot_deps = [Dep("ws%d" % i) for i in range(6)]
        self.wslot_i = 0
        self.negb = sb("negb", [128, 16], F32)
        self.esink = sb("esink", [128, 16], F32)
        self.cosF = sb("cosF", [128, TT], F32)
        self.sinF = sb("sinF", [128, TT], F32)
        self.zeros = sb("zeros", [128, TT], F32)
        self.rf = [sb("rf%d" % i, [128, TT], F32) for i in range(4)]
        self.d_rs = Dep("ropescratch")
        self.Dst = [[sb("D%d_%d" % (j, h), [128, 257], F32) for h in range(4)] for j in range(2)]
        self.Cb = [[sb("Cb%d_%d" % (j, h), [128, 257], BF16) for h in range(4)] for j in range(2)]
        self.nrep = [[sb("nr%d_%d" % (j, h), [128, 128], BF16) for h in range(4)] for j in range(2)]
        self.car = [sb("car%d" % j, [128, 12], F32) for j in range(2)]
        self.prevK = [sb("pK%d" % j, [128, 2, 128], BF16) for j in range(2)]
        self.prevV = [sb("pV%d" % j, [128, 4, 128], BF16) for j in range(2)]
        self.memKn = sb("memKn", [128, 4, 4, NMEM], BF16)
        self.memV = sb("memV", [128, 2, 512], BF16)
        self.region_base = self.sb_off
        self.psum = [nc.alloc_psum_tensor("ps%d" % i, [128, 512], F32) for i in range(8)]
        self.psum_dep = [Dep("ps%d" % i) for i in range(8)]

        self.d_C = Dep("C")
        self.d_xc = [Dep("xT%d" % c) for c in range(8)]
        self.d_xn = Dep("xn")
        self.d_sq = Dep("sq")
        self.d_sqc = [Dep("sq%d" % c) for c in range(8)]
        self.d_rstd = Dep("rstd")
        self.d_tmpA = Dep("tmpA")
        self.d_misc = Dep("misc")
        self.d_rope = Dep("rope")
        self.d_st = [[Dep("st%d_%d" % (j, h)) for h in range(4)] for j in range(2)]
        self.d_car = [Dep("car0"), Dep("car1")]
        self.d_prev = [Dep("prev0"), Dep("prev1")]
        self.d_mem = Dep("mem")

        self._alloc_ffn()
        self._alloc_prologue()
        self._alloc_mixA()
        self._alloc_mixB()
        assert self.region_max <= 229344, self.region_max

        P.op("sp", lambda h: h.dma_start(out=self.C[:, :], in_=self.cdram[:, :]), writes=[self.d_C], chan="c0")
        P.op("pool", lambda h: h.dma_start(out=self.CB[:, :], in_=self.cbdram[:, :]), writes=[self.d_C], chan="c1")
        ci = self.cidx
        P.op("dve", lambda h: h.tensor_scalar(out=self.negb[:, :], in0=self.C[:, ci["gb"]:ci["gb"] + 16],
                                              scalar1=-1.0, scalar2=None, op0=ALU.mult),
             reads=[self.d_C], writes=[self.d_misc])
        P.op("act", lambda h: h.activation(out=self.esink[:, :], in_=self.C[:, ci["sink"]:ci["sink"] + 16], func=AF.Exp),
             reads=[self.d_C], writes=[self.d_misc])
        P.op("dve", lambda h: h.memset(self.zeros[:, :], 0.0), writes=[self.d_misc])

        out_ops = []
        for s in range(self.n_super):
            seq, blk = divmod(s, SEQ // TT)
            self.wi = 0
            t0 = blk * TT
            if blk == 0:
                self.seq_start(seq)
            for c in range(8):
                P.op("sp", lambda h, seq=seq, t0=t0, c=c: h.dma_start(
                    out=self.xT[:, c, :], in_=self.xdram[seq, c * 128:(c + 1) * 128, t0:t0 + TT]),
                    writes=[self.d_xc[c]], chan=("x", c))
            if self.n_layers > 1 or os.environ.get("DBG_FORCEROPE"):
                self.rope_tables(seq, t0)
            for l in range(self.n_layers):
                self.ffn(l, 0)
                P.barrier()
                if l % 2 == 0:
                    self.mixA(l, blk)
                else:
                    self.mixB(l, blk)
                P.barrier()
                if getattr(self, "dbg_skip_ffn2", False):
                    continue
                self.ffn(l, 1)
            out_ops = []
            for c in range(8):
                o = P.op("sp", lambda h, seq=seq, t0=t0, c=c: h.dma_start(
                    out=self.odram[seq, c * 128:(c + 1) * 128, t0:t0 + TT], in_=self.xT[:, c, :]),
                    reads=[self.d_xc[c]], chan=("o", c))
                out_ops.append(o)
            self.first_super = False
        self.w_total = self.w_off
        if getattr(self, "dbg_dump", False):
            self.ddram = nc.dram_tensor("dbg", [128, max(self.dump_off, 1)], F32, kind="ExternalOutput").ap()
        self.wdram = nc.dram_tensor("wst", [128, self.w_total], F32, kind="ExternalInput").ap()
        P.emit(final_waits=out_ops)
        return nc

    def _register_consts(self):
        inp = self.inp
        c = {}

        def pcols(v):
            v = np.asarray(v, dtype=np.float32)
            return v.reshape(-1, 128).T

        def bc(v):
            v = np.asarray(v, dtype=np.float32).reshape(1, -1)
            return np.repeat(v, 128, axis=0)

        for l in range(DEPTH):
            c["f1g", l] = self.const(pcols(inp["ffn1_norm_g"][l]))
            c["mg", l] = self.const(pcols(inp["mix_norm_g"][l]))
            c["f2g", l] = self.const(pcols(inp["ffn2_norm_g"][l]))
            c["xaq", l] = self.const(pcols(inp["xa_q_norm_g"][l]))
            c["xak", l] = self.const(pcols(inp["xa_k_norm_g"][l]))
        c["memg"] = self.const(pcols(inp["mem_norm_g"]))
        c["gb"] = self.const(np.concatenate([bc(inp["a_gate_b"][0]), bc(inp["a_gate_b"][1])], axis=1))
        for j in range(2):
            c["ghn", j] = self.const(pcols(inp["a_h_norm_g"][j]))
            c["bqg", j] = self.const(np.tile(np.asarray(inp["b_q_norm_g"][j], np.float32), 2).reshape(128, 1))
            c["bkg", j] = self.const(np.tile(np.asarray(inp["b_k_norm_g"][j], np.float32), 2).reshape(128, 1))
        sk = []
        for j in range(2):
            s_ = np.asarray(inp["b_sinks"][j], np.float32)
            a = np.zeros((128, 8), np.float32)
            for cc in range(8):
                a[:64, cc] = s_[2 * cc]
                a[64:, cc] = s_[2 * cc + 1]
            sk.append(a)
        c["sink"] = self.const(np.concatenate(sk, axis=1))
        invf = (500000.0 ** (-np.arange(0, 16, 2, dtype=np.float32) / np.float32(16))).astype(np.float32)
        iv = np.zeros((128, 1), np.float32)
        for p in range(128):
            d_ = p % 64
            if d_ < 16:
                iv[p, 0] = invf[d_ % 8]
        c["invf"] = self.const(iv)
        self.cidx = c
        cb = {}
        n = 0
        cbl = []

        def addb(name, arr):
            nonlocal n
            cb[name] = n
            cbl.append(np.ascontiguousarray(arr, dtype=np.float32))
            n += arr.shape[1]

        addb("ones", np.ones((128, 128), np.float32))
        addb("ident", np.eye(128, dtype=np.float32))
        s_i = np.arange(128)[:, None]
        j_i = np.arange(128)[None, :]
        cur = (s_i <= j_i).astype(np.float32)
        prev = (s_i > j_i).astype(np.float32)
        addb("maskcur", cur)
        addb("mask4", np.concatenate([prev, prev, cur, cur], axis=1))
        bo = np.zeros((128, 128), np.float32)
        bo[:64, :64] = 1
        bo[64:, 64:] = 1
        addb("blockones", bo)
        rt = np.zeros((128, 128), np.float32)
        for m in range(128):
            d_ = m % 64
            if d_ < 8:
                rt[m + 8, m] = -1.0
            elif d_ < 16:
                rt[m - 8, m] = 1.0
        addb("RT", rt)
        lo = np.zeros((128, 128), np.float32)
        lo[:, :64] = 1
        hi = np.zeros((128, 128), np.float32)
        hi[:, 64:] = 1
        addb("onelo", lo)
        addb("onehi", hi)
        self.cb = cb
        self.cb_n = n
        self.cb_arr = np.concatenate(cbl, axis=1)

    def cbv(self, name, n=128):
        o = self.cb[name]
        return self.CB[:, o:o + n]

    def _region(self):
        self.sb_off = self.region_base

    def _region_end(self):
        self.region_max = max(getattr(self, "region_max", 0), self.sb_off)

    def _alloc_ffn(self):
        self._region()
        self.act = self.sb("act", [128, NFF, TT], BF16)
        self.sg = [self.sb("sg%d" % i, [128, TT], F32) for i in range(2)]
        self.d_act = [Dep("act%d" % j) for j in range(NFF)]
        self.d_sg = [Dep("sg0"), Dep("sg1")]
        self._region_end()

    def _alloc_common_mix(self):
        sb = self.sb
        self.yT = sb("yT", [128, 12, TT], BF16)
        self.d_y = [Dep("y%d" % i) for i in range(12)]
        self.qn = [sb("qn%d" % i, [128, TT], BF16) for i in range(2)]
        self.d_qn = [Dep("qn0"), Dep("qn1")]
        self.Em = [sb("Em%d" % i, [128, 2, TT], BF16) for i in range(2)]
        self.d_Em = [Dep("Em0"), Dep("Em1")]
        self.rden = sb("rden", [128, TT], F32)
        self.d_rden = Dep("rden")
        self.sqq = [sb("sqq%d" % i, [128, TT], BF16) for i in range(2)]
        self.d_sqq = [Dep("sqq0"), Dep("sqq1")]
        self.t1 = [sb("t1_%d" % i, [128, TT], F32) for i in range(3)]
        self.d_t1 = [Dep("t1_%d" % i) for i in range(3)]

    def _alloc_prologue(self):
        self._region()
        sb = self.sb
        self.mT = sb("mT", [128, 8, NMEM], F32)
        self.memn = sb("memn", [128, 8, NMEM], BF16)
        self.memK = sb("memK", [128, 4, NMEM], F32)
        self.rstdk = sb("rstdk", [128, 4, NMEM], F32)
        self.psq = sb("psq", [128, NMEM], BF16)
        self.d_psq = Dep("psq")
        self._region_end()

    def _alloc_mixA(self):
        self._region()
        sb = self.sb
        self._alloc_common_mix()
        self.mix_common_end = self.sb_off
        self.gt1 = sb("gt1", [128, TT], F32)
        self.negi = sb("negi", [128, TT], F32)
        self.nega = sb("nega", [128, TT], F32)
        self.Bext = sb("Bext", [128, TT + 1], F32)
        self.Aext = sb("Aext", [128, TT + 1], F32)
        self.tmpB = sb("tmpB", [128, TT], F32)
        self.uB = [sb("uB%d" % h, [128, TT], BF16) for h in range(4)]
        self.gB = [sb("gB%d" % h, [128, TT], BF16) for h in range(4)]
        self.eBA = [sb("eBA%d" % h, [128, TT], BF16) for h in range(4)]
        self.posM = [sb("posM%d" % h, [128, 4], F32) for h in range(4)]
        self.gbuf = [sb("gbuf%d" % h, [128, 8], F32) for h in range(4)]
        self.qp = [sb("qp%d" % h, [128, TT], BF16) for h in range(4)]
        self.kp = [sb("kp%d" % h, [128, TT], BF16) for h in range(4)]
        self.vtok = sb("vtok", [128, 4, 4, 257], BF16)
        self.SmT = [sb("SmT%d" % i, [128, 128], BF16) for i in range(4)]
        self.kptok = [sb("kptok%d" % i, [128, 128], BF16) for i in range(4)]
        self.numS = [sb("numS%d" % h, [128, 3, TT], F32) for h in range(4)]
        self.ogT = [sb("ogT%d" % i, [128, 2, TT], BF16) for i in range(2)]
        self.sqh = [sb("sqh%d" % i, [128, 2, TT], BF16) for i in range(2)]
        self.d_g = Dep("gatebufs")
        self.d_uB = [Dep("uB%d" % h) for h in range(4)]
        self.d_qp = [Dep("qp%d" % h) for h in range(4)]
        self.d_kp = [Dep("kp%d" % h) for h in range(4)]
        self.d_vtok = [Dep("vtok%d" % i) for i in range(4)]
        self.d_SmT = [Dep("SmT%d" % i) for i in range(4)]
        self.d_kptok = [Dep("kptok%d" % i) for i in range(4)]
        self.d_numS = [Dep("numS%d" % h) for h in range(4)]
        self.d_ogT = [Dep("ogT0"), Dep("ogT1")]
        self.d_sqh = [Dep("sqh0"), Dep("sqh1")]
        self.d_gh = [Dep("gh%d" % h) for h in range(4)]
        self._region_end()
        self.mixA_end = self.sb_off

    def _alloc_mixB(self):
        self.sb_off = self.mix_common_end
        sb = self.sb
        self.kT = [sb("kT%d" % g, [128, 640], BF16) for g in range(2)]
        self.d_kT = [Dep("kT0"), Dep("kT1")]
        self.vt = sb("vt", [128, 5, 4, 128], BF16)
        self.d_vt = [Dep("vt%d" % i) for i in range(5)]
        self.qlo = sb("qlo", [128, 8, TT], BF16)
        self.qhi = sb("qhi", [128, 8, TT], BF16)
        self.d_qT = [Dep("qT%d" % i) for i in range(8)]
        self.Es = [sb("Es%d" % i, [128, TT], BF16) for i in range(3)]
        self.d_Es = [Dep("Es%d" % i) for i in range(3)]
        self._region_end()


    def wload_mem(self, off, n):
        k = self.wslot_i % len(self.wslots)
        self.wslot_i += 1
        slot, dep = self.wslots[k], self.wslot_deps[k]
        self.P.op("pool", lambda h, slot=slot, off=off, n=n: h.dma_start(out=slot[:, 0:n], in_=self.wmdram[:, off:off + n]),
                  writes=[dep], chan=("w", k))
        return slot, dep

    def seq_start(self, seq):
        P = self.P
        ci = self.cidx
        P.barrier()
        for j in range(2):
            for h in range(4):
                P.op("pool", lambda hh, j=j, h=h: hh.memset(self.Dst[j][h][:, :], 0.0), writes=[self.d_st[j][h]])
                P.op("pool", lambda hh, j=j, h=h: hh.memset(self.Cb[j][h][:, :], 0.0), writes=[self.d_st[j][h]])
                P.op("pool", lambda hh, j=j, h=h: hh.memset(self.nrep[j][h][:, :], 0.0), writes=[self.d_st[j][h]])
            P.op("pool", lambda hh, j=j: hh.memset(self.car[j][:, 0:8], 0.0), writes=[self.d_car[j]])
            P.op("pool", lambda hh, j=j: hh.memset(self.car[j][:, 8:12], 1.0), writes=[self.d_car[j]])
        d_m = self.d_mem
        d_mt = Dep("mT")
        P.op("sp", lambda h: h.dma_start(out=self.mT[:, :, :],
                                         in_=self.mdram[seq, :, :].rearrange("(c p) t -> p c t", p=128)),
             writes=[d_mt], chan="m")
        ones = self.cbv("ones")
        sqv = self.sq[:, :, 0:NMEM]
        for c in range(8):
            P.op("act", lambda h, c=c: h.activation(out=self.sq[:, c, 0:NMEM], in_=self.mT[:, c, :], func=AF.Square),
                 reads=[d_mt], writes=[self.d_sq])
        bk, bd = self.bank()
        for c in range(8):
            P.op("pe", lambda h, c=c, bk=bk: h.matmul(bk[:, 0:NMEM], ones, self.sq[:, c, 0:NMEM], start=(c == 0), stop=(c == 7)),
                 reads=[self.d_sq, self.d_C], writes=[bd])
        P.op("act", lambda h, bk=bk: h.activation(out=self.tmpA[:, 0:NMEM], in_=bk[:, 0:NMEM], func=AF.Ln,
                                                   bias=self.epsc, scale=1.0 / D), reads=[bd, self.d_C], writes=[self.d_tmpA])
        P.op("act", lambda h: h.activation(out=self.rstd[:, 0:NMEM], in_=self.tmpA[:, 0:NMEM], func=AF.Exp, scale=-0.5),
             reads=[self.d_tmpA], writes=[self.d_rstd])
        d_memn = Dep("memn")
        for c in range(8):
            P.op("dve", lambda h, c=c: h.scalar_tensor_tensor(
                out=self.memn[:, c, :], in0=self.mT[:, c, :], scalar=self.C[:, ci["memg"] + c:ci["memg"] + c + 1],
                in1=self.rstd[:, 0:NMEM], op0=ALU.mult, op1=ALU.mult),
                reads=[d_mt, self.d_rstd, self.d_C], writes=[d_memn])
        d_memK = Dep("memK")
        for hd in range(4):
            slot, wd = self.wload_mem(hd * 1024, 1024)
            bk, bd = self.bank()
            for k in range(8):
                P.op("pe", lambda h, k=k, slot=slot, bk=bk: h.matmul(
                    bk[:, 0:NMEM], slot[:, k * 128:(k + 1) * 128], self.memn[:, k, :], start=(k == 0), stop=(k == 7)),
                    reads=[wd, d_memn], writes=[bd])
            P.op("act", lambda h, hd=hd, bk=bk: h.activation(out=self.memK[:, hd, :], in_=bk[:, 0:NMEM], func=AF.Copy),
                 reads=[bd], writes=[d_memK])
            P.op("act", lambda h, hd=hd, bk=bk: h.activation(out=self.psq[:, :], in_=bk[:, 0:NMEM], func=AF.Square),
                 reads=[bd], writes=[self.d_psq])
            b2, d2 = self.bank()
            P.op("pe", lambda h, b2=b2: h.matmul(b2[:, 0:NMEM], ones, self.psq[:, :], start=True, stop=True),
                 reads=[self.d_psq, self.d_C], writes=[d2])
            P.op("act", lambda h, b2=b2: h.activation(out=self.tmpA[:, 0:NMEM], in_=b2[:, 0:NMEM], func=AF.Ln,
                                                       bias=self.epsc, scale=1.0 / 128), reads=[d2], writes=[self.d_tmpA])
            P.op("act", lambda h, hd=hd: h.activation(out=self.rstdk[:, hd, :], in_=self.tmpA[:, 0:NMEM], func=AF.Exp, scale=-0.5),
                 reads=[self.d_tmpA], writes=[d_memK])
            for l in range(4):
                P.op("dve", lambda h, hd=hd, l=l: h.scalar_tensor_tensor(
                    out=self.memKn[:, l, hd, :], in0=self.memK[:, hd, :], scalar=self.C[:, ci["xak", l]:ci["xak", l] + 1],
                    in1=self.rstdk[:, hd, :], op0=ALU.mult, op1=ALU.mult),
                    reads=[d_memK, self.d_C], writes=[d_m])
        vs = [self.wload_mem(4096, 2048), self.wload_mem(6144, 2048)]
        for mc in range(2):
            bk, bd = self.bank()
            for hf in range(2):
                slot, wd = vs[hf]
                for k in range(8):
                    P.op("pe", lambda h, k=k, mc=mc, hf=hf, slot=slot, bk=bk: h.matmul(
                        bk[:, hf * 256:(hf + 1) * 256], self.memn[:, k, mc * 128:(mc + 1) * 128], slot[:, k * 256:(k + 1) * 256],
                        start=(k == 0), stop=(k == 7)), reads=[wd, d_memn], writes=[bd])
            P.op("act", lambda h, mc=mc, bk=bk: h.activation(out=self.memV[:, mc, :], in_=bk[:, :], func=AF.Copy),
                 reads=[bd], writes=[d_m])
        P.barrier()

    def rope_tables(self, seq, t0):
        P = self.P
        if os.environ.get("DBG_NOROPE"):
            P.op("dve", lambda h: h.memset(self.cosF[:, :], 1.0), writes=[self.d_rope])
            P.op("dve", lambda h: h.memset(self.sinF[:, :], 0.0), writes=[self.d_rope])
            return
        ci = self.cidx
        ds = self.d_rs
        if os.environ.get("DBG_ROPEBAR"):
            P.op("dve", lambda h: h.memset(self.cosF[:, :], 1.0), writes=[self.d_rope])
            P.op("dve", lambda h: h.memset(self.sinF[:, :], 0.0), writes=[self.d_rope])
            P.barrier()
            return
        TWO_PI = 2.0 * math.pi
        C1 = 6.28125
        C2 = TWO_PI - C1
        PIB = 3.141592
        posf, ang, y, r = self.rf
        P.op("sp", lambda h: h.dma_start(out=posf[:, :], in_=self.pdram[seq, :, t0:t0 + TT]),
             writes=[ds], chan="p")
        P.op("dve", lambda h: h.tensor_scalar(out=ang[:, :], in0=posf[:, :], scalar1=self.C[:, ci["invf"]:ci["invf"] + 1],
                                              scalar2=None, op0=ALU.mult), reads=[ds, self.d_C], writes=[ds])
        MAGIC = 12582912.0
        for which, dst in ((0, self.sinF), (1, self.cosF)):
            if which == 1:
                P.op("dve", lambda h: h.tensor_scalar(out=ang[:, :], in0=ang[:, :], scalar1=math.pi / 2, scalar2=None, op0=ALU.add),
                     reads=[ds], writes=[ds])
            P.op("dve", lambda h: h.tensor_scalar(out=y[:, :], in0=ang[:, :], scalar1=1.0 / TWO_PI, scalar2=MAGIC,
                                                  op0=ALU.mult, op1=ALU.add), reads=[ds], writes=[ds])
            P.op("dve", lambda h: h.tensor_scalar(out=y[:, :], in0=y[:, :], scalar1=-MAGIC, scalar2=None, op0=ALU.add),
                 reads=[ds], writes=[ds])
            P.op("dve", lambda h: h.scalar_tensor_tensor(out=r[:, :], in0=y[:, :], scalar=-C1, in1=ang[:, :],
                                                         op0=ALU.mult, op1=ALU.add), reads=[ds], writes=[ds])
            P.op("dve", lambda h: h.scalar_tensor_tensor(out=r[:, :], in0=y[:, :], scalar=-C2, in1=r[:, :],
                                                         op0=ALU.mult, op1=ALU.add), reads=[ds], writes=[ds])
            P.op("dve", lambda h: h.tensor_scalar(out=y[:, :], in0=r[:, :], scalar1=-math.pi, scalar2=TWO_PI,
                                                  op0=ALU.is_lt, op1=ALU.mult), reads=[ds], writes=[ds])
            P.op("dve", lambda h: h.tensor_tensor(out=r[:, :], in0=r[:, :], in1=y[:, :], op=ALU.add), reads=[ds], writes=[ds])
            P.op("dve", lambda h: h.tensor_scalar(out=y[:, :], in0=r[:, :], scalar1=math.pi, scalar2=-TWO_PI,
                                                  op0=ALU.is_gt, op1=ALU.mult), reads=[ds], writes=[ds])
            P.op("dve", lambda h: h.tensor_tensor(out=r[:, :], in0=r[:, :], in1=y[:, :], op=ALU.add), reads=[ds], writes=[ds])
            P.op("dve", lambda h: h.tensor_scalar(out=r[:, :], in0=r[:, :], scalar1=-PIB, scalar2=PIB,
                                                  op0=ALU.max, op1=ALU.min), reads=[ds], writes=[ds])
            s2 = posf if which == 1 else self.rstd
            dd = ds
            P.op("dve", lambda h, s2=s2: h.tensor_tensor(out=s2[:, :], in0=r[:, :], in1=r[:, :], op=ALU.mult),
                 reads=[ds, self.d_rstd], writes=[ds, self.d_rstd])
            P.op("dve", lambda h, s2=s2: h.tensor_scalar(out=y[:, :], in0=s2[:, :], scalar1=1.0 / 6227020800.0, scalar2=None, op0=ALU.mult),
                 reads=[ds, self.d_rstd], writes=[ds])
            for cf in (-1.0 / 39916800.0, 1.0 / 362880.0, -1.0 / 5040.0, 1.0 / 120.0, -1.0 / 6.0):
                P.op("dve", lambda h, s2=s2, cf=cf: h.scalar_tensor_tensor(out=y[:, :], in0=y[:, :], scalar=cf, in1=s2[:, :],
                                                                           op0=ALU.add, op1=ALU.mult),
                     reads=[ds, self.d_rstd], writes=[ds])
            P.op("dve", lambda h, dst=dst: h.scalar_tensor_tensor(out=dst[:, :], in0=y[:, :], scalar=1.0, in1=r[:, :],
                                                                  op0=ALU.add, op1=ALU.mult),
                 reads=[ds], writes=[self.d_rope])
        self.dump("cosF", self.cosF[:, :], TT, [self.d_rope])
        self.dump("sinF", self.sinF[:, :], TT, [self.d_rope])

    def feat_proj(self, mk, nk=8, rhs=None, d_rhs=None, ncols=TT, bank=None):
        P = self.P
        slot, wd = self.wload(mk, nk * 128)
        bk, bd = self.bank() if bank is None else self.bx(bank)
        for k in range(nk):
            P.op("pe", lambda h, k=k, slot=slot, bk=bk: h.matmul(
                bk[:, 0:ncols], slot[:, k * 128:(k + 1) * 128], self.xn[:, k, :], start=(k == 0), stop=(k == nk - 1)),
                reads=[wd, self.d_xn], writes=[bd])
        return bk, bd

    def colchunk(self, w, c0):
        return lambda: w[:, c0:c0 + 128].reshape(8, 128, 128).transpose(1, 0, 2).reshape(128, 1024)

    def mem_attn(self, l, w, xq0):
        P = self.P
        ci = self.cidx
        ones = self.cbv("ones")
        st = {}

        def stA(hd):
            bq, dq = self.feat_proj(self.colchunk(w, xq0 + hd * 128) if self.host else None, bank=hd % 2)
            i2 = hd % 2
            P.op("act", lambda h: h.activation(out=self.sqq[i2][:, :], in_=bq[:, :], func=AF.Square),
                 reads=[dq], writes=[self.d_sqq[i2]])
            bs, ds = self.bx(2)
            P.op("pe", lambda h: h.matmul(bs[:, :], ones, self.sqq[i2][:, :], start=True, stop=True),
                 reads=[self.d_sqq[i2], self.d_C], writes=[ds])
            P.op("act", lambda h: h.activation(out=self.tmpA[:, :], in_=bs[:, :], func=AF.Ln,
                                               bias=self.epsc, scale=1.0 / 128), reads=[ds], writes=[self.d_tmpA])
            P.op("act", lambda h: h.activation(out=self.rstd[:, :], in_=self.tmpA[:, :], func=AF.Exp, scale=-0.5),
                 reads=[self.d_tmpA], writes=[self.d_rstd])
            P.op("dve", lambda h: h.scalar_tensor_tensor(
                out=self.qn[i2][:, :], in0=bq[:, :], scalar=self.C[:, ci["xaq", l]:ci["xaq", l] + 1],
                in1=self.rstd[:, :], op0=ALU.mult, op1=ALU.mult),
                reads=[dq, self.d_rstd, self.d_C], writes=[self.d_qn[i2]])

        def stB(hd):
            i2 = hd % 2
            for mc in range(2):
                b_, d_ = self.bx(3 + mc)
                P.op("pe", lambda h, b_=b_, mc=mc: h.matmul(
                    b_[:, :], self.memKn[:, l, hd, mc * 128:(mc + 1) * 128], self.qn[i2][:, :], start=True, stop=True),
                    reads=[self.d_mem, self.d_qn[i2]], writes=[d_])
                P.op("act", lambda h, b_=b_, mc=mc: h.activation(
                    out=self.Em[i2][:, mc, :], in_=b_[:, :], func=AF.Exp, scale=128.0 ** -0.5),
                    reads=[d_], writes=[self.d_Em[i2]])

        def stC(hd):
            i2 = hd % 2
            bo, do = self.bx(5 + i2)
            for mc in range(2):
                P.op("pe", lambda h, mc=mc: h.matmul(
                    bo[:, :], self.memV[:, mc, hd * 128:(hd + 1) * 128], self.Em[i2][:, mc, :],
                    start=(mc == 0), stop=(mc == 1)), reads=[self.d_mem, self.d_Em[i2]], writes=[do])
            bd_, dd_ = self.bx(7)
            for mc in range(2):
                P.op("pe", lambda h, mc=mc: h.matmul(
                    bd_[:, :], ones, self.Em[i2][:, mc, :], start=(mc == 0), stop=(mc == 1)),
                    reads=[self.d_C, self.d_Em[i2]], writes=[dd_])
            P.op("act", lambda h: h.activation(out=self.rden[:, :], in_=bd_[:, :], func=AF.Ln),
                 reads=[dd_], writes=[self.d_rden])
            P.op("act", lambda h: h.activation(out=self.rden[:, :], in_=self.rden[:, :], func=AF.Exp, scale=-1.0),
                 reads=[self.d_rden], writes=[self.d_rden])
            P.op("dve", lambda h: h.tensor_tensor(
                out=self.yT[:, 8 + hd, :], in0=bo[:, :], in1=self.rden[:, :], op=ALU.mult),
                reads=[do, self.d_rden], writes=[self.d_y[8 + hd]])

        for s in range(6):
            if s < 4:
                stA(s)
            if 0 <= s - 1 < 4:
                stB(s - 1)
            if 0 <= s - 2 < 4:
                stC(s - 2)
        self.psum_i = 0

    def out_proj(self, w_out):
        P = self.P
        for c in range(8):
            def mk(c=c):
                a = w_out[:, c * 128:(c + 1) * 128].reshape(12, 128, 128)
                return a.transpose(1, 0, 2).reshape(128, 12 * 128)
            slot, wd = self.wload(mk, 12 * 128)
            bk, bd = self.bank()
            for k in range(12):
                P.op("pe", lambda h, k=k, slot=slot, bk=bk: h.matmul(
                    bk[:, :], slot[:, k * 128:(k + 1) * 128], self.yT[:, k, :], start=(k == 0), stop=(k == 11)),
                    reads=[wd, self.d_y[k]], writes=[bd])
            P.op("dve", lambda h, c=c, bk=bk: h.tensor_tensor(
                out=self.xT[:, c, :], in0=bk[:, :], in1=self.xT[:, c, :], op=ALU.add),
                reads=[bd, self.d_xc[c]], writes=[self.d_xc[c]])

    def mixA(self, l, blk):
        P = self.P
        ci = self.cidx
        j = l // 2
        w = self.inp["a_w_in"][j] if self.host else None
        w_out = self.inp["a_w_out"][j] if self.host else None
        self.rmsnorm(ci["mg", l])
        ones = self.cbv("ones")
        ident = self.cbv("ident")
        maskcur = self.cbv("maskcur")
        LNS = math.log(128.0 ** -0.5)
        car = self.car[j]
        d_car = self.d_car[j]
        dg = self.d_g
        for hd in range(4):
            def mkg(hd=hd):
                g = np.stack([w[:, 3072 + hd], w[:, 3076 + hd]], axis=1).reshape(8, 128, 2)
                g = g.transpose(1, 0, 2)[:, :, :, None]
                return np.broadcast_to(g, (128, 8, 2, 128)).reshape(128, 2048)
            slot, wd = self.wload(mkg, 2048)
            gb_ = []
            for g in range(2):
                bk, bd = self.bank()
                for k in range(8):
                    P.op("pe", lambda h, k=k, g=g, slot=slot, bk=bk: h.matmul(
                        bk[:, :], slot[:, (k * 2 + g) * 128:(k * 2 + g + 1) * 128], self.xn[:, k, :],
                        start=(k == 0), stop=(k == 7)), reads=[wd, self.d_xn], writes=[bd])
                gb_.append((bk, bd))
            (bi, dbi), (bf, dbf) = gb_
            nb = self.negb
            P.op("act", lambda h, bf=bf, hd=hd: h.activation(out=self.gt1[:, :], in_=bf[:, :], func=AF.Exp,
                                                               bias=nb[:, 8 * j + 4 + hd:8 * j + 5 + hd], scale=-1.0),
                 reads=[dbf, self.d_misc], writes=[dg])
            P.op("act", lambda h: h.activation(out=self.gt1[:, :], in_=self.gt1[:, :], func=AF.Ln, bias=1.0, scale=1.0),
                 reads=[dg], writes=[dg])
            P.op("dve", lambda h, hd=hd: h.tensor_copy(out=self.Bext[:, 0:1], in_=car[:, hd:hd + 1]),
                 reads=[d_car], writes=[dg])
            P.op("dve", lambda h: h.tensor_tensor_scan(out=self.Bext[:, 1:TT + 1], data0=self.gt1[:, :], data1=self.zeros[:, :],
                                                        initial=self.Bext[:, 0:1], op0=ALU.add, op1=ALU.add),
                 reads=[dg, self.d_misc], writes=[dg])
            P.op("act", lambda h, bi=bi, hd=hd: h.activation(out=self.negi[:, :], in_=bi[:, :], func=AF.Identity,
                                                               bias=nb[:, 8 * j + hd:8 * j + hd + 1], scale=-1.0),
                 reads=[dbi, self.d_misc], writes=[dg])
            P.op("dve", lambda h: h.tensor_tensor(out=self.nega[:, :], in0=self.negi[:, :], in1=self.Bext[:, 1:TT + 1],
                                                   op=ALU.subtract), reads=[dg], writes=[dg])
            P.op("dve", lambda h, hd=hd: h.tensor_copy(out=self.Aext[:, 0:1], in_=car[:, 4 + hd:5 + hd]),
                 reads=[d_car], writes=[dg])
            P.op("dve", lambda h: h.tensor_tensor_scan(out=self.Aext[:, 1:TT + 1], data0=self.nega[:, :], data1=self.nega[:, :],
                                                        initial=self.Aext[:, 0:1], op0=ALU.min, op1=ALU.min),
                 reads=[dg], writes=[dg])
            P.op("dve", lambda h, hd=hd: h.tensor_scalar(out=self.posM[hd][:, :], in0=self.Aext[:, 0:TT:128],
                                                          scalar1=-1.0, scalar2=LNS, op0=ALU.mult, op1=ALU.add),
                 reads=[dg], writes=[self.d_gh[hd]])
            P.op("dve", lambda h, hd=hd: h.tensor_copy(out=self.gbuf[hd][:, 0:1], in_=car[:, 8 + hd:9 + hd]),
                 reads=[d_car], writes=[self.d_gh[hd]])
            P.op("dve", lambda h, hd=hd: h.tensor_tensor(out=self.gbuf[hd][:, 1:5], in0=self.Aext[:, 128:TT + 1:128],
                                                          in1=self.Aext[:, 0:TT:128], op=ALU.subtract),
                 reads=[dg], writes=[self.d_gh[hd]])
            P.op("act", lambda h, hd=hd: h.activation(out=self.gbuf[hd][:, 1:5], in_=self.gbuf[hd][:, 1:5], func=AF.Exp),
                 reads=[self.d_gh[hd]], writes=[self.d_gh[hd]])
            P.op("dve", lambda h: h.tensor_tensor(out=self.tmpB[:, :], in0=self.Bext[:, 1:TT + 1], in1=self.Aext[:, 1:TT + 1],
                                                   op=ALU.add), reads=[dg], writes=[dg])
            P.op("act", lambda h, hd=hd: h.activation(out=self.eBA[hd][:, :], in_=self.tmpB[:, :], func=AF.Exp),
                 reads=[dg], writes=[self.d_gh[hd]])
            for c in range(4):
                cs = slice(c * 128, (c + 1) * 128)
                P.op("act", lambda h, hd=hd, c=c, cs=cs: h.activation(
                    out=self.uB[hd][:, cs], in_=self.nega[:, cs], func=AF.Exp,
                    bias=self.Aext[:, c * 128:c * 128 + 1], scale=-1.0), reads=[dg], writes=[self.d_uB[hd]])
                P.op("act", lambda h, hd=hd, c=c, cs=cs: h.activation(
                    out=self.gB[hd][:, cs], in_=self.Aext[:, 1 + c * 128:1 + (c + 1) * 128], func=AF.Exp,
                    bias=self.posM[hd][:, c:c + 1], scale=1.0), reads=[dg, self.d_gh[hd]], writes=[self.d_uB[hd]])
            P.op("dve", lambda h, hd=hd: h.tensor_copy(out=car[:, hd:hd + 1], in_=self.Bext[:, TT:TT + 1]),
                 reads=[dg], writes=[d_car])
            P.op("dve", lambda h, hd=hd: h.tensor_copy(out=car[:, 4 + hd:5 + hd], in_=self.Aext[:, TT:TT + 1]),
                 reads=[dg], writes=[d_car])
            P.op("dve", lambda h, hd=hd: h.tensor_copy(out=car[:, 8 + hd:9 + hd], in_=self.gbuf[hd][:, 4:5]),
                 reads=[self.d_gh[hd]], writes=[d_car])
            bq, dq = self.feat_proj(self.colchunk(w, hd * 128) if self.host else None)
            P.op("dve", lambda h, hd=hd, bq=bq: h.tensor_tensor(out=self.qp[hd][:, :], in0=bq[:, :], in1=self.gB[hd][:, :],
                                                                 op=ALU.mult), reads=[dq, self.d_uB[hd]], writes=[self.d_qp[hd]])
            bk_, dk_ = self.feat_proj(self.colchunk(w, 512 + hd * 128) if self.host else None)
            P.op("dve", lambda h, hd=hd, bk_=bk_: h.tensor_tensor(out=self.kp[hd][:, :], in0=bk_[:, :], in1=self.uB[hd][:, :],
                                                                   op=ALU.mult), reads=[dk_, self.d_uB[hd]], writes=[self.d_kp[hd]])
        if blk == 0 or True:
            P.op("pool", lambda h: h.memset(self.vtok[:, :, :, 256:257], 1.0), writes=self.d_vtok)
        for qh in range(4):
            def mk(qh=qh):
                v = w[:, 1024 + qh * 256:1024 + (qh + 1) * 256].reshape(8, 128, 256)
                return v.transpose(1, 0, 2).reshape(128, 2048)
            slot, wd = self.wload(mk, 2048)
            for b in range(4):
                bk, bd = self.bank()
                for k in range(8):
                    P.op("pe", lambda h, k=k, b=b, slot=slot, bk=bk: h.matmul(
                        bk[:, 0:256], self.xn[:, k, b * 128:(b + 1) * 128], slot[:, k * 256:(k + 1) * 256],
                        start=(k == 0), stop=(k == 7)), reads=[wd, self.d_xn], writes=[bd])
                if b % 2 == 0:
                    P.op("act", lambda h, b=b, qh=qh, bk=bk: h.activation(
                        out=self.vtok[:, b, qh, 0:256], in_=bk[:, 0:256], func=AF.Copy), reads=[bd], writes=[self.d_vtok[b]])
                else:
                    P.op("dve", lambda h, b=b, qh=qh, bk=bk: h.tensor_copy(
                        out=self.vtok[:, b, qh, 0:256], in_=bk[:, 0:256]), reads=[bd], writes=[self.d_vtok[b]])
        self.mem_attn(l, w, A_TOK)
        its = [(c, hd) for c in range(4) for hd in range(4)]
        cst = {}

        def ck_ST(i):
            c, hd = its[i]
            cs = slice(c * 128, (c + 1) * 128)
            r = i % 4
            b1, d1 = self.bx(0 + i % 2)
            P.op("pe", lambda h: h.matmul(b1[:, 0:128], self.kp[hd][:, cs], self.qp[hd][:, cs], start=True, stop=True),
                 reads=[self.d_kp[hd], self.d_qp[hd]], writes=[d1])
            P.op("dve", lambda h: h.tensor_tensor(out=self.SmT[r][:, :], in0=b1[:, 0:128], in1=maskcur, op=ALU.mult),
                 reads=[d1, self.d_C], writes=[self.d_SmT[r]])
            b2, d2 = self.bx(2 + i % 2)
            b2b = b2.bitcast(BF16)
            P.op("pe", lambda h: h.transpose(b2b[:, 0:128], self.kp[hd][:, cs], ident),
                 reads=[self.d_kp[hd], self.d_C], writes=[d2])
            P.op("act", lambda h: h.activation(out=self.kptok[r][:, :], in_=b2b[:, 0:128], func=AF.Copy),
                 reads=[d2], writes=[self.d_kptok[r]])

        def ck_ND(i):
            c, hd = its[i]
            cs = slice(c * 128, (c + 1) * 128)
            r = i % 4
            dst = self.d_st[j][hd]
            b3, d3 = self.bx(4 + i % 2)
            for g in range(3):
                if g < 2:
                    l1 = self.vtok[:, c, hd, g * 128:(g + 1) * 128]
                    l2 = self.Cb[j][hd][:, g * 128:(g + 1) * 128]
                else:
                    l1 = ones
                    l2 = self.nrep[j][hd][:, :]
                P.op("pe", lambda h, g=g, l1=l1: h.matmul(b3[:, g * 128:(g + 1) * 128], l1, self.SmT[r][:, :],
                                                          start=True, stop=False),
                     reads=[self.d_vtok[c], self.d_SmT[r], self.d_C], writes=[d3])
                P.op("pe", lambda h, g=g, l2=l2: h.matmul(b3[:, g * 128:(g + 1) * 128], l2, self.qp[hd][:, cs],
                                                          start=False, stop=True),
                     reads=[dst, self.d_qp[hd]], writes=[d3])
            P.op("act", lambda h: h.activation(
                out=self.numS[hd][:, :, cs], in_=b3[:, 0:384].rearrange("p (g n) -> p g n", g=3), func=AF.Copy),
                reads=[d3], writes=[self.d_numS[hd]])
            b4, d4 = self.bx(6 + i % 2)
            P.op("pe", lambda h: h.matmul(b4[:, 0:257], self.kptok[r][:, :], self.vtok[:, c, hd, :], start=True, stop=True),
                 reads=[self.d_kptok[r], self.d_vtok[c]], writes=[d4])
            P.op("dve", lambda h: h.scalar_tensor_tensor(
                out=self.Dst[j][hd][:, :], in0=self.Dst[j][hd][:, :], scalar=self.gbuf[hd][:, c:c + 1], in1=b4[:, 0:257],
                op0=ALU.mult, op1=ALU.add), reads=[d4, dst, self.d_gh[hd]], writes=[dst])
            P.op("act", lambda h: h.activation(out=self.Cb[j][hd][:, :], in_=self.Dst[j][hd][:, :], func=AF.Identity,
                                               scale=self.gbuf[hd][:, c + 1:c + 2]),
                 reads=[dst, self.d_gh[hd]], writes=[dst])
            P.op("act", lambda h: h.activation(
                out=self.nrep[j][hd][:, :], in_=self.Dst[j][hd][:, 256:257].to_broadcast([128, 128]), func=AF.Identity,
                scale=self.gbuf[hd][:, c + 1:c + 2]), reads=[dst, self.d_gh[hd]], writes=[dst])

        ck_ST(0)
        for i in range(len(its)):
            if i + 1 < len(its):
                ck_ST(i + 1)
            ck_ND(i)
        self.psum_i = 0
        for hd in range(4):
            i2 = hd % 2
            for dvc in range(2):
                bo, do = self.feat_proj(self.colchunk(w, 2048 + (2 * hd + dvc) * 128) if self.host else None)
                P.op("act", lambda h, bo=bo, i2=i2, dvc=dvc: h.activation(out=self.ogT[i2][:, dvc, :], in_=bo[:, :], func=AF.Sigmoid),
                     reads=[do], writes=[self.d_ogT[i2]])
            ns = self.numS[hd]
            for dvc in range(2):
                P.op("act", lambda h, ns=ns, i2=i2, dvc=dvc: h.activation(out=self.sqh[i2][:, dvc, :], in_=ns[:, dvc, :], func=AF.Square),
                     reads=[self.d_numS[hd]], writes=[self.d_sqh[i2]])
            bm, dm = self.bank()
            for dvc in range(2):
                P.op("pe", lambda h, bm=bm, i2=i2, dvc=dvc: h.matmul(bm[:, :], ones, self.sqh[i2][:, dvc, :],
                                                                     start=(dvc == 0), stop=(dvc == 1)),
                     reads=[self.d_sqh[i2], self.d_C], writes=[dm])
            t0_, t1_, t2_ = self.t1
            dt0, dt1, dt2 = self.d_t1
            P.op("act", lambda h, ns=ns: h.activation(out=t0_[:, :], in_=ns[:, 2, :], func=AF.Abs),
                 reads=[self.d_numS[hd]], writes=[dt0])
            P.op("dve", lambda h, hd=hd: h.tensor_tensor(out=t0_[:, :], in0=t0_[:, :], in1=self.eBA[hd][:, :], op=ALU.max),
                 reads=[dt0, self.d_gh[hd]], writes=[dt0])
            P.op("dve", lambda h: h.scalar_tensor_tensor(out=t1_[:, :], in0=t0_[:, :], scalar=EPS, in1=t0_[:, :],
                                                          op0=ALU.mult, op1=ALU.mult), reads=[dt0], writes=[dt1])
            P.op("dve", lambda h, bm=bm: h.scalar_tensor_tensor(out=t2_[:, :], in0=bm[:, :], scalar=1.0 / 256, in1=t1_[:, :],
                                                                 op0=ALU.mult, op1=ALU.add), reads=[dm, dt1], writes=[dt2])
            P.op("act", lambda h: h.activation(out=t2_[:, :], in_=t2_[:, :], func=AF.Ln), reads=[dt2], writes=[dt2])
            P.op("act", lambda h: h.activation(out=t0_[:, :], in_=t2_[:, :], func=AF.Exp, scale=-0.5), reads=[dt2], writes=[dt0])
            gh0 = ci["ghn", j]
            for dvc in range(2):
                ch = 2 * hd + dvc
                P.op("dve", lambda h, ns=ns, dvc=dvc, ch=ch: h.scalar_tensor_tensor(
                    out=t1_[:, :], in0=ns[:, dvc, :], scalar=self.C[:, gh0 + ch:gh0 + ch + 1], in1=t0_[:, :],
                    op0=ALU.mult, op1=ALU.mult), reads=[self.d_numS[hd], dt0, self.d_C], writes=[dt1])
                P.op("dve", lambda h, i2=i2, dvc=dvc, ch=ch: h.tensor_tensor(
                    out=self.yT[:, ch, :], in0=t1_[:, :], in1=self.ogT[i2][:, dvc, :], op=ALU.mult),
                    reads=[dt1, self.d_ogT[i2]], writes=[self.d_y[ch]])
        self.out_proj(w_out)

    def mixB(self, l, blk):
        P = self.P
        ci = self.cidx
        j = l // 2
        w = self.inp["b_w_in"][j] if self.host else None
        w_out = self.inp["b_w_out"][j] if self.host else None
        self.rmsnorm(ci["mg", l])
        mask4 = self.cbv("mask4", 512)
        lohi = [self.cbv("onelo"), self.cbv("onehi")]
        bones = self.cbv("blockones")
        RT = self.cbv("RT")
        dprev = self.d_prev[j]
        P.op("pool", lambda h: h.memset(self.vt[:, :, :, :], 0.0), writes=self.d_vt)
        if blk > 0:
            P.op("pool", lambda h: h.tensor_copy(out=self.vt[:, 0, :, :], in_=self.prevV[j][:, :, :]),
                 reads=[dprev], writes=[self.d_vt[0]])
        slot, wd = self.wload(self.colchunk(w, 1152) if self.host else None, 1024)
        for b in range(4):
            bk, bd = self.bank()
            for k in range(8):
                P.op("pe", lambda h, k=k, b=b, slot=slot, bk=bk: h.matmul(
                    bk[:, 0:128], self.xn[:, k, b * 128:(b + 1) * 128], slot[:, k * 128:(k + 1) * 128],
                    start=(k == 0), stop=(k == 7)), reads=[wd, self.d_xn], writes=[bd])
            pv = bk[:, 0:128].rearrange("p (g n) -> p g n", g=2)
            P.op("act", lambda h, b=b, pv=pv: h.activation(out=self.vt[:, b + 1, 0:4:2, 0:64], in_=pv, func=AF.Copy),
                 reads=[bd], writes=[self.d_vt[b + 1]])
            P.op("dve", lambda h, b=b, pv=pv: h.tensor_copy(out=self.vt[:, b + 1, 1:4:2, 64:128], in_=pv),
                 reads=[bd], writes=[self.d_vt[b + 1]])
        P.op("pool", lambda h: h.memset(self.qlo[:, :, :], 0.0), writes=self.d_qT)
        P.op("pool", lambda h: h.memset(self.qhi[:, :, :], 0.0), writes=self.d_qT)
        for g in range(2):
            if blk > 0:
                P.op("pool", lambda h, g=g: h.tensor_copy(out=self.kT[g][:, 0:128], in_=self.prevK[j][:, g, :]),
                     reads=[dprev], writes=[self.d_kT[g]])
        self.psum_i = 0
        ta, tb = self.t1[0], self.t1[1]
        da, db = self.d_t1[0], self.d_t1[1]
        chunks = []
        for g in range(2):
            def mk(g=g):
                kc = w[:, 1024 + g * 64:1024 + (g + 1) * 64]
                kk = np.concatenate([kc, kc], axis=1)
                return kk.reshape(8, 128, 128).transpose(1, 0, 2).reshape(128, 1024)
            chunks.append((mk, ci["bkg", j], self.kT[g][:, 128:640], self.d_kT[g]))
        for c in range(8):
            chunks.append((self.colchunk(w, c * 128) if self.host else None, ci["bqg", j],
                           (self.qlo[:, c, :], self.qhi[:, c, :]), self.d_qT[c]))
        pst = {}

        def nrA1(s):
            mk, gcol, out_ap, d_out = chunks[s]
            bk, bd = self.feat_proj(mk if self.host else None, bank=s % 3)
            i2 = s % 2
            P.op("act", lambda h: h.activation(out=self.sqq[i2][:, :], in_=bk[:, :], func=AF.Square),
                 reads=[bd], writes=[self.d_sqq[i2]])
            pst[s] = (bk, bd)

        def nrA2(s):
            mk, gcol, out_ap, d_out = chunks[s]
            bk, bd = pst[s]
            i2 = s % 2
            bs, ds = self.bx(3 + s % 2)
            P.op("pe", lambda h: h.matmul(bs[:, :], bones, self.sqq[i2][:, :], start=True, stop=True),
                 reads=[self.d_sqq[i2], self.d_C], writes=[ds])
            P.op("act", lambda h: h.activation(out=self.tmpA[:, :], in_=bs[:, :], func=AF.Ln, bias=self.epsc, scale=1.0 / 64),
                 reads=[ds], writes=[self.d_tmpA])
            P.op("act", lambda h: h.activation(out=self.rstd[:, :], in_=self.tmpA[:, :], func=AF.Exp, scale=-0.5),
                 reads=[self.d_tmpA], writes=[self.d_rstd])
            P.op("dve", lambda h: h.scalar_tensor_tensor(out=self.qn[i2][:, :], in0=bk[:, :], scalar=self.C[:, gcol:gcol + 1],
                                                         in1=self.rstd[:, :], op0=ALU.mult, op1=ALU.mult),
                 reads=[bd, self.d_rstd, self.d_C], writes=[self.d_qn[i2]])

        def nrB(s):
            mk, gcol, out_ap, d_out = chunks[s]
            i2 = s % 2
            br, dr = self.bx(5 + s % 2)
            P.op("pe", lambda h: h.matmul(br[:, :], RT, self.qn[i2][:, :], start=True, stop=True),
                 reads=[self.d_qn[i2], self.d_C], writes=[dr])
            P.op("dve", lambda h: h.tensor_tensor(out=ta[:, :], in0=self.qn[i2][:, :], in1=self.cosF[:, :], op=ALU.mult),
                 reads=[self.d_qn[i2], self.d_rope], writes=[da])
            P.op("dve", lambda h: h.tensor_tensor(out=tb[:, :], in0=br[:, :], in1=self.sinF[:, :], op=ALU.mult),
                 reads=[dr, self.d_rope], writes=[db])
            if isinstance(out_ap, tuple):
                lo_ap, hi_ap = out_ap
                P.op("dve", lambda h: h.tensor_tensor(out=lo_ap[0:64, :], in0=ta[0:64, :], in1=tb[0:64, :], op=ALU.add),
                     reads=[da, db], writes=[d_out])
                P.op("dve", lambda h: h.tensor_tensor(out=hi_ap[64:128, :], in0=ta[64:128, :], in1=tb[64:128, :], op=ALU.add),
                     reads=[da, db], writes=[d_out])
            else:
                P.op("dve", lambda h: h.tensor_tensor(out=out_ap, in0=ta[:, :], in1=tb[:, :], op=ALU.add),
                     reads=[da, db], writes=[d_out])

        nch = len(chunks)
        for s in range(nch + 2):
            if s < nch:
                nrA1(s)
            if 0 <= s - 1 < nch:
                nrA2(s - 1)
            if 0 <= s - 2 < nch:
                nrB(s - 2)
        self.psum_i = 0
        self.mem_attn(l, w, 1280)
        its = [(c, qb) for c in range(8) for qb in range(4)]
        ist = {}

        def att_S(i):
            c, qb = its[i]
            g = c // 4
            has_prev = not (blk == 0 and qb == 0)
            kbs = (0, 1) if has_prev else (1,)
            bs, ds = self.bx(4 + i % 4)
            for kb in kbs:
                kc0 = (qb + kb) * 128
                for e in range(2):
                    qsrc = self.qlo if e == 0 else self.qhi
                    col = (kb * 2 + e) * 128
                    P.op("pe", lambda h, qsrc=qsrc, kc0=kc0, col=col: h.matmul(
                        bs[:, col:col + 128], self.kT[g][:, kc0:kc0 + 128], qsrc[:, c, qb * 128:(qb + 1) * 128],
                        start=True, stop=True), reads=[self.d_kT[g], self.d_qT[c]], writes=[ds])
            ist[i] = (bs, ds, kbs, has_prev)

        def att_E(i):
            bs, ds, kbs, has_prev = ist[i]
            lo_c = 0 if has_prev else 256
            E = self.Es[i % 3]
            dE = self.d_Es[i % 3]
            P.op("act", lambda h: h.activation(out=E[:, lo_c:512], in_=bs[:, lo_c:512], func=AF.Exp, scale=0.125),
                 reads=[ds], writes=[dE])
            P.op("dve", lambda h: h.tensor_tensor(out=E[:, lo_c:512], in0=E[:, lo_c:512], in1=mask4[:, lo_c:512], op=ALU.mult),
                 reads=[dE, self.d_C], writes=[dE])

        def att_O(i):
            c, qb = its[i]
            g = c // 4
            bs, ds, kbs, has_prev = ist[i]
            E = self.Es[i % 3]
            dE = self.d_Es[i % 3]
            bo, do = self.bx(0 + 2 * (c % 2))
            bdn, ddn = self.bx(1 + 2 * (c % 2))
            terms = [(kb, e) for kb in kbs for e in range(2)]
            nt = len(terms)
            for ti, (kb, e) in enumerate(terms):
                col = (kb * 2 + e) * 128
                P.op("pe", lambda h, kb=kb, e=e, col=col, ti=ti: h.matmul(
                    bo[:, qb * 128:(qb + 1) * 128], self.vt[:, qb + kb, 2 * g + e, :], E[:, col:col + 128],
                    start=(ti == 0), stop=(ti == nt - 1)), reads=[self.d_vt[qb + kb], dE], writes=[do])
            for ti, (kb, e) in enumerate(terms):
                col = (kb * 2 + e) * 128
                P.op("pe", lambda h, e=e, col=col, ti=ti: h.matmul(
                    bdn[:, qb * 128:(qb + 1) * 128], lohi[e], E[:, col:col + 128],
                    start=(ti == 0), stop=(ti == nt - 1)), reads=[self.d_C, dE], writes=[ddn])
            if qb == 3:
                t2_, dt2 = self.t1[2], self.d_t1[2]
                sc = 8 * j + c
                P.op("act", lambda h: h.activation(out=t2_[:, :], in_=bdn[:, :], func=AF.Ln,
                                                   bias=self.esink[:, sc:sc + 1], scale=1.0),
                     reads=[ddn, self.d_misc], writes=[dt2])
                P.op("act", lambda h: h.activation(out=t2_[:, :], in_=t2_[:, :], func=AF.Exp, scale=-1.0), reads=[dt2], writes=[dt2])
                P.op("dve", lambda h: h.tensor_tensor(out=self.yT[:, c, :], in0=bo[:, :], in1=t2_[:, :], op=ALU.mult),
                     reads=[do, dt2], writes=[self.d_y[c]])

        att_S(0)
        for i in range(len(its)):
            if i + 1 < len(its):
                att_S(i + 1)
            att_E(i)
            att_O(i)
        self.psum_i = 0
        for g in range(2):
            P.op("pool", lambda h, g=g: h.tensor_copy(out=self.prevK[j][:, g, :], in_=self.kT[g][:, 512:640]),
                 reads=[self.d_kT[g]], writes=[dprev])
        P.op("pool", lambda h: h.tensor_copy(out=self.prevV[j][:, :, :], in_=self.vt[:, 4, :, :]),
             reads=[self.d_vt[4]], writes=[dprev])
        for c_ in range(8):
            self.dump("yB%d" % c_, self.yT[:, c_, :], TT, [self.d_y[c_]])
        self.out_proj(w_out)

    def rmsnorm(self, gcol):
        P = self.P
        ones = self.CB[:, self.cb["ones"]:self.cb["ones"] + 128]
        for c in range(8):
            if c < 5:
                P.op("act", lambda h, c=c: h.activation(out=self.sq[:, c, :], in_=self.xT[:, c, :], func=AF.Square),
                     reads=[self.d_xc[c]], writes=[self.d_sqc[c]])
            else:
                P.op("pool", lambda h, c=c: h.tensor_tensor(out=self.sq[:, c, :], in0=self.xT[:, c, :], in1=self.xT[:, c, :], op=ALU.mult),
                     reads=[self.d_xc[c]], writes=[self.d_sqc[c]])
        bk, bd = self.bank()
        for c in range(8):
            P.op("pe", lambda h, c=c, bk=bk: h.matmul(bk[:, :], ones, self.sq[:, c, :], start=(c == 0), stop=(c == 7)),
                 reads=[self.d_sqc[c], self.d_C], writes=[bd])
        P.op("act", lambda h, bk=bk: h.activation(out=self.tmpA[:, :], in_=bk[:, :], func=AF.Ln,
                                                   bias=self.epsc, scale=1.0 / D),
             reads=[bd, self.d_C], writes=[self.d_tmpA])
        P.op("act", lambda h: h.activation(out=self.rstd[:, :], in_=self.tmpA[:, :], func=AF.Exp, scale=-0.5),
             reads=[self.d_tmpA], writes=[self.d_rstd])
        for c in range(8):
            P.op("dve", lambda h, c=c: h.scalar_tensor_tensor(
                out=self.xn[:, c, :], in0=self.xT[:, c, :], scalar=self.C[:, gcol + c:gcol + c + 1],
                in1=self.rstd[:, :], op0=ALU.mult, op1=ALU.mult),
                reads=[self.d_xc[c], self.d_rstd, self.d_C], writes=[self.d_xn])

    def ffn(self, l, which):
        P = self.P
        inp = self.inp
        pre = "ffn1" if which == 0 else "ffn2"
        self.rmsnorm(self.cidx[("f1g" if which == 0 else "f2g"), l])
        w_in = inp[pre + "_w_in"][l] if self.host else None
        w_out = inp[pre + "_w_out"][l] if self.host else None
        for j in range(NFF):
            def mk(j=j):
                g = w_in[:, j * 128:(j + 1) * 128].reshape(8, 128, 128)
                u = w_in[:, DFF + j * 128:DFF + (j + 1) * 128].reshape(8, 128, 128)
                a = np.stack([g, u], axis=2)
                return a.transpose(1, 0, 2, 3).reshape(128, 2048)
            slot, wd = self.wload(mk, 2048)
            bg, dg = self.bank()
            bu, du = self.bank()
            for k in range(8):
                P.op("pe", lambda h, k=k, slot=slot, bg=bg: h.matmul(
                    bg[:, :], slot[:, k * 256:k * 256 + 128], self.xn[:, k, :], start=(k == 0), stop=(k == 7)),
                    reads=[wd, self.d_xn], writes=[dg])
            for k in range(8):
                P.op("pe", lambda h, k=k, slot=slot, bu=bu: h.matmul(
                    bu[:, :], slot[:, k * 256 + 128:k * 256 + 256], self.xn[:, k, :], start=(k == 0), stop=(k == 7)),
                    reads=[wd, self.d_xn], writes=[du])
            sg, dsg = self.sg[j % 2], self.d_sg[j % 2]
            P.op("act", lambda h, sg=sg, bg=bg: h.activation(out=sg[:, :], in_=bg[:, :], func=AF.Silu),
                 reads=[dg], writes=[dsg])
            P.op("dve", lambda h, sg=sg, bu=bu, j=j: h.tensor_tensor(
                out=self.act[:, j, :], in0=sg[:, :], in1=bu[:, :], op=ALU.mult),
                reads=[dsg, du], writes=[self.d_act[j]])
        HK = NFF // 2
        for c in range(8):
            halves = []
            for hf in range(2):
                def mk(c=c, hf=hf):
                    a = w_out[hf * HK * 128:(hf + 1) * HK * 128, c * 128:(c + 1) * 128].reshape(HK, 128, 128)
                    return a.transpose(1, 0, 2).reshape(128, HK * 128)
                halves.append(self.wload(mk, HK * 128))
            bk, bd = self.bank()
            for k in range(NFF):
                slot, wd = halves[k // HK]
                kk = k % HK
                P.op("pe", lambda h, k=k, kk=kk, slot=slot, bk=bk: h.matmul(
                    bk[:, :], slot[:, kk * 128:(kk + 1) * 128], self.act[:, k, :], start=(k == 0), stop=(k == NFF - 1)),
                    reads=[wd, self.d_act[k]], writes=[bd])
            P.op("dve", lambda h, c=c, bk=bk: h.scalar_tensor_tensor(
                out=self.xT[:, c, :], in0=bk[:, :], scalar=0.5, in1=self.xT[:, c, :], op0=ALU.mult, op1=ALU.add),
                reads=[bd, self.d_xc[c]], writes=[self.d_xc[c]])


def _prep_inputs(inputs):
    return {k: np.asarray(v) for k, v in inputs.items()}


def host_wmem(inp):
    w = np.asarray(inp["mem_w_kv"], np.float32)
    parts = [w[:, h * 128:(h + 1) * 128].reshape(8, 128, 128).transpose(1, 0, 2).reshape(128, 1024) for h in range(4)]
    for hf in range(2):
        parts.append(w[:, 512 + hf * 256:512 + (hf + 1) * 256].reshape(8, 128, 256).transpose(1, 0, 2).reshape(128, 2048))
    return np.ascontiguousarray(np.concatenate(parts, axis=1))


def core_inputs(b, inp, c):
    sl = slice(c * SEQ_PER_CORE, (c + 1) * SEQ_PER_CORE)
    return {"xT": np.ascontiguousarray(inp["x"][sl].transpose(0, 2, 1)),
            "memT": np.ascontiguousarray(inp["mem"][sl].transpose(0, 2, 1)),
            "pos": np.ascontiguousarray(np.broadcast_to(inp["positions"][sl].astype(np.float32)[:, None, :], (SEQ_PER_CORE, 128, SEQ))),
            "consts": b.h_consts, "cbf": b.cb_arr, "wst": b.h_wst, "wmem": b.h_wmem}


def build_all(inp, dbg=None, **kw):
    b = Builder(inp, **kw)
    for k_, v_ in (dbg or {}).items():
        setattr(b, k_, v_)
    nc = b.build()
    b.h_wst = np.ascontiguousarray(np.concatenate(b.wgroups, axis=1))
    b.h_consts = np.ascontiguousarray(np.concatenate(b.consts, axis=1))
    b.h_wmem = host_wmem(inp)
    return b, nc


def kernel(**inputs):
    inp = _prep_inputs(inputs)
    b, nc = build_all(inp)
    in_maps = [core_inputs(b, inp, c) for c in range(N_CORES)]
    res = run_bass_kernel_spmd(nc, in_maps, core_ids=list(range(N_CORES)))
    out = np.concatenate([r["oT"].transpose(0, 2, 1) for r in res.results], axis=0)
    return np.ascontiguousarray(out.astype(np.float32))
```

```python
import math
import os
from contextlib import ExitStack

import numpy as np
import concourse.bass as bass
import concourse.mybir as mybir
from concourse.bass_utils import run_bass_kernel_spmd

F32 = mybir.dt.float32
BF16 = mybir.dt.bfloat16
I32 = mybir.dt.int32
AF = mybir.ActivationFunctionType
ALU = mybir.AluOpType

D = 1024
SEQ = 2048
DEPTH = 4
DFF = 2816
NFF = DFF // 128
EPS = 1e-6
TT = 512
NMEM = 256
A_TOK = 3080
N_CORES = 8
SEQ_PER_CORE = 2


class Dep:
    __slots__ = ("w", "r", "name")

    def __init__(self, name=""):
        self.w = None
        self.r = []
        self.name = name


class Op:
    __slots__ = ("eng", "fn", "waits", "signal", "idx", "dma", "semval", "chan")

    def __init__(self, eng, fn):
        self.eng = eng
        self.fn = fn
        self.waits = []
        self.signal = False
        self.idx = 0
        self.dma = False
        self.semval = 0
        self.chan = None


ENGS = ("pe", "act", "dve", "pool", "sp")


class Prog:
    def __init__(self, nc):
        self.nc = nc
        self.streams = {e: [] for e in ENGS}
        self.waited = {e: {} for e in ENGS}
        self.chan_cnt = {}
        self.chan_last = {}

    def _add_wait(self, op, src):
        if src is None:
            return
        if src.dma:
            key = ("c", src.chan)
            val = src.semval
        else:
            if src.eng == op.eng and op.eng == "pe":
                return
            key = ("e", src.eng)
            val = src.idx
        w = self.waited[op.eng]
        if w.get(key, -1) >= val:
            return
        w[key] = val
        src.signal = True
        op.waits.append(src)

    def op(self, eng, fn, reads=(), writes=(), chan=None):
        o = Op(eng, fn)
        st = self.streams[eng]
        o.idx = len(st)
        if chan is not None:
            o.dma = True
            o.chan = chan
            n = self.chan_cnt.get(chan, 0) + 1
            self.chan_cnt[chan] = n
            o.semval = 16 * n
            o.signal = True
            prev = self.chan_last.get(chan)
            if prev is not None:
                self._add_wait(o, prev)
            self.chan_last[chan] = o
        for d in reads:
            self._add_wait(o, d.w)
        for d in writes:
            self._add_wait(o, d.w)
            for r in d.r:
                self._add_wait(o, r)
        for d in reads:
            d.r.append(o)
        for d in writes:
            d.w = o
            d.r = []
        st.append(o)
        return o

    def barrier(self, engs=("pe", "act", "dve", "pool", "sp")):
        last = {}
        for e in engs:
            last[e] = None
            for o_ in reversed(self.streams[e]):
                if o_.fn is not None:
                    last[e] = o_
                    break
        for e in engs:
            o = Op(e, None)
            o.idx = len(self.streams[e])
            for f in engs:
                if f != e and last[f] is not None:
                    self._add_wait(o, last[f])
            if o.waits:
                self.streams[e].append(o)

    def emit(self, final_waits):
        nc = self.nc
        with ExitStack() as es:
            esem = {e: es.enter_context(nc.semaphore("s_" + e)) for e in ENGS}
            csem = {c: es.enter_context(nc.semaphore("c_%s" % (c,))) for c in self.chan_cnt}
            for e in ENGS:
                n = 0
                for o in self.streams[e]:
                    if o.dma:
                        continue
                    if o.signal:
                        n += 1
                        o.semval = n
            block = es.enter_context(nc.Block())
            handles = {"pe": block.tensor, "act": block.scalar, "dve": block.vector,
                       "pool": block.gpsimd, "sp": block.sync}

            def run(e):
                def body(h):
                    for o in self.streams[e]:
                        for s in o.waits:
                            if s.dma:
                                h.wait_ge(csem[s.chan], s.semval)
                            else:
                                h.wait_ge(esem[s.eng], s.semval)
                        if o.fn is None:
                            continue
                        ins = o.fn(h)
                        if o.dma:
                            ins.then_inc(csem[o.chan], 16)
                        elif o.signal:
                            ins.then_inc(esem[e], 1)
                    if e == "sp":
                        for s in final_waits:
                            h.wait_ge(csem[s.chan], s.semval)
                return body

            for e in ENGS:
                handles[e](run(e))


class Builder:
    def __init__(self, inputs, n_super=8, n_layers=DEPTH, host=True):
        self.inp = inputs
        self.n_super = n_super
        self.n_layers = n_layers
        self.host = host
        self.nc = bass.Bass("TRN2", target_bir_lowering=False)
        self.P = Prog(self.nc)
        self.sb_off = 16640
        self.consts = []
        self.const_off = 0
        self.wgroups = []
        self.w_off = 0
        self.w_offsets = []
        self.wi = 0
        self.first_super = True
        self.psum_i = 0
        self.epsc = EPS

    def sb(self, name, shape, dt):
        size = int(np.prod(shape[1:])) * (4 if dt in (F32, I32) else 2)
        size = (size + 31) // 32 * 32
        t = self.nc.alloc_sbuf_tensor_at(name, list(shape), dt, offset=self.sb_off)
        self.sb_off += size
        return t

    def const(self, arr):
        arr = np.ascontiguousarray(arr, dtype=np.float32)
        assert arr.shape[0] == 128
        off = self.const_off
        self.consts.append(arr)
        self.const_off += arr.shape[1]
        return off

    def dump(self, name, ap, ncols, reads):
        if not getattr(self, "dbg_dump", False):
            return
        if not hasattr(self, "dumps"):
            self.dumps = []
            self.dump_off = 0
        off = self.dump_off
        self.dumps.append((name, off, ncols))
        self.dump_off += ncols
        self.P.op("pool", lambda h, ap=ap, off=off, ncols=ncols: h.dma_start(out=self.ddram[:, off:off + ncols], in_=ap),
                  reads=reads, chan="dbg")

    def bx(self, i):
        return self.psum[i % 8], self.psum_dep[i % 8]

    def bank(self):
        b = self.psum[self.psum_i % 8]
        d = self.psum_dep[self.psum_i % 8]
        self.psum_i += 1
        return b, d

    def wload(self, make_arr, n):
        if self.first_super:
            if self.host:
                a = np.ascontiguousarray(make_arr(), dtype=np.float32)
                assert a.shape == (128, n), (a.shape, n)
                self.wgroups.append(a)
            self.w_offsets.append((self.w_off, n))
            self.w_off += n
        off, n0 = self.w_offsets[self.wi]
        assert n0 == n
        self.wi += 1
        k = self.wslot_i % len(self.wslots)
        self.wslot_i += 1
        slot, dep = self.wslots[k], self.wslot_deps[k]
        assert n <= slot.shape[1]
        dst = slot[:, 0:n]
        self.P.op("pool", lambda h, dst=dst, off=off, n=n: h.dma_start(out=dst, in_=self.wdram[:, off:off + n]),
                  writes=[dep], chan=("w", k))
        return slot, dep

    def build(self):
        nc, P = self.nc, self.P
        self.xdram = nc.dram_tensor("xT", [SEQ_PER_CORE, D, SEQ], F32, kind="ExternalInput").ap()
        self.mdram = nc.dram_tensor("memT", [SEQ_PER_CORE, D, NMEM], F32, kind="ExternalInput").ap()
        self.pdram = nc.dram_tensor("pos", [SEQ_PER_CORE, 128, SEQ], F32, kind="ExternalInput").ap()
        self.odram = nc.dram_tensor("oT", [SEQ_PER_CORE, D, SEQ], F32, kind="ExternalOutput").ap()
        self._register_consts()
        self.cdram = nc.dram_tensor("consts", [128, self.const_off], F32, kind="ExternalInput").ap()
        self.cbdram = nc.dram_tensor("cbf", [128, self.cb_n], F32, kind="ExternalInput").ap()
        self.wmdram = nc.dram_tensor("wmem", [128, 8192], F32, kind="ExternalInput").ap()

        sb = self.sb
        self.C = sb("C", [128, self.const_off], F32)
        self.CB = sb("CB", [128, self.cb_n], BF16)
        self.xT = sb("xT", [128, 8, TT], F32)
        self.xn = sb("xn", [128, 8, TT], BF16)
        self.sq = sb("sq", [128, 8, TT], BF16)
        self.rstd = sb("rstd", [128, TT], F32)
        self.tmpA = sb("tmpA", [128, TT], F32)
        self.wslots = [sb("ws%d" % i, [128, 2048], BF16) for i in range(6)]
        self.wslot_deps = [Dep("ws%d" % i) for i in range(6)]
        self.wslot_i = 0
        self.negb = sb("negb", [128, 16], F32)
        self.esink = sb("esink", [128, 16], F32)
        self.cosF = sb("cosF", [128, TT], F32)
        self.sinF = sb("sinF", [128, TT], F32)
        self.zeros = sb("zeros", [128, TT], F32)
        self.Dst = [[sb("D%d_%d" % (j, h), [128, 257], F32) for h in range(4)] for j in range(2)]
        self.Cb = [[sb("Cb%d_%d" % (j, h), [128, 257], BF16) for h in range(4)] for j in range(2)]
        self.nrep = [[sb("nr%d_%d" % (j, h), [128, 128], BF16) for h in range(4)] for j in range(2)]
        self.car = [sb("car%d" % j, [128, 12], F32) for j in range(2)]
        self.prevK = [sb("pK%d" % j, [128, 2, 128], BF16) for j in range(2)]
        self.prevV = [sb("pV%d" % j, [128, 4, 128], BF16) for j in range(2)]
        self.memKn = sb("memKn", [128, 4, 4, NMEM], BF16)
        self.memV = sb("memV", [128, 2, 512], BF16)
        self.region_base = self.sb_off
        self.psum = [nc.alloc_psum_tensor("ps%d" % i, [128, 512], F32) for i in range(8)]
        self.psum_dep = [Dep("ps%d" % i) for i in range(8)]

        self.d_C = Dep("C")
        self.d_xc = [Dep("xT%d" % c) for c in range(8)]
        self.d_xnc = [Dep("xn%d" % c) for c in range(8)]
        self.d_sq = Dep("sq")
        self.d_sqc = [Dep("sq%d" % c) for c in range(8)]
        self.d_rstd = Dep("rstd")
        self.d_tmpA = Dep("tmpA")
        self.d_misc = Dep("misc")
        self.d_rope = Dep("rope")
        self.d_st = [[Dep("st%d_%d" % (j, h)) for h in range(4)] for j in range(2)]
        self.d_car = [Dep("car0"), Dep("car1")]
        self.d_prev = [Dep("prev0"), Dep("prev1")]
        self.d_mem = Dep("mem")

        self._alloc_ffn()
        self._alloc_prologue()
        self._alloc_mixA()
        self._alloc_mixB()
        assert self.region_max <= 229344, self.region_max

        P.op("sp", lambda h: h.dma_start(out=self.C[:, :], in_=self.cdram[:, :]), writes=[self.d_C], chan="c0")
        P.op("pool", lambda h: h.dma_start(out=self.CB[:, :], in_=self.cbdram[:, :]), writes=[self.d_C], chan="c1")
        ci = self.cidx
        P.op("dve", lambda h: h.tensor_scalar(out=self.negb[:, :], in0=self.C[:, ci["gb"]:ci["gb"] + 16],
                                              scalar1=-1.0, scalar2=None, op0=ALU.mult),
             reads=[self.d_C], writes=[self.d_misc])
        P.op("act", lambda h: h.activation(out=self.esink[:, :], in_=self.C[:, ci["sink"]:ci["sink"] + 16], func=AF.Exp),
             reads=[self.d_C], writes=[self.d_misc])
        P.op("dve", lambda h: h.memset(self.zeros[:, :], 0.0), writes=[self.d_misc])

        out_ops = []
        for s in range(self.n_super):
            seq, blk = divmod(s, SEQ // TT)
            self.wi = 0
            t0 = blk * TT
            if blk == 0:
                self.seq_start(seq)
            P.op("sp", lambda h, seq=seq, t0=t0: h.dma_start(
                out=self.xT[:, :, :],
                in_=self.xdram[seq, :, t0:t0 + TT].rearrange("(c p) t -> p c t", p=128)),
                writes=self.d_xc, chan="x")
            if self.n_layers > 1 or os.environ.get("DBG_FORCEROPE"):
                self.rope_tables(seq, t0)
            for l in range(self.n_layers):
                self.ffn(l, 0)
                P.barrier()
                if l % 2 == 0:
                    self.mixA(l, blk)
                else:
                    self.mixB(l, blk)
                P.barrier()
                if getattr(self, "dbg_skip_ffn2", False):
                    continue
                self.ffn(l, 1)
            o = P.op("sp", lambda h, seq=seq, t0=t0: h.dma_start(
                out=self.odram[seq, :, t0:t0 + TT].rearrange("(c p) t -> p c t", p=128),
                in_=self.xT[:, :, :]), reads=self.d_xc, chan="o")
            out_ops.append(o)
            self.first_super = False
        self.w_total = self.w_off
        if getattr(self, "dbg_dump", False):
            self.ddram = nc.dram_tensor("dbg", [128, max(self.dump_off, 1)], F32, kind="ExternalOutput").ap()
        self.wdram = nc.dram_tensor("wst", [128, self.w_total], F32, kind="ExternalInput").ap()
        P.emit(final_waits=[out_ops[-1]])
        return nc

    def _register_consts(self):
        inp = self.inp
        c = {}

        def pcols(v):
            v = np.asarray(v, dtype=np.float32)
            return v.reshape(-1, 128).T

        def bc(v):
            v = np.asarray(v, dtype=np.float32).reshape(1, -1)
            return np.repeat(v, 128, axis=0)

        for l in range(DEPTH):
            c["f1g", l] = self.const(pcols(inp["ffn1_norm_g"][l]))
            c["mg", l] = self.const(pcols(inp["mix_norm_g"][l]))
            c["f2g", l] = self.const(pcols(inp["ffn2_norm_g"][l]))
            c["xaq", l] = self.const(pcols(inp["xa_q_norm_g"][l]))
            c["xak", l] = self.const(pcols(inp["xa_k_norm_g"][l]))
        c["memg"] = self.const(pcols(inp["mem_norm_g"]))
        c["gb"] = self.const(np.concatenate([bc(inp["a_gate_b"][0]), bc(inp["a_gate_b"][1])], axis=1))
        for j in range(2):
            c["ghn", j] = self.const(pcols(inp["a_h_norm_g"][j]))
            c["bqg", j] = self.const(np.tile(np.asarray(inp["b_q_norm_g"][j], np.float32), 2).reshape(128, 1))
            c["bkg", j] = self.const(np.tile(np.asarray(inp["b_k_norm_g"][j], np.float32), 2).reshape(128, 1))
        sk = []
        for j in range(2):
            s_ = np.asarray(inp["b_sinks"][j], np.float32)
            a = np.zeros((128, 8), np.float32)
            for cc in range(8):
                a[:64, cc] = s_[2 * cc]
                a[64:, cc] = s_[2 * cc + 1]
            sk.append(a)
        c["sink"] = self.const(np.concatenate(sk, axis=1))
        invf = (500000.0 ** (-np.arange(0, 16, 2, dtype=np.float32) / np.float32(16))).astype(np.float32)
        iv = np.zeros((128, 1), np.float32)
        for p in range(128):
            d_ = p % 64
            if d_ < 16:
                iv[p, 0] = invf[d_ % 8]
        c["invf"] = self.const(iv)
        self.cidx = c
        cb = {}
        n = 0
        cbl = []

        def addb(name, arr):
            nonlocal n
            cb[name] = n
            cbl.append(np.ascontiguousarray(arr, dtype=np.float32))
            n += arr.shape[1]

        addb("ones", np.ones((128, 128), np.float32))
        addb("ident", np.eye(128, dtype=np.float32))
        s_i = np.arange(128)[:, None]
        j_i = np.arange(128)[None, :]
        cur = (s_i <= j_i).astype(np.float32)
        prev = (s_i > j_i).astype(np.float32)
        addb("maskcur", cur)
        addb("mask4", np.concatenate([prev, prev, cur, cur], axis=1))
        bo = np.zeros((128, 128), np.float32)
        bo[:64, :64] = 1
        bo[64:, 64:] = 1
        addb("blockones", bo)
        rt = np.zeros((128, 128), np.float32)
        for m in range(128):
            d_ = m % 64
            if d_ < 8:
                rt[m + 8, m] = -1.0
            elif d_ < 16:
                rt[m - 8, m] = 1.0
        addb("RT", rt)
        lo = np.zeros((128, 128), np.float32)
        lo[:, :64] = 1
        hi = np.zeros((128, 128), np.float32)
        hi[:, 64:] = 1
        addb("onelo", lo)
        addb("onehi", hi)
        self.cb = cb
        self.cb_n = n
        self.cb_arr = np.concatenate(cbl, axis=1)

    def cbv(self, name, n=128):
        o = self.cb[name]
        return self.CB[:, o:o + n]

    def _region(self):
        self.sb_off = self.region_base

    def _region_end(self):
        self.region_max = max(getattr(self, "region_max", 0), self.sb_off)

    def _alloc_ffn(self):
        self._region()
        self.act = self.sb("act", [128, NFF, TT], BF16)
        self.sg = [self.sb("sg%d" % i, [128, TT], F32) for i in range(2)]
        self.d_act = [Dep("act%d" % j) for j in range(NFF)]
        self.d_sg = [Dep("sg0"), Dep("sg1")]
        self._region_end()

    def _alloc_common_mix(self):
        sb = self.sb
        self.yT = sb("yT", [128, 12, TT], BF16)
        self.d_y = [Dep("y%d" % i) for i in range(12)]
        self.qn = [sb("qn%d" % i, [128, TT], BF16) for i in range(2)]
        self.d_qn = [Dep("qn0"), Dep("qn1")]
        self.Em = [sb("Em%d" % i, [128, 2, TT], BF16) for i in range(2)]
        self.d_Em = [Dep("Em0"), Dep("Em1")]
        self.rden = sb("rden", [128, TT], F32)
        self.d_rden = Dep("rden")
        self.sqq = [sb("sqq%d" % i, [128, TT], BF16) for i in range(2)]
        self.d_sqq = [Dep("sqq0"), Dep("sqq1")]
        self.t1 = [sb("t1_%d" % i, [128, TT], F32) for i in range(3)]
        self.d_t1 = [Dep("t1_%d" % i) for i in range(3)]

    def _alloc_prologue(self):
        self._region()
        sb = self.sb
        self.mT = sb("mT", [128, 8, NMEM], F32)
        self.memn = sb("memn", [128, 8, NMEM], BF16)
        self.memK = sb("memK", [128, 4, NMEM], F32)
        self.rstdk = sb("rstdk", [128, 4, NMEM], F32)
        self.psq = sb("psq", [128, NMEM], BF16)
        self.d_psq = Dep("psq")
        self._region_end()

    def _alloc_mixA(self):
        self._region()
        sb = self.sb
        self._alloc_common_mix()
        self.mix_common_end = self.sb_off
        self.gt1 = sb("gt1", [128, TT], F32)
        self.negi = sb("negi", [128, TT], F32)
        self.nega = sb("nega", [128, TT], F32)
        self.Bext = sb("Bext", [128, TT + 1], F32)
        self.Aext = sb("Aext", [128, TT + 1], F32)
        self.tmpB = sb("tmpB", [128, TT], F32)
        self.uB = [sb("uB%d" % h, [128, TT], BF16) for h in range(4)]
        self.gB = [sb("gB%d" % h, [128, TT], BF16) for h in range(4)]
        self.eBA = [sb("eBA%d" % h, [128, TT], BF16) for h in range(4)]
        self.posM = [sb("posM%d" % h, [128, 4], F32) for h in range(4)]
        self.gbuf = [sb("gbuf%d" % h, [128, 8], F32) for h in range(4)]
        self.qp = [sb("qp%d" % h, [128, TT], BF16) for h in range(4)]
        self.kp = [sb("kp%d" % h, [128, TT], BF16) for h in range(4)]
        self.vtok = sb("vtok", [128, 4, 4, 257], BF16)
        self.SmT = [sb("SmT%d" % i, [128, 128], BF16) for i in range(4)]
        self.kptok = [sb("kptok%d" % i, [128, 128], BF16) for i in range(4)]
        self.numS = [sb("numS%d" % h, [128, 3, TT], F32) for h in range(4)]
        self.ogT = [sb("ogT%d" % i, [128, 2, TT], BF16) for i in range(2)]
        self.sqh = [sb("sqh%d" % i, [128, 2, TT], BF16) for i in range(2)]
        self.d_g = Dep("gatebufs")
        self.d_uB = [Dep("uB%d" % h) for h in range(4)]
        self.d_qp = [Dep("qp%d" % h) for h in range(4)]
        self.d_kp = [Dep("kp%d" % h) for h in range(4)]
        self.d_vtok = [Dep("vtok%d" % i) for i in range(4)]
        self.d_SmT = [Dep("SmT%d" % i) for i in range(4)]
        self.d_kptok = [Dep("kptok%d" % i) for i in range(4)]
        self.d_numS = [Dep("numS%d" % h) for h in range(4)]
        self.d_ogT = [Dep("ogT0"), Dep("ogT1")]
        self.d_sqh = [Dep("sqh0"), Dep("sqh1")]
        self.d_gh = [Dep("gh%d" % h) for h in range(4)]
        self._region_end()
        self.mixA_end = self.sb_off

    def _alloc_mixB(self):
        self.sb_off = self.mix_common_end
        sb = self.sb
        self.kT = [sb("kT%d" % g, [128, 640], BF16) for g in range(2)]
        self.d_kT = [Dep("kT0"), Dep("kT1")]
        self.vt = sb("vt", [128, 5, 4, 128], BF16)
        self.d_vt = [Dep("vt%d" % i) for i in range(5)]
        self.qlo = sb("qlo", [128, 8, TT], BF16)
        self.qhi = sb("qhi", [128, 8, TT], BF16)
        self.d_qT = [Dep("qT%d" % i) for i in range(8)]
        self.Es = [sb("Es%d" % i, [128, TT], BF16) for i in range(3)]
        self.d_Es = [Dep("Es%d" % i) for i in range(3)]
        self._region_end()
        self._region()
        self.posi = sb("posi", [128, TT], I32)
        self.ki = sb("ki", [128, TT], I32)
        self.rf = [sb("rf%d" % i, [128, TT], F32) for i in range(4)]
        self.d_rs = Dep("ropescratch")
        self._region_end()

    def wload_mem(self, off, n):
        k = self.wslot_i % len(self.wslots)
        self.wslot_i += 1
        slot, dep = self.wslots[k], self.wslot_deps[k]
        self.P.op("pool", lambda h, slot=slot, off=off, n=n: h.dma_start(out=slot[:, 0:n], in_=self.wmdram[:, off:off + n]),
                  writes=[dep], chan=("w", k))
        return slot, dep

    def seq_start(self, seq):
        P = self.P
        ci = self.cidx
        P.barrier()
        for j in range(2):
            for h in range(4):
                P.op("pool", lambda hh, j=j, h=h: hh.memset(self.Dst[j][h][:, :], 0.0), writes=[self.d_st[j][h]])
                P.op("pool", lambda hh, j=j, h=h: hh.memset(self.Cb[j][h][:, :], 0.0), writes=[self.d_st[j][h]])
                P.op("pool", lambda hh, j=j, h=h: hh.memset(self.nrep[j][h][:, :], 0.0), writes=[self.d_st[j][h]])
            P.op("pool", lambda hh, j=j: hh.memset(self.car[j][:, 0:8], 0.0), writes=[self.d_car[j]])
            P.op("pool", lambda hh, j=j: hh.memset(self.car[j][:, 8:12], 1.0), writes=[self.d_car[j]])
        d_m = self.d_mem
        d_mt = Dep("mT")
        P.op("sp", lambda h: h.dma_start(out=self.mT[:, :, :],
                                         in_=self.mdram[seq, :, :].rearrange("(c p) t -> p c t", p=128)),
             writes=[d_mt], chan="m")
        ones = self.cbv("ones")
        sqv = self.sq[:, :, 0:NMEM]
        for c in range(8):
            P.op("act", lambda h, c=c: h.activation(out=self.sq[:, c, 0:NMEM], in_=self.mT[:, c, :], func=AF.Square),
                 reads=[d_mt], writes=[self.d_sq])
        bk, bd = self.bank()
        for c in range(8):
            P.op("pe", lambda h, c=c, bk=bk: h.matmul(bk[:, 0:NMEM], ones, self.sq[:, c, 0:NMEM], start=(c == 0), stop=(c == 7)),
                 reads=[self.d_sq, self.d_C], writes=[bd])
        P.op("act", lambda h, bk=bk: h.activation(out=self.tmpA[:, 0:NMEM], in_=bk[:, 0:NMEM], func=AF.Ln,
                                                   bias=self.epsc, scale=1.0 / D), reads=[bd, self.d_C], writes=[self.d_tmpA])
        P.op("act", lambda h: h.activation(out=self.rstd[:, 0:NMEM], in_=self.tmpA[:, 0:NMEM], func=AF.Exp, scale=-0.5),
             reads=[self.d_tmpA], writes=[self.d_rstd])
        d_memn = Dep("memn")
        for c in range(8):
            P.op("dve", lambda h, c=c: h.scalar_tensor_tensor(
                out=self.memn[:, c, :], in0=self.mT[:, c, :], scalar=self.C[:, ci["memg"] + c:ci["memg"] + c + 1],
                in1=self.rstd[:, 0:NMEM], op0=ALU.mult, op1=ALU.mult),
                reads=[d_mt, self.d_rstd, self.d_C], writes=[d_memn])
        d_memK = Dep("memK")
        for hd in range(4):
            slot, wd = self.wload_mem(hd * 1024, 1024)
            bk, bd = self.bank()
            for k in range(8):
                P.op("pe", lambda h, k=k, slot=slot, bk=bk: h.matmul(
                    bk[:, 0:NMEM], slot[:, k * 128:(k + 1) * 128], self.memn[:, k, :], start=(k == 0), stop=(k == 7)),
                    reads=[wd, d_memn], writes=[bd])
            P.op("act", lambda h, hd=hd, bk=bk: h.activation(out=self.memK[:, hd, :], in_=bk[:, 0:NMEM], func=AF.Copy),
                 reads=[bd], writes=[d_memK])
            P.op("act", lambda h, hd=hd, bk=bk: h.activation(out=self.psq[:, :], in_=bk[:, 0:NMEM], func=AF.Square),
                 reads=[bd], writes=[self.d_psq])
            b2, d2 = self.bank()
            P.op("pe", lambda h, b2=b2: h.matmul(b2[:, 0:NMEM], ones, self.psq[:, :], start=True, stop=True),
                 reads=[self.d_psq, self.d_C], writes=[d2])
            P.op("act", lambda h, b2=b2: h.activation(out=self.tmpA[:, 0:NMEM], in_=b2[:, 0:NMEM], func=AF.Ln,
                                                       bias=self.epsc, scale=1.0 / 128), reads=[d2], writes=[self.d_tmpA])
            P.op("act", lambda h, hd=hd: h.activation(out=self.rstdk[:, hd, :], in_=self.tmpA[:, 0:NMEM], func=AF.Exp, scale=-0.5),
                 reads=[self.d_tmpA], writes=[d_memK])
            for l in range(4):
                P.op("dve", lambda h, hd=hd, l=l: h.scalar_tensor_tensor(
                    out=self.memKn[:, l, hd, :], in0=self.memK[:, hd, :], scalar=self.C[:, ci["xak", l]:ci["xak", l] + 1],
                    in1=self.rstdk[:, hd, :], op0=ALU.mult, op1=ALU.mult),
                    reads=[d_memK, self.d_C], writes=[d_m])
        vs = [self.wload_mem(4096, 2048), self.wload_mem(6144, 2048)]
        for mc in range(2):
            bk, bd = self.bank()
            for hf in range(2):
                slot, wd = vs[hf]
                for k in range(8):
                    P.op("pe", lambda h, k=k, mc=mc, hf=hf, slot=slot, bk=bk: h.matmul(
                        bk[:, hf * 256:(hf + 1) * 256], self.memn[:, k, mc * 128:(mc + 1) * 128], slot[:, k * 256:(k + 1) * 256],
                        start=(k == 0), stop=(k == 7)), reads=[wd, d_memn], writes=[bd])
            P.op("act", lambda h, mc=mc, bk=bk: h.activation(out=self.memV[:, mc, :], in_=bk[:, :], func=AF.Copy),
                 reads=[bd], writes=[d_m])
        P.barrier()

    def rope_tables(self, seq, t0):
        P = self.P
        if os.environ.get("DBG_NOROPE"):
            P.op("dve", lambda h: h.memset(self.cosF[:, :], 1.0), writes=[self.d_rope])
            P.op("dve", lambda h: h.memset(self.sinF[:, :], 0.0), writes=[self.d_rope])
            return
        ci = self.cidx
        ds = self.d_rs
        P.barrier()
        if os.environ.get("DBG_ROPEBAR"):
            P.op("dve", lambda h: h.memset(self.cosF[:, :], 1.0), writes=[self.d_rope])
            P.op("dve", lambda h: h.memset(self.sinF[:, :], 0.0), writes=[self.d_rope])
            P.barrier()
            return
        TWO_PI = 2.0 * math.pi
        C1 = 6.28125
        C2 = TWO_PI - C1
        PIB = 3.141592
        posf, ang, y, r = self.rf
        P.op("sp", lambda h: h.dma_start(out=posf[:, :], in_=self.pdram[seq, :, t0:t0 + TT]),
             writes=[ds], chan="p")
        P.op("dve", lambda h: h.tensor_scalar(out=ang[:, :], in0=posf[:, :], scalar1=self.C[:, ci["invf"]:ci["invf"] + 1],
                                              scalar2=None, op0=ALU.mult), reads=[ds, self.d_C], writes=[ds])
        MAGIC = 12582912.0
        for which, dst in ((0, self.sinF), (1, self.cosF)):
            if which == 1:
                P.op("dve", lambda h: h.tensor_scalar(out=ang[:, :], in0=ang[:, :], scalar1=math.pi / 2, scalar2=None, op0=ALU.add),
                     reads=[ds], writes=[ds])
            P.op("dve", lambda h: h.tensor_scalar(out=y[:, :], in0=ang[:, :], scalar1=1.0 / TWO_PI, scalar2=MAGIC,
                                                  op0=ALU.mult, op1=ALU.add), reads=[ds], writes=[ds])
            P.op("dve", lambda h: h.tensor_scalar(out=y[:, :], in0=y[:, :], scalar1=-MAGIC, scalar2=None, op0=ALU.add),
                 reads=[ds], writes=[ds])
            P.op("dve", lambda h: h.scalar_tensor_tensor(out=r[:, :], in0=y[:, :], scalar=-C1, in1=ang[:, :],
                                                         op0=ALU.mult, op1=ALU.add), reads=[ds], writes=[ds])
            P.op("dve", lambda h: h.scalar_tensor_tensor(out=r[:, :], in0=y[:, :], scalar=-C2, in1=r[:, :],
                                                         op0=ALU.mult, op1=ALU.add), reads=[ds], writes=[ds])
            P.op("dve", lambda h: h.tensor_scalar(out=y[:, :], in0=r[:, :], scalar1=-math.pi, scalar2=TWO_PI,
                                                  op0=ALU.is_lt, op1=ALU.mult), reads=[ds], writes=[ds])
            P.op("dve", lambda h: h.tensor_tensor(out=r[:, :], in0=r[:, :], in1=y[:, :], op=ALU.add), reads=[ds], writes=[ds])
            P.op("dve", lambda h: h.tensor_scalar(out=y[:, :], in0=r[:, :], scalar1=math.pi, scalar2=-TWO_PI,
                                                  op0=ALU.is_gt, op1=ALU.mult), reads=[ds], writes=[ds])
            P.op("dve", lambda h: h.tensor_tensor(out=r[:, :], in0=r[:, :], in1=y[:, :], op=ALU.add), reads=[ds], writes=[ds])
            P.op("dve", lambda h: h.tensor_scalar(out=r[:, :], in0=r[:, :], scalar1=-PIB, scalar2=PIB,
                                                  op0=ALU.max, op1=ALU.min), reads=[ds], writes=[ds])
            s2 = posf if which == 1 else self.rstd
            dd = ds
            P.op("dve", lambda h, s2=s2: h.tensor_tensor(out=s2[:, :], in0=r[:, :], in1=r[:, :], op=ALU.mult),
                 reads=[ds, self.d_rstd], writes=[ds, self.d_rstd])
            P.op("dve", lambda h, s2=s2: h.tensor_scalar(out=y[:, :], in0=s2[:, :], scalar1=1.0 / 6227020800.0, scalar2=None, op0=ALU.mult),
                 reads=[ds, self.d_rstd], writes=[ds])
            for cf in (-1.0 / 39916800.0, 1.0 / 362880.0, -1.0 / 5040.0, 1.0 / 120.0, -1.0 / 6.0):
                P.op("dve", lambda h, s2=s2, cf=cf: h.scalar_tensor_tensor(out=y[:, :], in0=y[:, :], scalar=cf, in1=s2[:, :],
                                                                           op0=ALU.add, op1=ALU.mult),
                     reads=[ds, self.d_rstd], writes=[ds])
            P.op("dve", lambda h, dst=dst: h.scalar_tensor_tensor(out=dst[:, :], in0=y[:, :], scalar=1.0, in1=r[:, :],
                                                                  op0=ALU.add, op1=ALU.mult),
                 reads=[ds], writes=[self.d_rope])
        self.dump("cosF", self.cosF[:, :], TT, [self.d_rope])
        self.dump("sinF", self.sinF[:, :], TT, [self.d_rope])
        P.barrier()

    def feat_proj(self, mk, nk=8, rhs=None, d_rhs=None, ncols=TT, bank=None):
        P = self.P
        slot, wd = self.wload(mk, nk * 128)
        bk, bd = self.bank() if bank is None else self.bx(bank)
        for k in range(nk):
            P.op("pe", lambda h, k=k, slot=slot, bk=bk: h.matmul(
                bk[:, 0:ncols], slot[:, k * 128:(k + 1) * 128], self.xn[:, k, :], start=(k == 0), stop=(k == nk - 1)),
                reads=[wd, self.d_xnc[k]], writes=[bd])
        return bk, bd

    def colchunk(self, w, c0):
        return lambda: w[:, c0:c0 + 128].reshape(8, 128, 128).transpose(1, 0, 2).reshape(128, 1024)

    def mem_attn(self, l, w, xq0):
        P = self.P
        ci = self.cidx
        ones = self.cbv("ones")
        st = {}

        def stA(hd):
            bq, dq = self.feat_proj(self.colchunk(w, xq0 + hd * 128) if self.host else None, bank=hd % 2)
            i2 = hd % 2
            P.op("act", lambda h: h.activation(out=self.sqq[i2][:, :], in_=bq[:, :], func=AF.Square),
                 reads=[dq], writes=[self.d_sqq[i2]])
            bs, ds = self.bx(2)
            P.op("pe", lambda h: h.matmul(bs[:, :], ones, self.sqq[i2][:, :], start=True, stop=True),
                 reads=[self.d_sqq[i2], self.d_C], writes=[ds])
            P.op("act", lambda h: h.activation(out=self.tmpA[:, :], in_=bs[:, :], func=AF.Ln,
                                               bias=self.epsc, scale=1.0 / 128), reads=[ds], writes=[self.d_tmpA])
            P.op("act", lambda h: h.activation(out=self.rstd[:, :], in_=self.tmpA[:, :], func=AF.Exp, scale=-0.5),
                 reads=[self.d_tmpA], writes=[self.d_rstd])
            P.op("dve", lambda h: h.scalar_tensor_tensor(
                out=self.qn[i2][:, :], in0=bq[:, :], scalar=self.C[:, ci["xaq", l]:ci["xaq", l] + 1],
                in1=self.rstd[:, :], op0=ALU.mult, op1=ALU.mult),
                reads=[dq, self.d_rstd, self.d_C], writes=[self.d_qn[i2]])

        def stB(hd):
            i2 = hd % 2
            for mc in range(2):
                b_, d_ = self.bx(3 + mc)
                P.op("pe", lambda h, b_=b_, mc=mc: h.matmul(
                    b_[:, :], self.memKn[:, l, hd, mc * 128:(mc + 1) * 128], self.qn[i2][:, :], start=True, stop=True),
                    reads=[self.d_mem, self.d_qn[i2]], writes=[d_])
                P.op("act", lambda h, b_=b_, mc=mc: h.activation(
                    out=self.Em[i2][:, mc, :], in_=b_[:, :], func=AF.Exp, scale=128.0 ** -0.5),
                    reads=[d_], writes=[self.d_Em[i2]])

        def stC(hd):
            i2 = hd % 2
            bo, do = self.bx(5 + i2)
            for mc in range(2):
                P.op("pe", lambda h, mc=mc: h.matmul(
                    bo[:, :], self.memV[:, mc, hd * 128:(hd + 1) * 128], self.Em[i2][:, mc, :],
                    start=(mc == 0), stop=(mc == 1)), reads=[self.d_mem, self.d_Em[i2]], writes=[do])
            bd_, dd_ = self.bx(7)
            for mc in range(2):
                P.op("pe", lambda h, mc=mc: h.matmul(
                    bd_[:, :], ones, self.Em[i2][:, mc, :], start=(mc == 0), stop=(mc == 1)),
                    reads=[self.d_C, self.d_Em[i2]], writes=[dd_])
            P.op("act", lambda h: h.activation(out=self.rden[:, :], in_=bd_[:, :], func=AF.Ln),
                 reads=[dd_], writes=[self.d_rden])
            P.op("act", lambda h: h.activation(out=self.rden[:, :], in_=self.rden[:, :], func=AF.Exp, scale=-1.0),
                 reads=[self.d_rden], writes=[self.d_rden])
            P.op("dve", lambda h: h.tensor_tensor(
                out=self.yT[:, 8 + hd, :], in0=bo[:, :], in1=self.rden[:, :], op=ALU.mult),
                reads=[do, self.d_rden], writes=[self.d_y[8 + hd]])

        for s in range(6):
            if s < 4:
                stA(s)
            if 0 <= s - 1 < 4:
                stB(s - 1)
            if 0 <= s - 2 < 4:
                stC(s - 2)
        self.psum_i = 0

    def out_proj(self, w_out):
        P = self.P
        for c in range(8):
            def mk(c=c):
                a = w_out[:, c * 128:(c + 1) * 128].reshape(12, 128, 128)
                return a.transpose(1, 0, 2).reshape(128, 12 * 128)
            slot, wd = self.wload(mk, 12 * 128)
            bk, bd = self.bank()
            for k in range(12):
                P.op("pe", lambda h, k=k, slot=slot, bk=bk: h.matmul(
                    bk[:, :], slot[:, k * 128:(k + 1) * 128], self.yT[:, k, :], start=(k == 0), stop=(k == 11)),
                    reads=[wd, self.d_y[k]], writes=[bd])
            P.op("dve", lambda h, c=c, bk=bk: h.tensor_tensor(
                out=self.xT[:, c, :], in0=bk[:, :], in1=self.xT[:, c, :], op=ALU.add),
                reads=[bd, self.d_xc[c]], writes=[self.d_xc[c]])

    def mixA(self, l, blk):
        P = self.P
        ci = self.cidx
        j = l // 2
        w = self.inp["a_w_in"][j] if self.host else None
        w_out = self.inp["a_w_out"][j] if self.host else None
        self.rmsnorm(ci["mg", l])
        ones = self.cbv("ones")
        ident = self.cbv("ident")
        maskcur = self.cbv("maskcur")
        LNS = math.log(128.0 ** -0.5)
        car = self.car[j]
        d_car = self.d_car[j]
        dg = self.d_g
        for hd in range(4):
            def mkg(hd=hd):
                g = np.stack([w[:, 3072 + hd], w[:, 3076 + hd]], axis=1).reshape(8, 128, 2)
                g = g.transpose(1, 0, 2)[:, :, :, None]
                return np.broadcast_to(g, (128, 8, 2, 128)).reshape(128, 2048)
            slot, wd = self.wload(mkg, 2048)
            gb_ = []
            for g in range(2):
                bk, bd = self.bank()
                for k in range(8):
                    P.op("pe", lambda h, k=k, g=g, slot=slot, bk=bk: h.matmul(
                        bk[:, :], slot[:, (k * 2 + g) * 128:(k * 2 + g + 1) * 128], self.xn[:, k, :],
                        start=(k == 0), stop=(k == 7)), reads=[wd, self.d_xnc[k]], writes=[bd])
                gb_.append((bk, bd))
            (bi, dbi), (bf, dbf) = gb_
            nb = self.negb
            P.op("act", lambda h, bf=bf, hd=hd: h.activation(out=self.gt1[:, :], in_=bf[:, :], func=AF.Exp,
                                                               bias=nb[:, 8 * j + 4 + hd:8 * j + 5 + hd], scale=-1.0),
                 reads=[dbf, self.d_misc], writes=[dg])
            P.op("act", lambda h: h.activation(out=self.gt1[:, :], in_=self.gt1[:, :], func=AF.Ln, bias=1.0, scale=1.0),
                 reads=[dg], writes=[dg])
            P.op("dve", lambda h, hd=hd: h.tensor_copy(out=self.Bext[:, 0:1], in_=car[:, hd:hd + 1]),
                 reads=[d_car], writes=[dg])
            P.op("dve", lambda h: h.tensor_tensor_scan(out=self.Bext[:, 1:TT + 1], data0=self.gt1[:, :], data1=self.zeros[:, :],
                                                        initial=self.Bext[:, 0:1], op0=ALU.add, op1=ALU.add),
                 reads=[dg, self.d_misc], writes=[dg])
            P.op("act", lambda h, bi=bi, hd=hd: h.activation(out=self.negi[:, :], in_=bi[:, :], func=AF.Identity,
                                                               bias=nb[:, 8 * j + hd:8 * j + hd + 1], scale=-1.0),
                 reads=[dbi, self.d_misc], writes=[dg])
            P.op("dve", lambda h: h.tensor_tensor(out=self.nega[:, :], in0=self.negi[:, :], in1=self.Bext[:, 1:TT + 1],
                                                   op=ALU.subtract), reads=[dg], writes=[dg])
            P.op("dve", lambda h, hd=hd: h.tensor_copy(out=self.Aext[:, 0:1], in_=car[:, 4 + hd:5 + hd]),
                 reads=[d_car], writes=[dg])
            P.op("dve", lambda h: h.tensor_tensor_scan(out=self.Aext[:, 1:TT + 1], data0=self.nega[:, :], data1=self.nega[:, :],
                                                        initial=self.Aext[:, 0:1], op0=ALU.min, op1=ALU.min),
                 reads=[dg], writes=[dg])
            P.op("dve", lambda h, hd=hd: h.tensor_scalar(out=self.posM[hd][:, :], in0=self.Aext[:, 0:TT:128],
                                                          scalar1=-1.0, scalar2=LNS, op0=ALU.mult, op1=ALU.add),
                 reads=[dg], writes=[self.d_gh[hd]])
            P.op("dve", lambda h, hd=hd: h.tensor_copy(out=self.gbuf[hd][:, 0:1], in_=car[:, 8 + hd:9 + hd]),
                 reads=[d_car], writes=[self.d_gh[hd]])
            P.op("dve", lambda h, hd=hd: h.tensor_tensor(out=self.gbuf[hd][:, 1:5], in0=self.Aext[:, 128:TT + 1:128],
                                                          in1=self.Aext[:, 0:TT:128], op=ALU.subtract),
                 reads=[dg], writes=[self.d_gh[hd]])
            P.op("act", lambda h, hd=hd: h.activation(out=self.gbuf[hd][:, 1:5], in_=self.gbuf[hd][:, 1:5], func=AF.Exp),
                 reads=[self.d_gh[hd]], writes=[self.d_gh[hd]])
            P.op("dve", lambda h: h.tensor_tensor(out=self.tmpB[:, :], in0=self.Bext[:, 1:TT + 1], in1=self.Aext[:, 1:TT + 1],
                                                   op=ALU.add), reads=[dg], writes=[dg])
            P.op("act", lambda h, hd=hd: h.activation(out=self.eBA[hd][:, :], in_=self.tmpB[:, :], func=AF.Exp),
                 reads=[dg], writes=[self.d_gh[hd]])
            for c in range(4):
                cs = slice(c * 128, (c + 1) * 128)
                P.op("act", lambda h, hd=hd, c=c, cs=cs: h.activation(
                    out=self.uB[hd][:, cs], in_=self.nega[:, cs], func=AF.Exp,
                    bias=self.Aext[:, c * 128:c * 128 + 1], scale=-1.0), reads=[dg], writes=[self.d_uB[hd]])
                P.op("act", lambda h, hd=hd, c=c, cs=cs: h.activation(
                    out=self.gB[hd][:, cs], in_=self.Aext[:, 1 + c * 128:1 + (c + 1) * 128], func=AF.Exp,
                    bias=self.posM[hd][:, c:c + 1], scale=1.0), reads=[dg, self.d_gh[hd]], writes=[self.d_uB[hd]])
            P.op("dve", lambda h, hd=hd: h.tensor_copy(out=car[:, hd:hd + 1], in_=self.Bext[:, TT:TT + 1]),
                 reads=[dg], writes=[d_car])
            P.op("dve", lambda h, hd=hd: h.tensor_copy(out=car[:, 4 + hd:5 + hd], in_=self.Aext[:, TT:TT + 1]),
                 reads=[dg], writes=[d_car])
            P.op("dve", lambda h, hd=hd: h.tensor_copy(out=car[:, 8 + hd:9 + hd], in_=self.gbuf[hd][:, 4:5]),
                 reads=[self.d_gh[hd]], writes=[d_car])
            bq, dq = self.feat_proj(self.colchunk(w, hd * 128) if self.host else None)
            P.op("dve", lambda h, hd=hd, bq=bq: h.tensor_tensor(out=self.qp[hd][:, :], in0=bq[:, :], in1=self.gB[hd][:, :],
                                                                 op=ALU.mult), reads=[dq, self.d_uB[hd]], writes=[self.d_qp[hd]])
            bk_, dk_ = self.feat_proj(self.colchunk(w, 512 + hd * 128) if self.host else None)
            P.op("dve", lambda h, hd=hd, bk_=bk_: h.tensor_tensor(out=self.kp[hd][:, :], in0=bk_[:, :], in1=self.uB[hd][:, :],
                                                                   op=ALU.mult), reads=[dk_, self.d_uB[hd]], writes=[self.d_kp[hd]])
        if blk == 0 or True:
            P.op("pool", lambda h: h.memset(self.vtok[:, :, :, 256:257], 1.0), writes=self.d_vtok)
        for qh in range(4):
            def mk(qh=qh):
                v = w[:, 1024 + qh * 256:1024 + (qh + 1) * 256].reshape(8, 128, 256)
                return v.transpose(1, 0, 2).reshape(128, 2048)
            slot, wd = self.wload(mk, 2048)
            for b in range(4):
                bk, bd = self.bank()
                for k in range(8):
                    P.op("pe", lambda h, k=k, b=b, slot=slot, bk=bk: h.matmul(
                        bk[:, 0:256], self.xn[:, k, b * 128:(b + 1) * 128], slot[:, k * 256:(k + 1) * 256],
                        start=(k == 0), stop=(k == 7)), reads=[wd, self.d_xnc[k]], writes=[bd])
                if b % 2 == 0:
                    P.op("act", lambda h, b=b, qh=qh, bk=bk: h.activation(
                        out=self.vtok[:, b, qh, 0:256], in_=bk[:, 0:256], func=AF.Copy), reads=[bd], writes=[self.d_vtok[b]])
                else:
                    P.op("dve", lambda h, b=b, qh=qh, bk=bk: h.tensor_copy(
                        out=self.vtok[:, b, qh, 0:256], in_=bk[:, 0:256]), reads=[bd], writes=[self.d_vtok[b]])
        self.mem_attn(l, w, A_TOK)
        its = [(c, hd) for c in range(4) for hd in range(4)]
        cst = {}

        def ck_ST(i):
            c, hd = its[i]
            cs = slice(c * 128, (c + 1) * 128)
            r = i % 4
            b1, d1 = self.bx(0 + i % 2)
            P.op("pe", lambda h: h.matmul(b1[:, 0:128], self.kp[hd][:, cs], self.qp[hd][:, cs], start=True, stop=True),
                 reads=[self.d_kp[hd], self.d_qp[hd]], writes=[d1])
            P.op("dve", lambda h: h.tensor_tensor(out=self.SmT[r][:, :], in0=b1[:, 0:128], in1=maskcur, op=ALU.mult),
                 reads=[d1, self.d_C], writes=[self.d_SmT[r]])
            b2, d2 = self.bx(2 + i % 2)
            b2b = b2.bitcast(BF16)
            P.op("pe", lambda h: h.transpose(b2b[:, 0:128], self.kp[hd][:, cs], ident),
                 reads=[self.d_kp[hd], self.d_C], writes=[d2])
            P.op("act", lambda h: h.activation(out=self.kptok[r][:, :], in_=b2b[:, 0:128], func=AF.Copy),
                 reads=[d2], writes=[self.d_kptok[r]])

        def ck_ND(i):
            c, hd = its[i]
            cs = slice(c * 128, (c + 1) * 128)
            r = i % 4
            dst = self.d_st[j][hd]
            b3, d3 = self.bx(4 + i % 2)
            for g in range(3):
                if g < 2:
                    l1 = self.vtok[:, c, hd, g * 128:(g + 1) * 128]
                    l2 = self.Cb[j][hd][:, g * 128:(g + 1) * 128]
                else:
                    l1 = ones
                    l2 = self.nrep[j][hd][:, :]
                P.op("pe", lambda h, g=g, l1=l1: h.matmul(b3[:, g * 128:(g + 1) * 128], l1, self.SmT[r][:, :],
                                                          start=True, stop=False),
                     reads=[self.d_vtok[c], self.d_SmT[r], self.d_C], writes=[d3])
                P.op("pe", lambda h, g=g, l2=l2: h.matmul(b3[:, g * 128:(g + 1) * 128], l2, self.qp[hd][:, cs],
                                                          start=False, stop=True),
                     reads=[dst, self.d_qp[hd]], writes=[d3])
            P.op("act", lambda h: h.activation(
                out=self.numS[hd][:, :, cs], in_=b3[:, 0:384].rearrange("p (g n) -> p g n", g=3), func=AF.Copy),
                reads=[d3], writes=[self.d_numS[hd]])
            b4, d4 = self.bx(6 + i % 2)
            P.op("pe", lambda h: h.matmul(b4[:, 0:257], self.kptok[r][:, :], self.vtok[:, c, hd, :], start=True, stop=True),
                 reads=[self.d_kptok[r], self.d_vtok[c]], writes=[d4])
            P.op("dve", lambda h: h.scalar_tensor_tensor(
                out=self.Dst[j][hd][:, :], in0=self.Dst[j][hd][:, :], scalar=self.gbuf[hd][:, c:c + 1], in1=b4[:, 0:257],
                op0=ALU.mult, op1=ALU.add), reads=[d4, dst, self.d_gh[hd]], writes=[dst])
            P.op("act", lambda h: h.activation(out=self.Cb[j][hd][:, :], in_=self.Dst[j][hd][:, :], func=AF.Identity,
                                               scale=self.gbuf[hd][:, c + 1:c + 2]),
                 reads=[dst, self.d_gh[hd]], writes=[dst])
            P.op("act", lambda h: h.activation(
                out=self.nrep[j][hd][:, :], in_=self.Dst[j][hd][:, 256:257].to_broadcast([128, 128]), func=AF.Identity,
                scale=self.gbuf[hd][:, c + 1:c + 2]), reads=[dst, self.d_gh[hd]], writes=[dst])

        ck_ST(0)
        for i in range(len(its)):
            if i + 1 < len(its):
                ck_ST(i + 1)
            ck_ND(i)
        self.psum_i = 0
        for hd in range(4):
            i2 = hd % 2
            for dvc in range(2):
                bo, do = self.feat_proj(self.colchunk(w, 2048 + (2 * hd + dvc) * 128) if self.host else None)
                P.op("act", lambda h, bo=bo, i2=i2, dvc=dvc: h.activation(out=self.ogT[i2][:, dvc, :], in_=bo[:, :], func=AF.Sigmoid),
                     reads=[do], writes=[self.d_ogT[i2]])
            ns = self.numS[hd]
            for dvc in range(2):
                P.op("act", lambda h, ns=ns, i2=i2, dvc=dvc: h.activation(out=self.sqh[i2][:, dvc, :], in_=ns[:, dvc, :], func=AF.Square),
                     reads=[self.d_numS[hd]], writes=[self.d_sqh[i2]])
            bm, dm = self.bank()
            for dvc in range(2):
                P.op("pe", lambda h, bm=bm, i2=i2, dvc=dvc: h.matmul(bm[:, :], ones, self.sqh[i2][:, dvc, :],
                                                                     start=(dvc == 0), stop=(dvc == 1)),
                     reads=[self.d_sqh[i2], self.d_C], writes=[dm])
            t0_, t1_, t2_ = self.t1
            dt0, dt1, dt2 = self.d_t1
            P.op("act", lambda h, ns=ns: h.activation(out=t0_[:, :], in_=ns[:, 2, :], func=AF.Abs),
                 reads=[self.d_numS[hd]], writes=[dt0])
            P.op("dve", lambda h, hd=hd: h.tensor_tensor(out=t0_[:, :], in0=t0_[:, :], in1=self.eBA[hd][:, :], op=ALU.max),
                 reads=[dt0, self.d_gh[hd]], writes=[dt0])
            P.op("dve", lambda h: h.scalar_tensor_tensor(out=t1_[:, :], in0=t0_[:, :], scalar=EPS, in1=t0_[:, :],
                                                          op0=ALU.mult, op1=ALU.mult), reads=[dt0], writes=[dt1])
            P.op("dve", lambda h, bm=bm: h.scalar_tensor_tensor(out=t2_[:, :], in0=bm[:, :], scalar=1.0 / 256, in1=t1_[:, :],
                                                                 op0=ALU.mult, op1=ALU.add), reads=[dm, dt1], writes=[dt2])
            P.op("act", lambda h: h.activation(out=t2_[:, :], in_=t2_[:, :], func=AF.Ln), reads=[dt2], writes=[dt2])
            P.op("act", lambda h: h.activation(out=t0_[:, :], in_=t2_[:, :], func=AF.Exp, scale=-0.5), reads=[dt2], writes=[dt0])
            gh0 = ci["ghn", j]
            for dvc in range(2):
                ch = 2 * hd + dvc
                P.op("dve", lambda h, ns=ns, dvc=dvc, ch=ch: h.scalar_tensor_tensor(
                    out=t1_[:, :], in0=ns[:, dvc, :], scalar=self.C[:, gh0 + ch:gh0 + ch + 1], in1=t0_[:, :],
                    op0=ALU.mult, op1=ALU.mult), reads=[self.d_numS[hd], dt0, self.d_C], writes=[dt1])
                P.op("dve", lambda h, i2=i2, dvc=dvc, ch=ch: h.tensor_tensor(
                    out=self.yT[:, ch, :], in0=t1_[:, :], in1=self.ogT[i2][:, dvc, :], op=ALU.mult),
                    reads=[dt1, self.d_ogT[i2]], writes=[self.d_y[ch]])
        self.out_proj(w_out)

    def mixB(self, l, blk):
        P = self.P
        ci = self.cidx
        j = l // 2
        w = self.inp["b_w_in"][j] if self.host else None
        w_out = self.inp["b_w_out"][j] if self.host else None
        self.rmsnorm(ci["mg", l])
        mask4 = self.cbv("mask4", 512)
        lohi = [self.cbv("onelo"), self.cbv("onehi")]
        bones = self.cbv("blockones")
        RT = self.cbv("RT")
        dprev = self.d_prev[j]
        P.op("pool", lambda h: h.memset(self.vt[:, :, :, :], 0.0), writes=self.d_vt)
        if blk > 0:
            P.op("pool", lambda h: h.tensor_copy(out=self.vt[:, 0, :, :], in_=self.prevV[j][:, :, :]),
                 reads=[dprev], writes=[self.d_vt[0]])
        slot, wd = self.wload(self.colchunk(w, 1152) if self.host else None, 1024)
        for b in range(4):
            bk, bd = self.bank()
            for k in range(8):
                P.op("pe", lambda h, k=k, b=b, slot=slot, bk=bk: h.matmul(
                    bk[:, 0:128], self.xn[:, k, b * 128:(b + 1) * 128], slot[:, k * 128:(k + 1) * 128],
                    start=(k == 0), stop=(k == 7)), reads=[wd, self.d_xnc[k]], writes=[bd])
            pv = bk[:, 0:128].rearrange("p (g n) -> p g n", g=2)
            P.op("act", lambda h, b=b, pv=pv: h.activation(out=self.vt[:, b + 1, 0:4:2, 0:64], in_=pv, func=AF.Copy),
                 reads=[bd], writes=[self.d_vt[b + 1]])
            P.op("dve", lambda h, b=b, pv=pv: h.tensor_copy(out=self.vt[:, b + 1, 1:4:2, 64:128], in_=pv),
                 reads=[bd], writes=[self.d_vt[b + 1]])
        P.op("pool", lambda h: h.memset(self.qlo[:, :, :], 0.0), writes=self.d_qT)
        P.op("pool", lambda h: h.memset(self.qhi[:, :, :], 0.0), writes=self.d_qT)
        for g in range(2):
            if blk > 0:
                P.op("pool", lambda h, g=g: h.tensor_copy(out=self.kT[g][:, 0:128], in_=self.prevK[j][:, g, :]),
                     reads=[dprev], writes=[self.d_kT[g]])
        self.psum_i = 0
        ta, tb = self.t1[0], self.t1[1]
        da, db = self.d_t1[0], self.d_t1[1]
        chunks = []
        for g in range(2):
            def mk(g=g):
                kc = w[:, 1024 + g * 64:1024 + (g + 1) * 64]
                kk = np.concatenate([kc, kc], axis=1)
                return kk.reshape(8, 128, 128).transpose(1, 0, 2).reshape(128, 1024)
            chunks.append((mk, ci["bkg", j], self.kT[g][:, 128:640], self.d_kT[g]))
        for c in range(8):
            chunks.append((self.colchunk(w, c * 128) if self.host else None, ci["bqg", j],
                           (self.qlo[:, c, :], self.qhi[:, c, :]), self.d_qT[c]))
        pst = {}

        def nrA1(s):
            mk, gcol, out_ap, d_out = chunks[s]
            bk, bd = self.feat_proj(mk if self.host else None, bank=s % 3)
            i2 = s % 2
            P.op("act", lambda h: h.activation(out=self.sqq[i2][:, :], in_=bk[:, :], func=AF.Square),
                 reads=[bd], writes=[self.d_sqq[i2]])
            pst[s] = (bk, bd)

        def nrA2(s):
            mk, gcol, out_ap, d_out = chunks[s]
            bk, bd = pst[s]
            i2 = s % 2
            bs, ds = self.bx(3 + s % 2)
            P.op("pe", lambda h: h.matmul(bs[:, :], bones, self.sqq[i2][:, :], start=True, stop=True),
                 reads=[self.d_sqq[i2], self.d_C], writes=[ds])
            P.op("act", lambda h: h.activation(out=self.tmpA[:, :], in_=bs[:, :], func=AF.Ln, bias=self.epsc, scale=1.0 / 64),
                 reads=[ds], writes=[self.d_tmpA])
            P.op("act", lambda h: h.activation(out=self.rstd[:, :], in_=self.tmpA[:, :], func=AF.Exp, scale=-0.5),
                 reads=[self.d_tmpA], writes=[self.d_rstd])
            P.op("dve", lambda h: h.scalar_tensor_tensor(out=self.qn[i2][:, :], in0=bk[:, :], scalar=self.C[:, gcol:gcol + 1],
                                                         in1=self.rstd[:, :], op0=ALU.mult, op1=ALU.mult),
                 reads=[bd, self.d_rstd, self.d_C], writes=[self.d_qn[i2]])

        def nrB(s):
            mk, gcol, out_ap, d_out = chunks[s]
            i2 = s % 2
            br, dr = self.bx(5 + s % 2)
            P.op("pe", lambda h: h.matmul(br[:, :], RT, self.qn[i2][:, :], start=True, stop=True),
                 reads=[self.d_qn[i2], self.d_C], writes=[dr])
            P.op("dve", lambda h: h.tensor_tensor(out=ta[:, :], in0=self.qn[i2][:, :], in1=self.cosF[:, :], op=ALU.mult),
                 reads=[self.d_qn[i2], self.d_rope], writes=[da])
            P.op("dve", lambda h: h.tensor_tensor(out=tb[:, :], in0=br[:, :], in1=self.sinF[:, :], op=ALU.mult),
                 reads=[dr, self.d_rope], writes=[db])
            if isinstance(out_ap, tuple):
                lo_ap, hi_ap = out_ap
                P.op("dve", lambda h: h.tensor_tensor(out=lo_ap[0:64, :], in0=ta[0:64, :], in1=tb[0:64, :], op=ALU.add),
                     reads=[da, db], writes=[d_out])
                P.op("dve", lambda h: h.tensor_tensor(out=hi_ap[64:128, :], in0=ta[64:128, :], in1=tb[64:128, :], op=ALU.add),
                     reads=[da, db], writes=[d_out])
            else:
                P.op("dve", lambda h: h.tensor_tensor(out=out_ap, in0=ta[:, :], in1=tb[:, :], op=ALU.add),
                     reads=[da, db], writes=[d_out])

        nch = len(chunks)
        for s in range(nch + 2):
            if s < nch:
                nrA1(s)
            if 0 <= s - 1 < nch:
                nrA2(s - 1)
            if 0 <= s - 2 < nch:
                nrB(s - 2)
        self.psum_i = 0
        self.mem_attn(l, w, 1280)
        its = [(c, qb) for c in range(8) for qb in range(4)]
        ist = {}

        def att_S(i):
            c, qb = its[i]
            g = c // 4
            has_prev = not (blk == 0 and qb == 0)
            kbs = (0, 1) if has_prev else (1,)
            bs, ds = self.bx(4 + i % 4)
            for kb in kbs:
                kc0 = (qb + kb) * 128
                for e in range(2):
                    qsrc = self.qlo if e == 0 else self.qhi
                    col = (kb * 2 + e) * 128
                    P.op("pe", lambda h, qsrc=qsrc, kc0=kc0, col=col: h.matmul(
                        bs[:, col:col + 128], self.kT[g][:, kc0:kc0 + 128], qsrc[:, c, qb * 128:(qb + 1) * 128],
                        start=True, stop=True), reads=[self.d_kT[g], self.d_qT[c]], writes=[ds])
            ist[i] = (bs, ds, kbs, has_prev)

        def att_E(i):
            bs, ds, kbs, has_prev = ist[i]
            lo_c = 0 if has_prev else 256
            E = self.Es[i % 3]
            dE = self.d_Es[i % 3]
            P.op("act", lambda h: h.activation(out=E[:, lo_c:512], in_=bs[:, lo_c:512], func=AF.Exp, scale=0.125),
                 reads=[ds], writes=[dE])
            P.op("dve", lambda h: h.tensor_tensor(out=E[:, lo_c:512], in0=E[:, lo_c:512], in1=mask4[:, lo_c:512], op=ALU.mult),
                 reads=[dE, self.d_C], writes=[dE])

        def att_O(i):
            c, qb = its[i]
            g = c // 4
            bs, ds, kbs, has_prev = ist[i]
            E = self.Es[i % 3]
            dE = self.d_Es[i % 3]
            bo, do = self.bx(0 + 2 * (c % 2))
            bdn, ddn = self.bx(1 + 2 * (c % 2))
            terms = [(kb, e) for kb in kbs for e in range(2)]
            nt = len(terms)
            for ti, (kb, e) in enumerate(terms):
                col = (kb * 2 + e) * 128
                P.op("pe", lambda h, kb=kb, e=e, col=col, ti=ti: h.matmul(
                    bo[:, qb * 128:(qb + 1) * 128], self.vt[:, qb + kb, 2 * g + e, :], E[:, col:col + 128],
                    start=(ti == 0), stop=(ti == nt - 1)), reads=[self.d_vt[qb + kb], dE], writes=[do])
            for ti, (kb, e) in enumerate(terms):
                col = (kb * 2 + e) * 128
                P.op("pe", lambda h, e=e, col=col, ti=ti: h.matmul(
                    bdn[:, qb * 128:(qb + 1) * 128], lohi[e], E[:, col:col + 128],
                    start=(ti == 0), stop=(ti == nt - 1)), reads=[self.d_C, dE], writes=[ddn])
            if qb == 3:
                t2_, dt2 = self.t1[2], self.d_t1[2]
                sc = 8 * j + c
                P.op("act", lambda h: h.activation(out=t2_[:, :], in_=bdn[:, :], func=AF.Ln,
                                                   bias=self.esink[:, sc:sc + 1], scale=1.0),
                     reads=[ddn, self.d_misc], writes=[dt2])
                P.op("act", lambda h: h.activation(out=t2_[:, :], in_=t2_[:, :], func=AF.Exp, scale=-1.0), reads=[dt2], writes=[dt2])
                P.op("dve", lambda h: h.tensor_tensor(out=self.yT[:, c, :], in0=bo[:, :], in1=t2_[:, :], op=ALU.mult),
                     reads=[do, dt2], writes=[self.d_y[c]])

        att_S(0)
        for i in range(len(its)):
            if i + 1 < len(its):
                att_S(i + 1)
            att_E(i)
            att_O(i)
        self.psum_i = 0
        for g in range(2):
            P.op("pool", lambda h, g=g: h.tensor_copy(out=self.prevK[j][:, g, :], in_=self.kT[g][:, 512:640]),
                 reads=[self.d_kT[g]], writes=[dprev])
        P.op("pool", lambda h: h.tensor_copy(out=self.prevV[j][:, :, :], in_=self.vt[:, 4, :, :]),
             reads=[self.d_vt[4]], writes=[dprev])
        for c_ in range(8):
            self.dump("yB%d" % c_, self.yT[:, c_, :], TT, [self.d_y[c_]])
        self.out_proj(w_out)

    def rmsnorm(self, gcol):
        P = self.P
        ones = self.CB[:, self.cb["ones"]:self.cb["ones"] + 128]
        for c in range(8):
            if c >= 3:
                P.op("act", lambda h, c=c: h.activation(out=self.sq[:, c, :], in_=self.xT[:, c, :], func=AF.Square),
                     reads=[self.d_xc[c]], writes=[self.d_sqc[c]])
            else:
                P.op("pool", lambda h, c=c: h.tensor_tensor(out=self.sq[:, c, :], in0=self.xT[:, c, :], in1=self.xT[:, c, :], op=ALU.mult),
                     reads=[self.d_xc[c]], writes=[self.d_sqc[c]])
        bk, bd = self.bank()
        for c in range(8):
            P.op("pe", lambda h, c=c, bk=bk: h.matmul(bk[:, :], ones, self.sq[:, c, :], start=(c == 0), stop=(c == 7)),
                 reads=[self.d_sqc[c], self.d_C], writes=[bd])
        P.op("act", lambda h, bk=bk: h.activation(out=self.tmpA[:, :], in_=bk[:, :], func=AF.Ln,
                                                   bias=self.epsc, scale=1.0 / D),
             reads=[bd, self.d_C], writes=[self.d_tmpA])
        P.op("act", lambda h: h.activation(out=self.rstd[:, :], in_=self.tmpA[:, :], func=AF.Exp, scale=-0.5),
             reads=[self.d_tmpA], writes=[self.d_rstd])
        for c in range(8):
            P.op("dve", lambda h, c=c: h.scalar_tensor_tensor(
                out=self.xn[:, c, :], in0=self.xT[:, c, :], scalar=self.C[:, gcol + c:gcol + c + 1],
                in1=self.rstd[:, :], op0=ALU.mult, op1=ALU.mult),
                reads=[self.d_xc[c], self.d_rstd, self.d_C], writes=[self.d_xnc[c]])

    def ffn(self, l, which):
        P = self.P
        inp = self.inp
        pre = "ffn1" if which == 0 else "ffn2"
        self.rmsnorm(self.cidx[("f1g" if which == 0 else "f2g"), l])
        w_in = inp[pre + "_w_in"][l] if self.host else None
        w_out = inp[pre + "_w_out"][l] if self.host else None
        for j in range(NFF):
            def mk(j=j):
                g = w_in[:, j * 128:(j + 1) * 128].reshape(8, 128, 128)
                u = w_in[:, DFF + j * 128:DFF + (j + 1) * 128].reshape(8, 128, 128)
                a = np.stack([g, u], axis=2)
                return a.transpose(1, 0, 2, 3).reshape(128, 2048)
            slot, wd = self.wload(mk, 2048)
            bg, dg = self.bank()
            bu, du = self.bank()
            for k in range(8):
                P.op("pe", lambda h, k=k, slot=slot, bg=bg: h.matmul(
                    bg[:, :], slot[:, k * 256:k * 256 + 128], self.xn[:, k, :], start=(k == 0), stop=(k == 7)),
                    reads=[wd, self.d_xnc[k]], writes=[dg])
            for k in range(8):
                P.op("pe", lambda h, k=k, slot=slot, bu=bu: h.matmul(
                    bu[:, :], slot[:, k * 256 + 128:k * 256 + 256], self.xn[:, k, :], start=(k == 0), stop=(k == 7)),
                    reads=[wd, self.d_xnc[k]], writes=[du])
            sg, dsg = self.sg[j % 2], self.d_sg[j % 2]
            P.op("act", lambda h, sg=sg, bg=bg: h.activation(out=sg[:, :], in_=bg[:, :], func=AF.Silu),
                 reads=[dg], writes=[dsg])
            P.op("dve", lambda h, sg=sg, bu=bu, j=j: h.tensor_tensor(
                out=self.act[:, j, :], in0=sg[:, :], in1=bu[:, :], op=ALU.mult),
                reads=[dsg, du], writes=[self.d_act[j]])
        HK = NFF // 2
        for c in range(8):
            halves = []
            for hf in range(2):
                def mk(c=c, hf=hf):
                    a = w_out[hf * HK * 128:(hf + 1) * HK * 128, c * 128:(c + 1) * 128].reshape(HK, 128, 128)
                    return a.transpose(1, 0, 2).reshape(128, HK * 128)
                halves.append(self.wload(mk, HK * 128))
            bk, bd = self.bank()
            for k in range(NFF):
                slot, wd = halves[k // HK]
                kk = k % HK
                P.op("pe", lambda h, k=k, kk=kk, slot=slot, bk=bk: h.matmul(
                    bk[:, :], slot[:, kk * 128:(kk + 1) * 128], self.act[:, k, :], start=(k == 0), stop=(k == NFF - 1)),
                    reads=[wd, self.d_act[k]], writes=[bd])
            P.op("dve", lambda h, c=c, bk=bk: h.scalar_tensor_tensor(
                out=self.xT[:, c, :], in0=bk[:, :], scalar=0.5, in1=self.xT[:, c, :], op0=ALU.mult, op1=ALU.add),
                reads=[bd, self.d_xc[c]], writes=[self.d_xc[c]])


def _prep_inputs(inputs):
    return {k: np.asarray(v) for k, v in inputs.items()}


def host_wmem(inp):
    w = np.asarray(inp["mem_w_kv"], np.float32)
    parts = [w[:, h * 128:(h + 1) * 128].reshape(8, 128, 128).transpose(1, 0, 2).reshape(128, 1024) for h in range(4)]
    for hf in range(2):
        parts.append(w[:, 512 + hf * 256:512 + (hf + 1) * 256].reshape(8, 128, 256).transpose(1, 0, 2).reshape(128, 2048))
    return np.ascontiguousarray(np.concatenate(parts, axis=1))


def core_inputs(b, inp, c):
    sl = slice(c * SEQ_PER_CORE, (c + 1) * SEQ_PER_CORE)
    return {"xT": np.ascontiguousarray(inp["x"][sl].transpose(0, 2, 1)),
            "memT": np.ascontiguousarray(inp["mem"][sl].transpose(0, 2, 1)),
            "pos": np.ascontiguousarray(np.broadcast_to(inp["positions"][sl].astype(np.float32)[:, None, :], (SEQ_PER_CORE, 128, SEQ))),
            "consts": b.h_consts, "cbf": b.cb_arr, "wst": b.h_wst, "wmem": b.h_wmem}


def build_all(inp, dbg=None, **kw):
    b = Builder(inp, **kw)
    for k_, v_ in (dbg or {}).items():
        setattr(b, k_, v_)
    nc = b.build()
    b.h_wst = np.ascontiguousarray(np.concatenate(b.wgroups, axis=1))
    b.h_consts = np.ascontiguousarray(np.concatenate(b.consts, axis=1))
    b.h_wmem = host_wmem(inp)
    return b, nc


def kernel(**inputs):
    inp = _prep_inputs(inputs)
    b, nc = build_all(inp)
    in_maps = [core_inputs(b, inp, c) for c in range(N_CORES)]
    res = run_bass_kernel_spmd(nc, in_maps, core_ids=list(range(N_CORES)))
    out = np.concatenate([r["oT"].transpose(0, 2, 1) for r in res.results], axis=0)
    return np.ascontiguousarray(out.astype(np.float32))
```

```python
import math
import os
from contextlib import ExitStack

import numpy as np
import concourse.bass as bass
import concourse.mybir as mybir
from concourse.bass_utils import run_bass_kernel_spmd

F32 = mybir.dt.float32
BF16 = mybir.dt.bfloat16
I32 = mybir.dt.int32
AF = mybir.ActivationFunctionType
ALU = mybir.AluOpType

D = 1024
SEQ = 2048
DEPTH = 4
DFF = 2816
NFF = DFF // 128
EPS = 1e-6
TT = 512
NMEM = 256
A_TOK = 3080
N_CORES = 8
SEQ_PER_CORE = 2


class Dep:
    __slots__ = ("w", "r", "name")

    def __init__(self, name=""):
        self.w = None
        self.r = []
        self.name = name


class Op:
    __slots__ = ("eng", "fn", "waits", "signal", "idx", "dma", "semval", "chan")

    def __init__(self, eng, fn):
        self.eng = eng
        self.fn = fn
        self.waits = []
        self.signal = False
        self.idx = 0
        self.dma = False
        self.semval = 0
        self.chan = None


ENGS = ("pe", "act", "dve", "pool", "sp")


class Prog:
    def __init__(self, nc):
        self.nc = nc
        self.streams = {e: [] for e in ENGS}
        self.waited = {e: {} for e in ENGS}
        self.chan_cnt = {}
        self.chan_last = {}

    def _add_wait(self, op, src):
        if src is None:
            return
        if src.dma:
            key = ("c", src.chan)
            val = src.semval
        else:
            if src.eng == op.eng and op.eng == "pe":
                return
            key = ("e", src.eng)
            val = src.idx
        w = self.waited[op.eng]
        if w.get(key, -1) >= val:
            return
        w[key] = val
        src.signal = True
        op.waits.append(src)

    def op(self, eng, fn, reads=(), writes=(), chan=None):
        o = Op(eng, fn)
        st = self.streams[eng]
        o.idx = len(st)
        if chan is not None:
            o.dma = True
            o.chan = chan
            n = self.chan_cnt.get(chan, 0) + 1
            self.chan_cnt[chan] = n
            o.semval = 16 * n
            o.signal = True
            prev = self.chan_last.get(chan)
            if prev is not None:
                self._add_wait(o, prev)
            self.chan_last[chan] = o
        for d in reads:
            self._add_wait(o, d.w)
        for d in writes:
            self._add_wait(o, d.w)
            for r in d.r:
                self._add_wait(o, r)
        for d in reads:
            d.r.append(o)
        for d in writes:
            d.w = o
            d.r = []
        st.append(o)
        return o

    def barrier(self, engs=("pe", "act", "dve", "pool", "sp")):
        last = {}
        for e in engs:
            last[e] = None
            for o_ in reversed(self.streams[e]):
                if o_.fn is not None:
                    last[e] = o_
                    break
        for e in engs:
            o = Op(e, None)
            o.idx = len(self.streams[e])
            for f in engs:
                if f != e and last[f] is not None:
                    self._add_wait(o, last[f])
            if o.waits:
                self.streams[e].append(o)

    def emit(self, final_waits):
        nc = self.nc
        with ExitStack() as es:
            esem = {e: es.enter_context(nc.semaphore("s_" + e)) for e in ENGS}
            csem = {c: es.enter_context(nc.semaphore("c_%s" % (c,))) for c in self.chan_cnt}
            for e in ENGS:
                n = 0
                for o in self.streams[e]:
                    if o.dma:
                        continue
                    if o.signal:
                        n += 1
                        o.semval = n
            block = es.enter_context(nc.Block())
            handles = {"pe": block.tensor, "act": block.scalar, "dve": block.vector,
                       "pool": block.gpsimd, "sp": block.sync}

            def run(e):
                def body(h):
                    for o in self.streams[e]:
                        for s in o.waits:
                            if s.dma:
                                h.wait_ge(csem[s.chan], s.semval)
                            else:
                                h.wait_ge(esem[s.eng], s.semval)
                        if o.fn is None:
                            continue
                        ins = o.fn(h)
                        if o.dma:
                            ins.then_inc(csem[o.chan], 16)
                        elif o.signal:
                            ins.then_inc(esem[e], 1)
                    if e == "sp":
                        for s in final_waits:
                            h.wait_ge(csem[s.chan], s.semval)
                return body

            for e in ENGS:
                handles[e](run(e))


class Builder:
    def __init__(self, inputs, n_super=8, n_layers=DEPTH, host=True):
        self.inp = inputs
        self.n_super = n_super
        self.n_layers = n_layers
        self.host = host
        self.nc = bass.Bass("TRN2", target_bir_lowering=False)
        self.P = Prog(self.nc)
        self.sb_off = 16640
        self.consts = []
        self.const_off = 0
        self.wgroups = []
        self.w_off = 0
        self.w_offsets = []
        self.wi = 0
        self.first_super = True
        self.psum_i = 0
        self.epsc = EPS

    def sb(self, name, shape, dt):
        size = int(np.prod(shape[1:])) * (4 if dt in (F32, I32) else 2)
        size = (size + 31) // 32 * 32
        t = self.nc.alloc_sbuf_tensor_at(name, list(shape), dt, offset=self.sb_off)
        self.sb_off += size
        return t

    def const(self, arr):
        arr = np.ascontiguousarray(arr, dtype=np.float32)
        assert arr.shape[0] == 128
        off = self.const_off
        self.consts.append(arr)
        self.const_off += arr.shape[1]
        return off

    def dump(self, name, ap, ncols, reads):
        if not getattr(self, "dbg_dump", False):
            return
        if not hasattr(self, "dumps"):
            self.dumps = []
            self.dump_off = 0
        off = self.dump_off
        self.dumps.append((name, off, ncols))
        self.dump_off += ncols
        self.P.op("pool", lambda h, ap=ap, off=off, ncols=ncols: h.dma_start(out=self.ddram[:, off:off + ncols], in_=ap),
                  reads=reads, chan="dbg")

    def bx(self, i):
        return self.psum[i % 8], self.psum_dep[i % 8]

    def bank(self):
        b = self.psum[self.psum_i % 8]
        d = self.psum_dep[self.psum_i % 8]
        self.psum_i += 1
        return b, d

    def wload(self, make_arr, n):
        if self.first_super:
            if self.host:
                a = np.ascontiguousarray(make_arr(), dtype=np.float32)
                assert a.shape == (128, n), (a.shape, n)
                self.wgroups.append(a)
            self.w_offsets.append((self.w_off, n))
            self.w_off += n
        off, n0 = self.w_offsets[self.wi]
        assert n0 == n
        self.wi += 1
        k = self.wslot_i % len(self.wslots)
        self.wslot_i += 1
        slot, dep = self.wslots[k], self.wslot_deps[k]
        assert n <= slot.shape[1]
        dst = slot[:, 0:n]
        self.P.op("pool", lambda h, dst=dst, off=off, n=n: h.dma_start(out=dst, in_=self.wdram[:, off:off + n]),
                  writes=[dep], chan=("w", k))
        return slot, dep

    def build(self):
        nc, P = self.nc, self.P
        self.xdram = nc.dram_tensor("xT", [SEQ_PER_CORE, D, SEQ], F32, kind="ExternalInput").ap()
        self.mdram = nc.dram_tensor("memT", [SEQ_PER_CORE, D, NMEM], F32, kind="ExternalInput").ap()
        self.pdram = nc.dram_tensor("pos", [SEQ_PER_CORE, 128, SEQ], F32, kind="ExternalInput").ap()
        self.odram = nc.dram_tensor("oT", [SEQ_PER_CORE, D, SEQ], F32, kind="ExternalOutput").ap()
        self._register_consts()
        self.cdram = nc.dram_tensor("consts", [128, self.const_off], F32, kind="ExternalInput").ap()
        self.cbdram = nc.dram_tensor("cbf", [128, self.cb_n], F32, kind="ExternalInput").ap()
        self.wmdram = nc.dram_tensor("wmem", [128, 8192], F32, kind="ExternalInput").ap()

        sb = self.sb
        self.C = sb("C", [128, self.const_off], F32)
        self.CB = sb("CB", [128, self.cb_n], BF16)
        self.xT = sb("xT", [128, 8, TT], F32)
        self.xn = sb("xn", [128, 8, TT], BF16)
        self.sq = sb("sq", [128, 8, TT], BF16)
        self.rstd = sb("rstd", [128, TT], F32)
        self.tmpA = sb("tmpA", [128, TT], F32)
        self.wslots = [sb("ws%d" % i, [128, 2048], BF16) for i in range(6)]
        self.wslot_deps = [Dep("ws%d" % i) for i in range(6)]
        self.wslot_i = 0
        self.negb = sb("negb", [128, 16], F32)
        self.esink = sb("esink", [128, 16], F32)
        self.cosF = sb("cosF", [128, TT], F32)
        self.sinF = sb("sinF", [128, TT], F32)
        self.zeros = sb("zeros", [128, TT], F32)
        self.rf = [sb("rf%d" % i, [128, TT], F32) for i in range(4)]
        self.d_rs = Dep("ropescratch")
        self.Dst = [[sb("D%d_%d" % (j, h), [128, 257], F32) for h in range(4)] for j in range(2)]
        self.Cb = [[sb("Cb%d_%d" % (j, h), [128, 257], BF16) for h in range(4)] for j in range(2)]
        self.nrep = [[sb("nr%d_%d" % (j, h), [128, 128], BF16) for h in range(4)] for j in range(2)]
        self.car = [sb("car%d" % j, [128, 12], F32) for j in range(2)]
        self.prevK = [sb("pK%d" % j, [128, 2, 128], BF16) for j in range(2)]
        self.prevV = [sb("pV%d" % j, [128, 4, 128], BF16) for j in range(2)]
        self.memKn = sb("memKn", [128, 4, 4, NMEM], BF16)
        self.memV = sb("memV", [128, 2, 512], BF16)
        self.region_base = self.sb_off
        self.psum = [nc.alloc_psum_tensor("ps%d" % i, [128, 512], F32) for i in range(8)]
        self.psum_dep = [Dep("ps%d" % i) for i in range(8)]

        self.d_C = Dep("C")
        self.d_xc = [Dep("xT%d" % c) for c in range(8)]
        self.d_xnc = [Dep("xn%d" % c) for c in range(8)]
        self.d_sq = Dep("sq")
        self.d_sqc = [Dep("sq%d" % c) for c in range(8)]
        self.d_rstd = Dep("rstd")
        self.d_tmpA = Dep("tmpA")
        self.d_misc = Dep("misc")
        self.d_rope = Dep("rope")
        self.d_st = [[Dep("st%d_%d" % (j, h)) for h in range(4)] for j in range(2)]
        self.d_car = [Dep("car0"), Dep("car1")]
        self.d_prev = [Dep("prev0"), Dep("prev1")]
        self.d_mem = Dep("mem")

        self._alloc_ffn()
        self._alloc_prologue()
        self._alloc_mixA()
        self._alloc_mixB()
        assert self.region_max <= 229344, self.region_max

        P.op("sp", lambda h: h.dma_start(out=self.C[:, :], in_=self.cdram[:, :]), writes=[self.d_C], chan="c0")
        P.op("pool", lambda h: h.dma_start(out=self.CB[:, :], in_=self.cbdram[:, :]), writes=[self.d_C], chan="c1")
        ci = self.cidx
        P.op("dve", lambda h: h.tensor_scalar(out=self.negb[:, :], in0=self.C[:, ci["gb"]:ci["gb"] + 16],
                                              scalar1=-1.0, scalar2=None, op0=ALU.mult),
             reads=[self.d_C], writes=[self.d_misc])
        P.op("act", lambda h: h.activation(out=self.esink[:, :], in_=self.C[:, ci["sink"]:ci["sink"] + 16], func=AF.Exp),
             reads=[self.d_C], writes=[self.d_misc])
        P.op("dve", lambda h: h.memset(self.zeros[:, :], 0.0), writes=[self.d_misc])

        out_ops = []
        for s in range(self.n_super):
            seq, blk = divmod(s, SEQ // TT)
            self.wi = 0
            t0 = blk * TT
            if blk == 0:
                self.seq_start(seq)
            P.op("sp", lambda h, seq=seq, t0=t0: h.dma_start(
                out=self.xT[:, :, :],
                in_=self.xdram[seq, :, t0:t0 + TT].rearrange("(c p) t -> p c t", p=128)),
                writes=self.d_xc, chan="x")
            if self.n_layers > 1 or os.environ.get("DBG_FORCEROPE"):
                self.rope_tables(seq, t0)
            for l in range(self.n_layers):
                self.ffn(l, 0)
                P.barrier()
                if l % 2 == 0:
                    self.mixA(l, blk)
                else:
                    self.mixB(l, blk)
                P.barrier()
                if getattr(self, "dbg_skip_ffn2", False):
                    continue
                self.ffn(l, 1)
            o = P.op("sp", lambda h, seq=seq, t0=t0: h.dma_start(
                out=self.odram[seq, :, t0:t0 + TT].rearrange("(c p) t -> p c t", p=128),
                in_=self.xT[:, :, :]), reads=self.d_xc, chan="o")
            out_ops.append(o)
            self.first_super = False
        self.w_total = self.w_off
        if getattr(self, "dbg_dump", False):
            self.ddram = nc.dram_tensor("dbg", [128, max(self.dump_off, 1)], F32, kind="ExternalOutput").ap()
        self.wdram = nc.dram_tensor("wst", [128, self.w_total], F32, kind="ExternalInput").ap()
        P.emit(final_waits=[out_ops[-1]])
        return nc

    def _register_consts(self):
        inp = self.inp
        c = {}

        def pcols(v):
            v = np.asarray(v, dtype=np.float32)
            return v.reshape(-1, 128).T

        def bc(v):
            v = np.asarray(v, dtype=np.float32).reshape(1, -1)
            return np.repeat(v, 128, axis=0)

        for l in range(DEPTH):
            c["f1g", l] = self.const(pcols(inp["ffn1_norm_g"][l]))
            c["mg", l] = self.const(pcols(inp["mix_norm_g"][l]))
            c["f2g", l] = self.const(pcols(inp["ffn2_norm_g"][l]))
            c["xaq", l] = self.const(pcols(inp["xa_q_norm_g"][l]))
            c["xak", l] = self.const(pcols(inp["xa_k_norm_g"][l]))
        c["memg"] = self.const(pcols(inp["mem_norm_g"]))
        c["gb"] = self.const(np.concatenate([bc(inp["a_gate_b"][0]), bc(inp["a_gate_b"][1])], axis=1))
        for j in range(2):
            c["ghn", j] = self.const(pcols(inp["a_h_norm_g"][j]))
            c["bqg", j] = self.const(np.tile(np.asarray(inp["b_q_norm_g"][j], np.float32), 2).reshape(128, 1))
            c["bkg", j] = self.const(np.tile(np.asarray(inp["b_k_norm_g"][j], np.float32), 2).reshape(128, 1))
        sk = []
        for j in range(2):
            s_ = np.asarray(inp["b_sinks"][j], np.float32)
            a = np.zeros((128, 8), np.float32)
            for cc in range(8):
                a[:64, cc] = s_[2 * cc]
                a[64:, cc] = s_[2 * cc + 1]
            sk.append(a)
        c["sink"] = self.const(np.concatenate(sk, axis=1))
        invf = (500000.0 ** (-np.arange(0, 16, 2, dtype=np.float32) / np.float32(16))).astype(np.float32)
        iv = np.zeros((128, 1), np.float32)
        for p in range(128):
            d_ = p % 64
            if d_ < 16:
                iv[p, 0] = invf[d_ % 8]
        c["invf"] = self.const(iv)
        self.cidx = c
        cb = {}
        n = 0
        cbl = []

        def addb(name, arr):
            nonlocal n
            cb[name] = n
            cbl.append(np.ascontiguousarray(arr, dtype=np.float32))
            n += arr.shape[1]

        addb("ones", np.ones((128, 128), np.float32))
        addb("ident", np.eye(128, dtype=np.float32))
        s_i = np.arange(128)[:, None]
        j_i = np.arange(128)[None, :]
        cur = (s_i <= j_i).astype(np.float32)
        prev = (s_i > j_i).astype(np.float32)
        addb("maskcur", cur)
        addb("mask4", np.concatenate([prev, prev, cur, cur], axis=1))
        bo = np.zeros((128, 128), np.float32)
        bo[:64, :64] = 1
        bo[64:, 64:] = 1
        addb("blockones", bo)
        rt = np.zeros((128, 128), np.float32)
        for m in range(128):
            d_ = m % 64
            if d_ < 8:
                rt[m + 8, m] = -1.0
            elif d_ < 16:
                rt[m - 8, m] = 1.0
        addb("RT", rt)
        lo = np.zeros((128, 128), np.float32)
        lo[:, :64] = 1
        hi = np.zeros((128, 128), np.float32)
        hi[:, 64:] = 1
        addb("onelo", lo)
        addb("onehi", hi)
        self.cb = cb
        self.cb_n = n
        self.cb_arr = np.concatenate(cbl, axis=1)

    def cbv(self, name, n=128):
        o = self.cb[name]
        return self.CB[:, o:o + n]

    def _region(self):
        self.sb_off = self.region_base

    def _region_end(self):
        self.region_max = max(getattr(self, "region_max", 0), self.sb_off)

    def _alloc_ffn(self):
        self._region()
        self.act = self.sb("act", [128, NFF, TT], BF16)
        self.sg = [self.sb("sg%d" % i, [128, TT], F32) for i in range(2)]
        self.d_act = [Dep("act%d" % j) for j in range(NFF)]
        self.d_sg = [Dep("sg0"), Dep("sg1")]
        self._region_end()

    def _alloc_common_mix(self):
        sb = self.sb
        self.yT = sb("yT", [128, 12, TT], BF16)
        self.d_y = [Dep("y%d" % i) for i in range(12)]
        self.qn = [sb("qn%d" % i, [128, TT], BF16) for i in range(2)]
        self.d_qn = [Dep("qn0"), Dep("qn1")]
        self.Em = [sb("Em%d" % i, [128, 2, TT], BF16) for i in range(2)]
        self.d_Em = [Dep("Em0"), Dep("Em1")]
        self.rden = sb("rden", [128, TT], F32)
        self.d_rden = Dep("rden")
        self.sqq = [sb("sqq%d" % i, [128, TT], BF16) for i in range(2)]
        self.d_sqq = [Dep("sqq0"), Dep("sqq1")]
        self.t1 = [sb("t1_%d" % i, [128, TT], F32) for i in range(3)]
        self.d_t1 = [Dep("t1_%d" % i) for i in range(3)]

    def _alloc_prologue(self):
        self._region()
        sb = self.sb
        self.mT = sb("mT", [128, 8, NMEM], F32)
        self.memn = sb("memn", [128, 8, NMEM], BF16)
        self.memK = sb("memK", [128, 4, NMEM], F32)
        self.rstdk = sb("rstdk", [128, 4, NMEM], F32)
        self.psq = sb("psq", [128, NMEM], BF16)
        self.d_psq = Dep("psq")
        self._region_end()

    def _alloc_mixA(self):
        self._region()
        sb = self.sb
        self._alloc_common_mix()
        self.mix_common_end = self.sb_off
        self.gt1 = sb("gt1", [128, TT], F32)
        self.negi = sb("negi", [128, TT], F32)
        self.nega = sb("nega", [128, TT], F32)
        self.Bext = sb("Bext", [128, TT + 1], F32)
        self.Aext = sb("Aext", [128, TT + 1], F32)
        self.tmpB = sb("tmpB", [128, TT], F32)
        self.uB = [sb("uB%d" % h, [128, TT], BF16) for h in range(4)]
        self.gB = [sb("gB%d" % h, [128, TT], BF16) for h in range(4)]
        self.eBA = [sb("eBA%d" % h, [128, TT], BF16) for h in range(4)]
        self.posM = [sb("posM%d" % h, [128, 4], F32) for h in range(4)]
        self.gbuf = [sb("gbuf%d" % h, [128, 8], F32) for h in range(4)]
        self.qp = [sb("qp%d" % h, [128, TT], BF16) for h in range(4)]
        self.kp = [sb("kp%d" % h, [128, TT], BF16) for h in range(4)]
        self.vtok = sb("vtok", [128, 4, 4, 257], BF16)
        self.SmT = [sb("SmT%d" % i, [128, 128], BF16) for i in range(4)]
        self.kptok = [sb("kptok%d" % i, [128, 128], BF16) for i in range(4)]
        self.numS = [sb("numS%d" % h, [128, 3, TT], F32) for h in range(4)]
        self.ogT = [sb("ogT%d" % i, [128, 2, TT], BF16) for i in range(2)]
        self.sqh = [sb("sqh%d" % i, [128, 2, TT], BF16) for i in range(2)]
        self.d_g = Dep("gatebufs")
        self.d_uB = [Dep("uB%d" % h) for h in range(4)]
        self.d_qp = [Dep("qp%d" % h) for h in range(4)]
        self.d_kp = [Dep("kp%d" % h) for h in range(4)]
        self.d_vtok = [Dep("vtok%d" % i) for i in range(4)]
        self.d_SmT = [Dep("SmT%d" % i) for i in range(4)]
        self.d_kptok = [Dep("kptok%d" % i) for i in range(4)]
        self.d_numS = [Dep("numS%d" % h) for h in range(4)]
        self.d_ogT = [Dep("ogT0"), Dep("ogT1")]
        self.d_sqh = [Dep("sqh0"), Dep("sqh1")]
        self.d_gh = [Dep("gh%d" % h) for h in range(4)]
        self._region_end()
        self.mixA_end = self.sb_off

    def _alloc_mixB(self):
        self.sb_off = self.mix_common_end
        sb = self.sb
        self.kT = [sb("kT%d" % g, [128, 640], BF16) for g in range(2)]
        self.d_kT = [Dep("kT0"), Dep("kT1")]
        self.vt = sb("vt", [128, 5, 4, 128], BF16)
        self.d_vt = [Dep("vt%d" % i) for i in range(5)]
        self.qlo = sb("qlo", [128, 8, TT], BF16)
        self.qhi = sb("qhi", [128, 8, TT], BF16)
        self.d_qT = [Dep("qT%d" % i) for i in range(8)]
        self.Es = [sb("Es%d" % i, [128, TT], BF16) for i in range(3)]
        self.d_Es = [Dep("Es%d" % i) for i in range(3)]
        self._region_end()


    def wload_mem(self, off, n):
        k = self.wslot_i % len(self.wslots)
        self.wslot_i += 1
        slot, dep = self.wslots[k], self.wslot_deps[k]
        self.P.op("pool", lambda h, slot=slot, off=off, n=n: h.dma_start(out=slot[:, 0:n], in_=self.wmdram[:, off:off + n]),
                  writes=[dep], chan=("w", k))
        return slot, dep

    def seq_start(self, seq):
        P = self.P
        ci = self.cidx
        P.barrier()
        for j in range(2):
            for h in range(4):
                P.op("pool", lambda hh, j=j, h=h: hh.memset(self.Dst[j][h][:, :], 0.0), writes=[self.d_st[j][h]])
                P.op("pool", lambda hh, j=j, h=h: hh.memset(self.Cb[j][h][:, :], 0.0), writes=[self.d_st[j][h]])
                P.op("pool", lambda hh, j=j, h=h: hh.memset(self.nrep[j][h][:, :], 0.0), writes=[self.d_st[j][h]])
            P.op("pool", lambda hh, j=j: hh.memset(self.car[j][:, 0:8], 0.0), writes=[self.d_car[j]])
            P.op("pool", lambda hh, j=j: hh.memset(self.car[j][:, 8:12], 1.0), writes=[self.d_car[j]])
        d_m = self.d_mem
        d_mt = Dep("mT")
        P.op("sp", lambda h: h.dma_start(out=self.mT[:, :, :],
                                         in_=self.mdram[seq, :, :].rearrange("(c p) t -> p c t", p=128)),
             writes=[d_mt], chan="m")
        ones = self.cbv("ones")
        sqv = self.sq[:, :, 0:NMEM]
        for c in range(8):
            P.op("act", lambda h, c=c: h.activation(out=self.sq[:, c, 0:NMEM], in_=self.mT[:, c, :], func=AF.Square),
                 reads=[d_mt], writes=[self.d_sq])
        bk, bd = self.bank()
        for c in range(8):
            P.op("pe", lambda h, c=c, bk=bk: h.matmul(bk[:, 0:NMEM], ones, self.sq[:, c, 0:NMEM], start=(c == 0), stop=(c == 7)),
                 reads=[self.d_sq, self.d_C], writes=[bd])
        P.op("act", lambda h, bk=bk: h.activation(out=self.tmpA[:, 0:NMEM], in_=bk[:, 0:NMEM], func=AF.Ln,
                                                   bias=self.epsc, scale=1.0 / D), reads=[bd, self.d_C], writes=[self.d_tmpA])
        P.op("act", lambda h: h.activation(out=self.rstd[:, 0:NMEM], in_=self.tmpA[:, 0:NMEM], func=AF.Exp, scale=-0.5),
             reads=[self.d_tmpA], writes=[self.d_rstd])
        d_memn = Dep("memn")
        for c in range(8):
            P.op("dve", lambda h, c=c: h.scalar_tensor_tensor(
                out=self.memn[:, c, :], in0=self.mT[:, c, :], scalar=self.C[:, ci["memg"] + c:ci["memg"] + c + 1],
                in1=self.rstd[:, 0:NMEM], op0=ALU.mult, op1=ALU.mult),
                reads=[d_mt, self.d_rstd, self.d_C], writes=[d_memn])
        d_memK = Dep("memK")
        for hd in range(4):
            slot, wd = self.wload_mem(hd * 1024, 1024)
            bk, bd = self.bank()
            for k in range(8):
                P.op("pe", lambda h, k=k, slot=slot, bk=bk: h.matmul(
                    bk[:, 0:NMEM], slot[:, k * 128:(k + 1) * 128], self.memn[:, k, :], start=(k == 0), stop=(k == 7)),
                    reads=[wd, d_memn], writes=[bd])
            P.op("act", lambda h, hd=hd, bk=bk: h.activation(out=self.memK[:, hd, :], in_=bk[:, 0:NMEM], func=AF.Copy),
                 reads=[bd], writes=[d_memK])
            P.op("act", lambda h, hd=hd, bk=bk: h.activation(out=self.psq[:, :], in_=bk[:, 0:NMEM], func=AF.Square),
                 reads=[bd], writes=[self.d_psq])
            b2, d2 = self.bank()
            P.op("pe", lambda h, b2=b2: h.matmul(b2[:, 0:NMEM], ones, self.psq[:, :], start=True, stop=True),
                 reads=[self.d_psq, self.d_C], writes=[d2])
            P.op("act", lambda h, b2=b2: h.activation(out=self.tmpA[:, 0:NMEM], in_=b2[:, 0:NMEM], func=AF.Ln,
                                                       bias=self.epsc, scale=1.0 / 128), reads=[d2], writes=[self.d_tmpA])
            P.op("act", lambda h, hd=hd: h.activation(out=self.rstdk[:, hd, :], in_=self.tmpA[:, 0:NMEM], func=AF.Exp, scale=-0.5),
                 reads=[self.d_tmpA], writes=[d_memK])
            for l in range(4):
                P.op("dve", lambda h, hd=hd, l=l: h.scalar_tensor_tensor(
                    out=self.memKn[:, l, hd, :], in0=self.memK[:, hd, :], scalar=self.C[:, ci["xak", l]:ci["xak", l] + 1],
                    in1=self.rstdk[:, hd, :], op0=ALU.mult, op1=ALU.mult),
                    reads=[d_memK, self.d_C], writes=[d_m])
        vs = [self.wload_mem(4096, 2048), self.wload_mem(6144, 2048)]
        for mc in range(2):
            bk, bd = self.bank()
            for hf in range(2):
                slot, wd = vs[hf]
                for k in range(8):
                    P.op("pe", lambda h, k=k, mc=mc, hf=hf, slot=slot, bk=bk: h.matmul(
                        bk[:, hf * 256:(hf + 1) * 256], self.memn[:, k, mc * 128:(mc + 1) * 128], slot[:, k * 256:(k + 1) * 256],
                        start=(k == 0), stop=(k == 7)), reads=[wd, d_memn], writes=[bd])
            P.op("act", lambda h, mc=mc, bk=bk: h.activation(out=self.memV[:, mc, :], in_=bk[:, :], func=AF.Copy),
                 reads=[bd], writes=[d_m])
        P.barrier()

    def rope_tables(self, seq, t0):
        P = self.P
        if os.environ.get("DBG_NOROPE"):
            P.op("dve", lambda h: h.memset(self.cosF[:, :], 1.0), writes=[self.d_rope])
            P.op("dve", lambda h: h.memset(self.sinF[:, :], 0.0), writes=[self.d_rope])
            return
        ci = self.cidx
        ds = self.d_rs
        if os.environ.get("DBG_ROPEBAR"):
            P.op("dve", lambda h: h.memset(self.cosF[:, :], 1.0), writes=[self.d_rope])
            P.op("dve", lambda h: h.memset(self.sinF[:, :], 0.0), writes=[self.d_rope])
            P.barrier()
            return
        TWO_PI = 2.0 * math.pi
        C1 = 6.28125
        C2 = TWO_PI - C1
        PIB = 3.141592
        posf, ang, y, r = self.rf
        P.op("sp", lambda h: h.dma_start(out=posf[:, :], in_=self.pdram[seq, :, t0:t0 + TT]),
             writes=[ds], chan="p")
        P.op("dve", lambda h: h.tensor_scalar(out=ang[:, :], in0=posf[:, :], scalar1=self.C[:, ci["invf"]:ci["invf"] + 1],
                                              scalar2=None, op0=ALU.mult), reads=[ds, self.d_C], writes=[ds])
        MAGIC = 12582912.0
        for which, dst in ((0, self.sinF), (1, self.cosF)):
            if which == 1:
                P.op("dve", lambda h: h.tensor_scalar(out=ang[:, :], in0=ang[:, :], scalar1=math.pi / 2, scalar2=None, op0=ALU.add),
                     reads=[ds], writes=[ds])
            P.op("dve", lambda h: h.tensor_scalar(out=y[:, :], in0=ang[:, :], scalar1=1.0 / TWO_PI, scalar2=MAGIC,
                                                  op0=ALU.mult, op1=ALU.add), reads=[ds], writes=[ds])
            P.op("dve", lambda h: h.tensor_scalar(out=y[:, :], in0=y[:, :], scalar1=-MAGIC, scalar2=None, op0=ALU.add),
                 reads=[ds], writes=[ds])
            P.op("dve", lambda h: h.scalar_tensor_tensor(out=r[:, :], in0=y[:, :], scalar=-C1, in1=ang[:, :],
                                                         op0=ALU.mult, op1=ALU.add), reads=[ds], writes=[ds])
            P.op("dve", lambda h: h.scalar_tensor_tensor(out=r[:, :], in0=y[:, :], scalar=-C2, in1=r[:, :],
                                                         op0=ALU.mult, op1=ALU.add), reads=[ds], writes=[ds])
            P.op("dve", lambda h: h.tensor_scalar(out=y[:, :], in0=r[:, :], scalar1=-math.pi, scalar2=TWO_PI,
                                                  op0=ALU.is_lt, op1=ALU.mult), reads=[ds], writes=[ds])
            P.op("dve", lambda h: h.tensor_tensor(out=r[:, :], in0=r[:, :], in1=y[:, :], op=ALU.add), reads=[ds], writes=[ds])
            P.op("dve", lambda h: h.tensor_scalar(out=y[:, :], in0=r[:, :], scalar1=math.pi, scalar2=-TWO_PI,
                                                  op0=ALU.is_gt, op1=ALU.mult), reads=[ds], writes=[ds])
            P.op("dve", lambda h: h.tensor_tensor(out=r[:, :], in0=r[:, :], in1=y[:, :], op=ALU.add), reads=[ds], writes=[ds])
            P.op("dve", lambda h: h.tensor_scalar(out=r[:, :], in0=r[:, :], scalar1=-PIB, scalar2=PIB,
                                                  op0=ALU.max, op1=ALU.min), reads=[ds], writes=[ds])
            s2 = posf if which == 1 else self.rstd
            dd = ds
            P.op("dve", lambda h, s2=s2: h.tensor_tensor(out=s2[:, :], in0=r[:, :], in1=r[:, :], op=ALU.mult),
                 reads=[ds, self.d_rstd], writes=[ds, self.d_rstd])
            P.op("dve", lambda h, s2=s2: h.tensor_scalar(out=y[:, :], in0=s2[:, :], scalar1=1.0 / 6227020800.0, scalar2=None, op0=ALU.mult),
                 reads=[ds, self.d_rstd], writes=[ds])
            for cf in (-1.0 / 39916800.0, 1.0 / 362880.0, -1.0 / 5040.0, 1.0 / 120.0, -1.0 / 6.0):
                P.op("dve", lambda h, s2=s2, cf=cf: h.scalar_tensor_tensor(out=y[:, :], in0=y[:, :], scalar=cf, in1=s2[:, :],
                                                                           op0=ALU.add, op1=ALU.mult),
                     reads=[ds, self.d_rstd], writes=[ds])
            P.op("dve", lambda h, dst=dst: h.scalar_tensor_tensor(out=dst[:, :], in0=y[:, :], scalar=1.0, in1=r[:, :],
                                                                  op0=ALU.add, op1=ALU.mult),
                 reads=[ds], writes=[self.d_rope])
        self.dump("cosF", self.cosF[:, :], TT, [self.d_rope])
        self.dump("sinF", self.sinF[:, :], TT, [self.d_rope])

    def feat_proj(self, mk, nk=8, rhs=None, d_rhs=None, ncols=TT, bank=None):
        P = self.P
        slot, wd = self.wload(mk, nk * 128)
        bk, bd = self.bank() if bank is None else self.bx(bank)
        for k in range(nk):
            P.op("pe", lambda h, k=k, slot=slot, bk=bk: h.matmul(
                bk[:, 0:ncols], slot[:, k * 128:(k + 1) * 128], self.xn[:, k, :], start=(k == 0), stop=(k == nk - 1)),
                reads=[wd, self.d_xnc[k]], writes=[bd])
        return bk, bd

    def colchunk(self, w, c0):
        return lambda: w[:, c0:c0 + 128].reshape(8, 128, 128).transpose(1, 0, 2).reshape(128, 1024)

    def mem_attn(self, l, w, xq0):
        P = self.P
        ci = self.cidx
        ones = self.cbv("ones")
        st = {}

        def stA(hd):
            bq, dq = self.feat_proj(self.colchunk(w, xq0 + hd * 128) if self.host else None, bank=hd % 2)
            i2 = hd % 2
            P.op("act", lambda h: h.activation(out=self.sqq[i2][:, :], in_=bq[:, :], func=AF.Square),
                 reads=[dq], writes=[self.d_sqq[i2]])
            bs, ds = self.bx(2)
            P.op("pe", lambda h: h.matmul(bs[:, :], ones, self.sqq[i2][:, :], start=True, stop=True),
                 reads=[self.d_sqq[i2], self.d_C], writes=[ds])
            P.op("act", lambda h: h.activation(out=self.tmpA[:, :], in_=bs[:, :], func=AF.Ln,
                                               bias=self.epsc, scale=1.0 / 128), reads=[ds], writes=[self.d_tmpA])
            P.op("act", lambda h: h.activation(out=self.rstd[:, :], in_=self.tmpA[:, :], func=AF.Exp, scale=-0.5),
                 reads=[self.d_tmpA], writes=[self.d_rstd])
            P.op("dve", lambda h: h.scalar_tensor_tensor(
                out=self.qn[i2][:, :], in0=bq[:, :], scalar=self.C[:, ci["xaq", l]:ci["xaq", l] + 1],
                in1=self.rstd[:, :], op0=ALU.mult, op1=ALU.mult),
                reads=[dq, self.d_rstd, self.d_C], writes=[self.d_qn[i2]])

        def stB(hd):
            i2 = hd % 2
            for mc in range(2):
                b_, d_ = self.bx(3 + mc)
                P.op("pe", lambda h, b_=b_, mc=mc: h.matmul(
                    b_[:, :], self.memKn[:, l, hd, mc * 128:(mc + 1) * 128], self.qn[i2][:, :], start=True, stop=True),
                    reads=[self.d_mem, self.d_qn[i2]], writes=[d_])
                P.op("act", lambda h, b_=b_, mc=mc: h.activation(
                    out=self.Em[i2][:, mc, :], in_=b_[:, :], func=AF.Exp, scale=128.0 ** -0.5),
                    reads=[d_], writes=[self.d_Em[i2]])

        def stC(hd):
            i2 = hd % 2
            bo, do = self.bx(5 + i2)
            for mc in range(2):
                P.op("pe", lambda h, mc=mc: h.matmul(
                    bo[:, :], self.memV[:, mc, hd * 128:(hd + 1) * 128], self.Em[i2][:, mc, :],
                    start=(mc == 0), stop=(mc == 1)), reads=[self.d_mem, self.d_Em[i2]], writes=[do])
            bd_, dd_ = self.bx(7)
            for mc in range(2):
                P.op("pe", lambda h, mc=mc: h.matmul(
                    bd_[:, :], ones, self.Em[i2][:, mc, :], start=(mc == 0), stop=(mc == 1)),
                    reads=[self.d_C, self.d_Em[i2]], writes=[dd_])
            P.op("act", lambda h: h.activation(out=self.rden[:, :], in_=bd_[:, :], func=AF.Ln),
                 reads=[dd_], writes=[self.d_rden])
            P.op("act", lambda h: h.activation(out=self.rden[:, :], in_=self.rden[:, :], func=AF.Exp, scale=-1.0),
                 reads=[self.d_rden], writes=[self.d_rden])
            P.op("dve", lambda h: h.tensor_tensor(
                out=self.yT[:, 8 + hd, :], in0=bo[:, :], in1=self.rden[:, :], op=ALU.mult),
                reads=[do, self.d_rden], writes=[self.d_y[8 + hd]])

        for s in range(6):
            if s < 4:
                stA(s)
            if 0 <= s - 1 < 4:
                stB(s - 1)
            if 0 <= s - 2 < 4:
                stC(s - 2)
        self.psum_i = 0

    def out_proj(self, w_out):
        P = self.P
        for c in range(8):
            def mk(c=c):
                a = w_out[:, c * 128:(c + 1) * 128].reshape(12, 128, 128)
                return a.transpose(1, 0, 2).reshape(128, 12 * 128)
            slot, wd = self.wload(mk, 12 * 128)
            bk, bd = self.bank()
            for k in range(12):
                P.op("pe", lambda h, k=k, slot=slot, bk=bk: h.matmul(
                    bk[:, :], slot[:, k * 128:(k + 1) * 128], self.yT[:, k, :], start=(k == 0), stop=(k == 11)),
                    reads=[wd, self.d_y[k]], writes=[bd])
            P.op("dve", lambda h, c=c, bk=bk: h.tensor_tensor(
                out=self.xT[:, c, :], in0=bk[:, :], in1=self.xT[:, c, :], op=ALU.add),
                reads=[bd, self.d_xc[c]], writes=[self.d_xc[c]])

    def mixA(self, l, blk):
        P = self.P
        ci = self.cidx
        j = l // 2
        w = self.inp["a_w_in"][j] if self.host else None
        w_out = self.inp["a_w_out"][j] if self.host else None
        self.rmsnorm(ci["mg", l])
        ones = self.cbv("ones")
        ident = self.cbv("ident")
        maskcur = self.cbv("maskcur")
        LNS = math.log(128.0 ** -0.5)
        car = self.car[j]
        d_car = self.d_car[j]
        dg = self.d_g
        for hd in range(4):
            def mkg(hd=hd):
                g = np.stack([w[:, 3072 + hd], w[:, 3076 + hd]], axis=1).reshape(8, 128, 2)
                g = g.transpose(1, 0, 2)[:, :, :, None]
                return np.broadcast_to(g, (128, 8, 2, 128)).reshape(128, 2048)
            slot, wd = self.wload(mkg, 2048)
            gb_ = []
            for g in range(2):
                bk, bd = self.bank()
                for k in range(8):
                    P.op("pe", lambda h, k=k, g=g, slot=slot, bk=bk: h.matmul(
                        bk[:, :], slot[:, (k * 2 + g) * 128:(k * 2 + g + 1) * 128], self.xn[:, k, :],
                        start=(k == 0), stop=(k == 7)), reads=[wd, self.d_xnc[k]], writes=[bd])
                gb_.append((bk, bd))
            (bi, dbi), (bf, dbf) = gb_
            nb = self.negb
            P.op("act", lambda h, bf=bf, hd=hd: h.activation(out=self.gt1[:, :], in_=bf[:, :], func=AF.Exp,
                                                               bias=nb[:, 8 * j + 4 + hd:8 * j + 5 + hd], scale=-1.0),
                 reads=[dbf, self.d_misc], writes=[dg])
            P.op("act", lambda h: h.activation(out=self.gt1[:, :], in_=self.gt1[:, :], func=AF.Ln, bias=1.0, scale=1.0),
                 reads=[dg], writes=[dg])
            P.op("dve", lambda h, hd=hd: h.tensor_copy(out=self.Bext[:, 0:1], in_=car[:, hd:hd + 1]),
                 reads=[d_car], writes=[dg])
            P.op("dve", lambda h: h.tensor_tensor_scan(out=self.Bext[:, 1:TT + 1], data0=self.gt1[:, :], data1=self.zeros[:, :],
                                                        initial=self.Bext[:, 0:1], op0=ALU.add, op1=ALU.add),
                 reads=[dg, self.d_misc], writes=[dg])
            P.op("act", lambda h, bi=bi, hd=hd: h.activation(out=self.negi[:, :], in_=bi[:, :], func=AF.Identity,
                                                               bias=nb[:, 8 * j + hd:8 * j + hd + 1], scale=-1.0),
                 reads=[dbi, self.d_misc], writes=[dg])
            P.op("dve", lambda h: h.tensor_tensor(out=self.nega[:, :], in0=self.negi[:, :], in1=self.Bext[:, 1:TT + 1],
                                                   op=ALU.subtract), reads=[dg], writes=[dg])
            P.op("dve", lambda h, hd=hd: h.tensor_copy(out=self.Aext[:, 0:1], in_=car[:, 4 + hd:5 + hd]),
                 reads=[d_car], writes=[dg])
            P.op("dve", lambda h: h.tensor_tensor_scan(out=self.Aext[:, 1:TT + 1], data0=self.nega[:, :], data1=self.nega[:, :],
                                                        initial=self.Aext[:, 0:1], op0=ALU.min, op1=ALU.min),
                 reads=[dg], writes=[dg])
            P.op("dve", lambda h, hd=hd: h.tensor_scalar(out=self.posM[hd][:, :], in0=self.Aext[:, 0:TT:128],
                                                          scalar1=-1.0, scalar2=LNS, op0=ALU.mult, op1=ALU.add),
                 reads=[dg], writes=[self.d_gh[hd]])
            P.op("dve", lambda h, hd=hd: h.tensor_copy(out=self.gbuf[hd][:, 0:1], in_=car[:, 8 + hd:9 + hd]),
                 reads=[d_car], writes=[self.d_gh[hd]])
            P.op("dve", lambda h, hd=hd: h.tensor_tensor(out=self.gbuf[hd][:, 1:5], in0=self.Aext[:, 128:TT + 1:128],
                                                          in1=self.Aext[:, 0:TT:128], op=ALU.subtract),
                 reads=[dg], writes=[self.d_gh[hd]])
            P.op("act", lambda h, hd=hd: h.activation(out=self.gbuf[hd][:, 1:5], in_=self.gbuf[hd][:, 1:5], func=AF.Exp),
                 reads=[self.d_gh[hd]], writes=[self.d_gh[hd]])
            P.op("dve", lambda h: h.tensor_tensor(out=self.tmpB[:, :], in0=self.Bext[:, 1:TT + 1], in1=self.Aext[:, 1:TT + 1],
                                                   op=ALU.add), reads=[dg], writes=[dg])
            P.op("act", lambda h, hd=hd: h.activation(out=self.eBA[hd][:, :], in_=self.tmpB[:, :], func=AF.Exp),
                 reads=[dg], writes=[self.d_gh[hd]])
            for c in range(4):
                cs = slice(c * 128, (c + 1) * 128)
                P.op("act", lambda h, hd=hd, c=c, cs=cs: h.activation(
                    out=self.uB[hd][:, cs], in_=self.nega[:, cs], func=AF.Exp,
                    bias=self.Aext[:, c * 128:c * 128 + 1], scale=-1.0), reads=[dg], writes=[self.d_uB[hd]])
                P.op("act", lambda h, hd=hd, c=c, cs=cs: h.activation(
                    out=self.gB[hd][:, cs], in_=self.Aext[:, 1 + c * 128:1 + (c + 1) * 128], func=AF.Exp,
                    bias=self.posM[hd][:, c:c + 1], scale=1.0), reads=[dg, self.d_gh[hd]], writes=[self.d_uB[hd]])
            P.op("dve", lambda h, hd=hd: h.tensor_copy(out=car[:, hd:hd + 1], in_=self.Bext[:, TT:TT + 1]),
                 reads=[dg], writes=[d_car])
            P.op("dve", lambda h, hd=hd: h.tensor_copy(out=car[:, 4 + hd:5 + hd], in_=self.Aext[:, TT:TT + 1]),
                 reads=[dg], writes=[d_car])
            P.op("dve", lambda h, hd=hd: h.tensor_copy(out=car[:, 8 + hd:9 + hd], in_=self.gbuf[hd][:, 4:5]),
                 reads=[self.d_gh[hd]], writes=[d_car])
            bq, dq = self.feat_proj(self.colchunk(w, hd * 128) if self.host else None)
            P.op("dve", lambda h, hd=hd, bq=bq: h.tensor_tensor(out=self.qp[hd][:, :], in0=bq[:, :], in1=self.gB[hd][:, :],
                                                                 op=ALU.mult), reads=[dq, self.d_uB[hd]], writes=[self.d_qp[hd]])
            bk_, dk_ = self.feat_proj(self.colchunk(w, 512 + hd * 128) if self.host else None)
            P.op("dve", lambda h, hd=hd, bk_=bk_: h.tensor_tensor(out=self.kp[hd][:, :], in0=bk_[:, :], in1=self.uB[hd][:, :],
                                                                   op=ALU.mult), reads=[dk_, self.d_uB[hd]], writes=[self.d_kp[hd]])
        if blk == 0 or True:
            P.op("pool", lambda h: h.memset(self.vtok[:, :, :, 256:257], 1.0), writes=self.d_vtok)
        for qh in range(4):
            def mk(qh=qh):
                v = w[:, 1024 + qh * 256:1024 + (qh + 1) * 256].reshape(8, 128, 256)
                return v.transpose(1, 0, 2).reshape(128, 2048)
            slot, wd = self.wload(mk, 2048)
            for b in range(4):
                bk, bd = self.bank()
                for k in range(8):
                    P.op("pe", lambda h, k=k, b=b, slot=slot, bk=bk: h.matmul(
                        bk[:, 0:256], self.xn[:, k, b * 128:(b + 1) * 128], slot[:, k * 256:(k + 1) * 256],
                        start=(k == 0), stop=(k == 7)), reads=[wd, self.d_xnc[k]], writes=[bd])
                if b % 2 == 0:
                    P.op("act", lambda h, b=b, qh=qh, bk=bk: h.activation(
                        out=self.vtok[:, b, qh, 0:256], in_=bk[:, 0:256], func=AF.Copy), reads=[bd], writes=[self.d_vtok[b]])
                else:
                    P.op("dve", lambda h, b=b, qh=qh, bk=bk: h.tensor_copy(
                        out=self.vtok[:, b, qh, 0:256], in_=bk[:, 0:256]), reads=[bd], writes=[self.d_vtok[b]])
        self.mem_attn(l, w, A_TOK)
        its = [(c, hd) for c in range(4) for hd in range(4)]
        cst = {}

        def ck_ST(i):
            c, hd = its[i]
            cs = slice(c * 128, (c + 1) * 128)
            r = i % 4
            b1, d1 = self.bx(0 + i % 2)
            P.op("pe", lambda h: h.matmul(b1[:, 0:128], self.kp[hd][:, cs], self.qp[hd][:, cs], start=True, stop=True),
                 reads=[self.d_kp[hd], self.d_qp[hd]], writes=[d1])
            P.op("dve", lambda h: h.tensor_tensor(out=self.SmT[r][:, :], in0=b1[:, 0:128], in1=maskcur, op=ALU.mult),
                 reads=[d1, self.d_C], writes=[self.d_SmT[r]])
            b2, d2 = self.bx(2 + i % 2)
            b2b = b2.bitcast(BF16)
            P.op("pe", lambda h: h.transpose(b2b[:, 0:128], self.kp[hd][:, cs], ident),
                 reads=[self.d_kp[hd], self.d_C], writes=[d2])
            P.op("dve", lambda h: h.tensor_copy(out=self.kptok[r][:, :], in_=b2b[:, 0:128]),
                 reads=[d2], writes=[self.d_kptok[r]])

        def ck_ND(i):
            c, hd = its[i]
            cs = slice(c * 128, (c + 1) * 128)
            r = i % 4
            dst = self.d_st[j][hd]
            b3, d3 = self.bx(4 + i % 2)
            for g in range(3):
                if g < 2:
                    l1 = self.vtok[:, c, hd, g * 128:(g + 1) * 128]
                    l2 = self.Cb[j][hd][:, g * 128:(g + 1) * 128]
                else:
                    l1 = ones
                    l2 = self.nrep[j][hd][:, :]
                P.op("pe", lambda h, g=g, l1=l1: h.matmul(b3[:, g * 128:(g + 1) * 128], l1, self.SmT[r][:, :],
                                                          start=True, stop=False),
                     reads=[self.d_vtok[c], self.d_SmT[r], self.d_C], writes=[d3])
                P.op("pe", lambda h, g=g, l2=l2: h.matmul(b3[:, g * 128:(g + 1) * 128], l2, self.qp[hd][:, cs],
                                                          start=False, stop=True),
                     reads=[dst, self.d_qp[hd]], writes=[d3])
            P.op("act", lambda h: h.activation(
                out=self.numS[hd][:, :, cs], in_=b3[:, 0:384].rearrange("p (g n) -> p g n", g=3), func=AF.Copy),
                reads=[d3], writes=[self.d_numS[hd]])
            b4, d4 = self.bx(6 + i % 2)
            P.op("pe", lambda h: h.matmul(b4[:, 0:257], self.kptok[r][:, :], self.vtok[:, c, hd, :], start=True, stop=True),
                 reads=[self.d_kptok[r], self.d_vtok[c]], writes=[d4])
            P.op("dve", lambda h: h.scalar_tensor_tensor(
                out=self.Dst[j][hd][:, :], in0=self.Dst[j][hd][:, :], scalar=self.gbuf[hd][:, c:c + 1], in1=b4[:, 0:257],
                op0=ALU.mult, op1=ALU.add), reads=[d4, dst, self.d_gh[hd]], writes=[dst])
            P.op("act", lambda h: h.activation(out=self.Cb[j][hd][:, :], in_=self.Dst[j][hd][:, :], func=AF.Identity,
                                               scale=self.gbuf[hd][:, c + 1:c + 2]),
                 reads=[dst, self.d_gh[hd]], writes=[dst])
            P.op("act", lambda h: h.activation(
                out=self.nrep[j][hd][:, :], in_=self.Dst[j][hd][:, 256:257].to_broadcast([128, 128]), func=AF.Identity,
                scale=self.gbuf[hd][:, c + 1:c + 2]), reads=[dst, self.d_gh[hd]], writes=[dst])

        ck_ST(0)
        for i in range(len(its)):
            if i + 1 < len(its):
                ck_ST(i + 1)
            ck_ND(i)
        self.psum_i = 0
        for hd in range(4):
            i2 = hd % 2
            for dvc in range(2):
                bo, do = self.feat_proj(self.colchunk(w, 2048 + (2 * hd + dvc) * 128) if self.host else None)
                P.op("act", lambda h, bo=bo, i2=i2, dvc=dvc: h.activation(out=self.ogT[i2][:, dvc, :], in_=bo[:, :], func=AF.Sigmoid),
                     reads=[do], writes=[self.d_ogT[i2]])
            ns = self.numS[hd]
            for dvc in range(2):
                P.op("act", lambda h, ns=ns, i2=i2, dvc=dvc: h.activation(out=self.sqh[i2][:, dvc, :], in_=ns[:, dvc, :], func=AF.Square),
                     reads=[self.d_numS[hd]], writes=[self.d_sqh[i2]])
            bm, dm = self.bank()
            for dvc in range(2):
                P.op("pe", lambda h, bm=bm, i2=i2, dvc=dvc: h.matmul(bm[:, :], ones, self.sqh[i2][:, dvc, :],
                                                                     start=(dvc == 0), stop=(dvc == 1)),
                     reads=[self.d_sqh[i2], self.d_C], writes=[dm])
            t0_, t1_, t2_ = self.t1
            dt0, dt1, dt2 = self.d_t1
            P.op("act", lambda h, ns=ns: h.activation(out=t0_[:, :], in_=ns[:, 2, :], func=AF.Abs),
                 reads=[self.d_numS[hd]], writes=[dt0])
            P.op("dve", lambda h, hd=hd: h.tensor_tensor(out=t0_[:, :], in0=t0_[:, :], in1=self.eBA[hd][:, :], op=ALU.max),
                 reads=[dt0, self.d_gh[hd]], writes=[dt0])
            P.op("dve", lambda h: h.scalar_tensor_tensor(out=t1_[:, :], in0=t0_[:, :], scalar=EPS, in1=t0_[:, :],
                                                          op0=ALU.mult, op1=ALU.mult), reads=[dt0], writes=[dt1])
            P.op("dve", lambda h, bm=bm: h.scalar_tensor_tensor(out=t2_[:, :], in0=bm[:, :], scalar=1.0 / 256, in1=t1_[:, :],
                                                                 op0=ALU.mult, op1=ALU.add), reads=[dm, dt1], writes=[dt2])
            P.op("act", lambda h: h.activation(out=t2_[:, :], in_=t2_[:, :], func=AF.Ln), reads=[dt2], writes=[dt2])
            P.op("act", lambda h: h.activation(out=t0_[:, :], in_=t2_[:, :], func=AF.Exp, scale=-0.5), reads=[dt2], writes=[dt0])
            gh0 = ci["ghn", j]
            for dvc in range(2):
                ch = 2 * hd + dvc
                P.op("dve", lambda h, ns=ns, dvc=dvc, ch=ch: h.scalar_tensor_tensor(
                    out=t1_[:, :], in0=ns[:, dvc, :], scalar=self.C[:, gh0 + ch:gh0 + ch + 1], in1=t0_[:, :],
                    op0=ALU.mult, op1=ALU.mult), reads=[self.d_numS[hd], dt0, self.d_C], writes=[dt1])
                P.op("dve", lambda h, i2=i2, dvc=dvc, ch=ch: h.tensor_tensor(
                    out=self.yT[:, ch, :], in0=t1_[:, :], in1=self.ogT[i2][:, dvc, :], op=ALU.mult),
                    reads=[dt1, self.d_ogT[i2]], writes=[self.d_y[ch]])
        self.out_proj(w_out)

    def mixB(self, l, blk):
        P = self.P
        ci = self.cidx
        j = l // 2
        w = self.inp["b_w_in"][j] if self.host else None
        w_out = self.inp["b_w_out"][j] if self.host else None
        self.rmsnorm(ci["mg", l])
        mask4 = self.cbv("mask4", 512)
        lohi = [self.cbv("onelo"), self.cbv("onehi")]
        bones = self.cbv("blockones")
        RT = self.cbv("RT")
        dprev = self.d_prev[j]
        P.op("pool", lambda h: h.memset(self.vt[:, :, :, :], 0.0), writes=self.d_vt)
        if blk > 0:
            P.op("pool", lambda h: h.tensor_copy(out=self.vt[:, 0, :, :], in_=self.prevV[j][:, :, :]),
                 reads=[dprev], writes=[self.d_vt[0]])
        slot, wd = self.wload(self.colchunk(w, 1152) if self.host else None, 1024)
        for b in range(4):
            bk, bd = self.bank()
            for k in range(8):
                P.op("pe", lambda h, k=k, b=b, slot=slot, bk=bk: h.matmul(
                    bk[:, 0:128], self.xn[:, k, b * 128:(b + 1) * 128], slot[:, k * 128:(k + 1) * 128],
                    start=(k == 0), stop=(k == 7)), reads=[wd, self.d_xnc[k]], writes=[bd])
            pv = bk[:, 0:128].rearrange("p (g n) -> p g n", g=2)
            P.op("act", lambda h, b=b, pv=pv: h.activation(out=self.vt[:, b + 1, 0:4:2, 0:64], in_=pv, func=AF.Copy),
                 reads=[bd], writes=[self.d_vt[b + 1]])
            P.op("dve", lambda h, b=b, pv=pv: h.tensor_copy(out=self.vt[:, b + 1, 1:4:2, 64:128], in_=pv),
                 reads=[bd], writes=[self.d_vt[b + 1]])
        P.op("pool", lambda h: h.memset(self.qlo[:, :, :], 0.0), writes=self.d_qT)
        P.op("pool", lambda h: h.memset(self.qhi[:, :, :], 0.0), writes=self.d_qT)
        for g in range(2):
            if blk > 0:
                P.op("pool", lambda h, g=g: h.tensor_copy(out=self.kT[g][:, 0:128], in_=self.prevK[j][:, g, :]),
                     reads=[dprev], writes=[self.d_kT[g]])
        self.psum_i = 0
        ta, tb = self.t1[0], self.t1[1]
        da, db = self.d_t1[0], self.d_t1[1]
        chunks = []
        for g in range(2):
            def mk(g=g):
                kc = w[:, 1024 + g * 64:1024 + (g + 1) * 64]
                kk = np.concatenate([kc, kc], axis=1)
                return kk.reshape(8, 128, 128).transpose(1, 0, 2).reshape(128, 1024)
            chunks.append((mk, ci["bkg", j], self.kT[g][:, 128:640], self.d_kT[g]))
        for c in range(8):
            chunks.append((self.colchunk(w, c * 128) if self.host else None, ci["bqg", j],
                           (self.qlo[:, c, :], self.qhi[:, c, :]), self.d_qT[c]))
        pst = {}

        def nrA1(s):
            mk, gcol, out_ap, d_out = chunks[s]
            bk, bd = self.feat_proj(mk if self.host else None, bank=s % 3)
            i2 = s % 2
            P.op("act", lambda h: h.activation(out=self.sqq[i2][:, :], in_=bk[:, :], func=AF.Square),
                 reads=[bd], writes=[self.d_sqq[i2]])
            pst[s] = (bk, bd)

        def nrA2(s):
            mk, gcol, out_ap, d_out = chunks[s]
            bk, bd = pst[s]
            i2 = s % 2
            bs, ds = self.bx(3 + s % 2)
            P.op("pe", lambda h: h.matmul(bs[:, :], bones, self.sqq[i2][:, :], start=True, stop=True),
                 reads=[self.d_sqq[i2], self.d_C], writes=[ds])
            P.op("act", lambda h: h.activation(out=self.tmpA[:, :], in_=bs[:, :], func=AF.Ln, bias=self.epsc, scale=1.0 / 64),
                 reads=[ds], writes=[self.d_tmpA])
            P.op("act", lambda h: h.activation(out=self.rstd[:, :], in_=self.tmpA[:, :], func=AF.Exp, scale=-0.5),
                 reads=[self.d_tmpA], writes=[self.d_rstd])
            P.op("dve", lambda h: h.scalar_tensor_tensor(out=self.qn[i2][:, :], in0=bk[:, :], scalar=self.C[:, gcol:gcol + 1],
                                                         in1=self.rstd[:, :], op0=ALU.mult, op1=ALU.mult),
                 reads=[bd, self.d_rstd, self.d_C], writes=[self.d_qn[i2]])

        def nrB(s):
            mk, gcol, out_ap, d_out = chunks[s]
            i2 = s % 2
            br, dr = self.bx(5 + s % 2)
            P.op("pe", lambda h: h.matmul(br[:, :], RT, self.qn[i2][:, :], start=True, stop=True),
                 reads=[self.d_qn[i2], self.d_C], writes=[dr])
            P.op("dve", lambda h: h.tensor_tensor(out=ta[:, :], in0=self.qn[i2][:, :], in1=self.cosF[:, :], op=ALU.mult),
                 reads=[self.d_qn[i2], self.d_rope], writes=[da])
            P.op("dve", lambda h: h.tensor_tensor(out=tb[:, :], in0=br[:, :], in1=self.sinF[:, :], op=ALU.mult),
                 reads=[dr, self.d_rope], writes=[db])
            if isinstance(out_ap, tuple):
                lo_ap, hi_ap = out_ap
                P.op("dve", lambda h: h.tensor_tensor(out=lo_ap[0:64, :], in0=ta[0:64, :], in1=tb[0:64, :], op=ALU.add),
                     reads=[da, db], writes=[d_out])
                P.op("dve", lambda h: h.tensor_tensor(out=hi_ap[64:128, :], in0=ta[64:128, :], in1=tb[64:128, :], op=ALU.add),
                     reads=[da, db], writes=[d_out])
            else:
                P.op("dve", lambda h: h.tensor_tensor(out=out_ap, in0=ta[:, :], in1=tb[:, :], op=ALU.add),
                     reads=[da, db], writes=[d_out])

        nch = len(chunks)
        for s in range(nch + 2):
            if s < nch:
                nrA1(s)
            if 0 <= s - 1 < nch:
                nrA2(s - 1)
            if 0 <= s - 2 < nch:
                nrB(s - 2)
        self.psum_i = 0
        self.mem_attn(l, w, 1280)
        its = [(c, qb) for c in range(8) for qb in range(4)]
        ist = {}

        def att_S(i):
            c, qb = its[i]
            g = c // 4
            has_prev = not (blk == 0 and qb == 0)
            kbs = (0, 1) if has_prev else (1,)
            bs, ds = self.bx(4 + i % 4)
            for kb in kbs:
                kc0 = (qb + kb) * 128
                for e in range(2):
                    qsrc = self.qlo if e == 0 else self.qhi
                    col = (kb * 2 + e) * 128
                    P.op("pe", lambda h, qsrc=qsrc, kc0=kc0, col=col: h.matmul(
                        bs[:, col:col + 128], self.kT[g][:, kc0:kc0 + 128], qsrc[:, c, qb * 128:(qb + 1) * 128],
                        start=True, stop=True), reads=[self.d_kT[g], self.d_qT[c]], writes=[ds])
            ist[i] = (bs, ds, kbs, has_prev)

        def att_E(i):
            bs, ds, kbs, has_prev = ist[i]
            lo_c = 0 if has_prev else 256
            E = self.Es[i % 3]
            dE = self.d_Es[i % 3]
            P.op("act", lambda h: h.activation(out=E[:, lo_c:512], in_=bs[:, lo_c:512], func=AF.Exp, scale=0.125),
                 reads=[ds], writes=[dE])
            P.op("dve", lambda h: h.tensor_tensor(out=E[:, lo_c:512], in0=E[:, lo_c:512], in1=mask4[:, lo_c:512], op=ALU.mult),
                 reads=[dE, self.d_C], writes=[dE])

        def att_O(i):
            c, qb = its[i]
            g = c // 4
            bs, ds, kbs, has_prev = ist[i]
            E = self.Es[i % 3]
            dE = self.d_Es[i % 3]
            bo, do = self.bx(0 + 2 * (c % 2))
            bdn, ddn = self.bx(1 + 2 * (c % 2))
            terms = [(kb, e) for kb in kbs for e in range(2)]
            nt = len(terms)
            for ti, (kb, e) in enumerate(terms):
                col = (kb * 2 + e) * 128
                P.op("pe", lambda h, kb=kb, e=e, col=col, ti=ti: h.matmul(
                    bo[:, qb * 128:(qb + 1) * 128], self.vt[:, qb + kb, 2 * g + e, :], E[:, col:col + 128],
                    start=(ti == 0), stop=(ti == nt - 1)), reads=[self.d_vt[qb + kb], dE], writes=[do])
            for ti, (kb, e) in enumerate(terms):
                col = (kb * 2 + e) * 128
                P.op("pe", lambda h, e=e, col=col, ti=ti: h.matmul(
                    bdn[:, qb * 128:(qb + 1) * 128], lohi[e], E[:, col:col + 128],
                    start=(ti == 0), stop=(ti == nt - 1)), reads=[self.d_C, dE], writes=[ddn])
            if qb == 3:
                t2_, dt2 = self.t1[2], self.d_t1[2]
                sc = 8 * j + c
                P.op("act", lambda h: h.activation(out=t2_[:, :], in_=bdn[:, :], func=AF.Ln,
                                                   bias=self.esink[:, sc:sc + 1], scale=1.0),
                     reads=[ddn, self.d_misc], writes=[dt2])
                P.op("act", lambda h: h.activation(out=t2_[:, :], in_=t2_[:, :], func=AF.Exp, scale=-1.0), reads=[dt2], writes=[dt2])
                P.op("dve", lambda h: h.tensor_tensor(out=self.yT[:, c, :], in0=bo[:, :], in1=t2_[:, :], op=ALU.mult),
                     reads=[do, dt2], writes=[self.d_y[c]])

        att_S(0)
        for i in range(len(its)):
            if i + 1 < len(its):
                att_S(i + 1)
            att_E(i)
            att_O(i)
        self.psum_i = 0
        for g in range(2):
            P.op("pool", lambda h, g=g: h.tensor_copy(out=self.prevK[j][:, g, :], in_=self.kT[g][:, 512:640]),
                 reads=[self.d_kT[g]], writes=[dprev])
        P.op("pool", lambda h: h.tensor_copy(out=self.prevV[j][:, :, :], in_=self.vt[:, 4, :, :]),
             reads=[self.d_vt[4]], writes=[dprev])
        for c_ in range(8):
            self.dump("yB%d" % c_, self.yT[:, c_, :], TT, [self.d_y[c_]])
        self.out_proj(w_out)

    def rmsnorm(self, gcol):
        P = self.P
        ones = self.CB[:, self.cb["ones"]:self.cb["ones"] + 128]
        for c in range(8):
            if c >= 3:
                P.op("act", lambda h, c=c: h.activation(out=self.sq[:, c, :], in_=self.xT[:, c, :], func=AF.Square),
                     reads=[self.d_xc[c]], writes=[self.d_sqc[c]])
            else:
                P.op("pool", lambda h, c=c: h.tensor_tensor(out=self.sq[:, c, :], in0=self.xT[:, c, :], in1=self.xT[:, c, :], op=ALU.mult),
                     reads=[self.d_xc[c]], writes=[self.d_sqc[c]])
        bk, bd = self.bank()
        for c in range(8):
            P.op("pe", lambda h, c=c, bk=bk: h.matmul(bk[:, :], ones, self.sq[:, c, :], start=(c == 0), stop=(c == 7)),
                 reads=[self.d_sqc[c], self.d_C], writes=[bd])
        P.op("act", lambda h, bk=bk: h.activation(out=self.tmpA[:, :], in_=bk[:, :], func=AF.Ln,
                                                   bias=self.epsc, scale=1.0 / D),
             reads=[bd, self.d_C], writes=[self.d_tmpA])
        P.op("act", lambda h: h.activation(out=self.rstd[:, :], in_=self.tmpA[:, :], func=AF.Exp, scale=-0.5),
             reads=[self.d_tmpA], writes=[self.d_rstd])
        for c in range(8):
            P.op("dve", lambda h, c=c: h.scalar_tensor_tensor(
                out=self.xn[:, c, :], in0=self.xT[:, c, :], scalar=self.C[:, gcol + c:gcol + c + 1],
                in1=self.rstd[:, :], op0=ALU.mult, op1=ALU.mult),
                reads=[self.d_xc[c], self.d_rstd, self.d_C], writes=[self.d_xnc[c]])

    def ffn(self, l, which):
        P = self.P
        inp = self.inp
        pre = "ffn1" if which == 0 else "ffn2"
        self.rmsnorm(self.cidx[("f1g" if which == 0 else "f2g"), l])
        w_in = inp[pre + "_w_in"][l] if self.host else None
        w_out = inp[pre + "_w_out"][l] if self.host else None
        for j in range(NFF):
            def mk(j=j):
                g = w_in[:, j * 128:(j + 1) * 128].reshape(8, 128, 128)
                u = w_in[:, DFF + j * 128:DFF + (j + 1) * 128].reshape(8, 128, 128)
                a = np.stack([g, u], axis=2)
                return a.transpose(1, 0, 2, 3).reshape(128, 2048)
            slot, wd = self.wload(mk, 2048)
            bg, dg = self.bank()
            bu, du = self.bank()
            for k in range(8):
                P.op("pe", lambda h, k=k, slot=slot, bg=bg: h.matmul(
                    bg[:, :], slot[:, k * 256:k * 256 + 128], self.xn[:, k, :], start=(k == 0), stop=(k == 7)),
                    reads=[wd, self.d_xnc[k]], writes=[dg])
            for k in range(8):
                P.op("pe", lambda h, k=k, slot=slot, bu=bu: h.matmul(
                    bu[:, :], slot[:, k * 256 + 128:k * 256 + 256], self.xn[:, k, :], start=(k == 0), stop=(k == 7)),
                    reads=[wd, self.d_xnc[k]], writes=[du])
            sg, dsg = self.sg[j % 2], self.d_sg[j % 2]
            P.op("act", lambda h, sg=sg, bg=bg: h.activation(out=sg[:, :], in_=bg[:, :], func=AF.Silu),
                 reads=[dg], writes=[dsg])
            P.op("dve", lambda h, sg=sg, bu=bu, j=j: h.tensor_tensor(
                out=self.act[:, j, :], in0=sg[:, :], in1=bu[:, :], op=ALU.mult),
                reads=[dsg, du], writes=[self.d_act[j]])
        HK = NFF // 2
        for c in range(8):
            halves = []
            for hf in range(2):
                def mk(c=c, hf=hf):
                    a = w_out[hf * HK * 128:(hf + 1) * HK * 128, c * 128:(c + 1) * 128].reshape(HK, 128, 128)
                    return a.transpose(1, 0, 2).reshape(128, HK * 128)
                halves.append(self.wload(mk, HK * 128))
            bk, bd = self.bank()
            for k in range(NFF):
                slot, wd = halves[k // HK]
                kk = k % HK
                P.op("pe", lambda h, k=k, kk=kk, slot=slot, bk=bk: h.matmul(
                    bk[:, :], slot[:, kk * 128:(kk + 1) * 128], self.act[:, k, :], start=(k == 0), stop=(k == NFF - 1)),
                    reads=[wd, self.d_act[k]], writes=[bd])
            P.op("dve", lambda h, c=c, bk=bk: h.scalar_tensor_tensor(
                out=self.xT[:, c, :], in0=bk[:, :], scalar=0.5, in1=self.xT[:, c, :], op0=ALU.mult, op1=ALU.add),
                reads=[bd, self.d_xc[c]], writes=[self.d_xc[c]])


def _prep_inputs(inputs):
    return {k: np.asarray(v) for k, v in inputs.items()}


def host_wmem(inp):
    w = np.asarray(inp["mem_w_kv"], np.float32)
    parts = [w[:, h * 128:(h + 1) * 128].reshape(8, 128, 128).transpose(1, 0, 2).reshape(128, 1024) for h in range(4)]
    for hf in range(2):
        parts.append(w[:, 512 + hf * 256:512 + (hf + 1) * 256].reshape(8, 128, 256).transpose(1, 0, 2).reshape(128, 2048))
    return np.ascontiguousarray(np.concatenate(parts, axis=1))


def core_inputs(b, inp, c):
    sl = slice(c * SEQ_PER_CORE, (c + 1) * SEQ_PER_CORE)
    return {"xT": np.ascontiguousarray(inp["x"][sl].transpose(0, 2, 1)),
            "memT": np.ascontiguousarray(inp["mem"][sl].transpose(0, 2, 1)),
            "pos": np.ascontiguousarray(np.broadcast_to(inp["positions"][sl].astype(np.float32)[:, None, :], (SEQ_PER_CORE, 128, SEQ))),
            "consts": b.h_consts, "cbf": b.cb_arr, "wst": b.h_wst, "wmem": b.h_wmem}


def build_all(inp, dbg=None, **kw):
    b = Builder(inp, **kw)
    for k_, v_ in (dbg or {}).items():
        setattr(b, k_, v_)
    nc = b.build()
    b.h_wst = np.ascontiguousarray(np.concatenate(b.wgroups, axis=1))
    b.h_consts = np.ascontiguousarray(np.concatenate(b.consts, axis=1))
    b.h_wmem = host_wmem(inp)
    return b, nc


def kernel(**inputs):
    inp = _prep_inputs(inputs)
    b, nc = build_all(inp)
    in_maps = [core_inputs(b, inp, c) for c in range(N_CORES)]
    res = run_bass_kernel_spmd(nc, in_maps, core_ids=list(range(N_CORES)))
    out = np.concatenate([r["oT"].transpose(0, 2, 1) for r in res.results], axis=0)
    return np.ascontiguousarray(out.astype(np.float32))
```
